# Optimizing a Trainium2 kernel written in Bass

```python
import jax, jax.numpy as jnp
from jax import lax
import numpy as np

D_MODEL = 1024
BATCH = 16
SEQ = 256
DEPTH = 1
DEC_BATCH = 4
DEC_SEQ = 4096
PAST_LEN = 256

GRID_W = 64
H_A = 4
DK_A = 128
DV_A = 128
D_A = H_A * DV_A
CONV_K = 5
CHUNK_A = 64
H_B = 4
DK_B = 64
DV_B = 128
D_B = H_B * DV_B
CHUNK_B = 128
ROPE_BASE = 10000.0
D_MIX = D_A + D_B
D_IN = 3 * D_A + D_A + 2 * H_A + 2 * H_A + 2 * H_B * DK_B + D_B + D_B
D_FF = 4 * D_MODEL
ALPHA = (2.0 * DEPTH) ** 0.25
BETA_DN = (8.0 * DEPTH) ** -0.25

kernel_name = "hybrid_deltanet_retention_dit_step"


def layer_norm(x, w=None, b=None, eps=1e-6):
    xf = x.astype(jnp.float32)
    mu = jnp.mean(xf, -1, keepdims=True)
    var = jnp.mean(jnp.square(xf - mu), -1, keepdims=True)
    y = (xf - mu) * lax.rsqrt(var + eps)
    if w is not None:
        y = y * w.astype(jnp.float32) + b.astype(jnp.float32)
    return y.astype(x.dtype)


def l2norm(x, eps=1e-6):
    return x * lax.rsqrt(jnp.sum(jnp.square(x), -1, keepdims=True) + eps)


def heads(t, n):
    return t.reshape(t.shape[0], t.shape[1], n, -1).transpose(0, 2, 1, 3)


def merge_heads(t):
    b, n, s, d = t.shape
    return t.transpose(0, 2, 1, 3).reshape(b, s, n * d)


def flip_t(t):
    return jnp.flip(t, axis=2)


def short_conv(x, w):
    T = x.shape[1]
    pad = (CONV_K - 1) // 2
    xp = jnp.pad(x, ((0, 0), (pad, CONV_K - 1 - pad), (0, 0)))
    y = sum(xp[:, k:k + T] * w[k] for k in range(CONV_K))
    return jax.nn.silu(y)


def axial_rope(x):
    T = x.shape[2]
    rows = T // GRID_W
    r = jnp.repeat(jnp.arange(rows, dtype=jnp.float32), GRID_W)
    col = jnp.tile(jnp.arange(GRID_W, dtype=jnp.float32), rows)
    nf = DK_B // 4
    inv = ROPE_BASE ** (-jnp.arange(nf, dtype=jnp.float32) / nf)
    ang = jnp.concatenate([r[:, None] * inv, col[:, None] * inv], -1)
    cos, sin = jnp.cos(ang), jnp.sin(ang)
    x1, x2 = jnp.split(x, 2, axis=-1)
    return jnp.concatenate([x1 * cos - x2 * sin, x1 * sin + x2 * cos], -1)


def gated_delta_chunked(q, k, v, g, beta, s0):
    Bt, H, T, DK = q.shape
    DV = v.shape[-1]
    C = CHUNK_A
    N = T // C
    qc = q.reshape(Bt, H, N, C, DK)
    kc = k.reshape(Bt, H, N, C, DK)
    vc = v.reshape(Bt, H, N, C, DV)
    gc = jnp.cumsum(g.reshape(Bt, H, N, C), -1)
    bc = beta.reshape(Bt, H, N, C)
    tri = jnp.tril(jnp.ones((C, C), bool))
    strict = jnp.tril(jnp.ones((C, C), bool), -1)
    diff = gc[..., :, None] - gc[..., None, :]
    L = jnp.where(tri, jnp.exp(jnp.where(tri, diff, 0.0)), 0.0)
    kb = kc * bc[..., None]
    M = jnp.where(strict, jnp.einsum('bhnid,bhnjd->bhnij', kb, kc) * L, 0.0)
    A = M + jnp.eye(C, dtype=M.dtype)
    rhs = jnp.concatenate([vc * bc[..., None], kb * jnp.exp(gc)[..., None]], -1)
    sol = lax.linalg.triangular_solve(A, rhs, left_side=True, lower=True, unit_diagonal=True)
    u, w = sol[..., :DV], sol[..., DV:]
    qk = jnp.where(tri, jnp.einsum('bhnid,bhnjd->bhnij', qc, kc) * L, 0.0)
    q_dec = qc * jnp.exp(gc)[..., None]
    k_dec = kc * jnp.exp(gc[..., -1:] - gc)[..., None]
    g_last = jnp.exp(gc[..., -1])

    def step(S, xs):
        u_i, w_i, qk_i, qd_i, kd_i, gl_i = xs
        v_new = u_i - jnp.einsum('bhcd,bhde->bhce', w_i, S)
        o = jnp.einsum('bhcd,bhde->bhce', qd_i, S) + jnp.einsum('bhij,bhje->bhie', qk_i, v_new)
        S = S * gl_i[..., None, None] + jnp.einsum('bhcd,bhce->bhde', kd_i, v_new)
        return S, o

    xs = (jnp.moveaxis(u, 2, 0), jnp.moveaxis(w, 2, 0), jnp.moveaxis(qk, 2, 0),
          jnp.moveaxis(q_dec, 2, 0), jnp.moveaxis(k_dec, 2, 0), jnp.moveaxis(g_last, 2, 0))
    s_final, o = lax.scan(step, s0, xs)
    o = jnp.moveaxis(o, 0, 2).reshape(Bt, H, T, DV)
    return o, s_final


def retention_log_decay():
    return jnp.log1p(-jnp.exp2(-5.0 - jnp.arange(H_B, dtype=jnp.float32)))


def retention_chunked(q, k, v, log_gamma, s0):
    Bt, H, T, DK = q.shape
    DV = v.shape[-1]
    C = CHUNK_B
    N = T // C
    pos = jnp.arange(C, dtype=jnp.float32)
    diff = pos[:, None] - pos[None, :]
    d_mask = jnp.where(diff >= 0, jnp.exp(log_gamma[:, None, None] * jnp.maximum(diff, 0.0)), 0.0)
    xi = jnp.exp(log_gamma[:, None] * (pos + 1.0))
    zeta = jnp.exp(log_gamma[:, None] * (C - 1.0 - pos))
    g_chunk = jnp.exp(log_gamma * C)
    qc = q.reshape(Bt, H, N, C, DK)
    kc = k.reshape(Bt, H, N, C, DK)
    vc = v.reshape(Bt, H, N, C, DV)
    scores = jnp.einsum('bhnid,bhnjd->bhnij', qc, kc) * d_mask[:, None]
    inner = jnp.einsum('bhnij,bhnje->bhnie', scores, vc)
    ds = jnp.einsum('bhncd,bhnce->bhnde', kc * zeta[:, None, :, None], vc)

    def step(s, ds_n):
        return s * g_chunk[:, None, None] + ds_n, s

    s_final, s_prev = lax.scan(step, s0, jnp.moveaxis(ds, 2, 0))
    s_prev = jnp.moveaxis(s_prev, 0, 2)
    cross = jnp.einsum('bhncd,bhnde->bhnce', qc * xi[:, None, :, None], s_prev)
    return (inner + cross).reshape(Bt, H, T, DV), s_final


def mixer(h, s_delta0, s_ret0, use_rope, w_in, conv_w, a_log, dt_bias, norm_a_w, gn_w, gn_b, w_o):
    Bt, T, _ = h.shape
    f32 = jnp.float32
    p = jnp.einsum('btd,de->bte', h, w_in).astype(f32)
    sizes = (3 * D_A, D_A, 2 * H_A, 2 * H_A, H_B * DK_B, H_B * DK_B, D_B, D_B)
    split_pts = np.cumsum(sizes)[:-1].tolist()
    qkv_a, gate_a, alpha_raw, beta_raw, q_b, k_b, v_b, gate_b = jnp.split(p, split_pts, axis=-1)

    qkv_a = short_conv(qkv_a, conv_w.astype(f32))
    qa, ka, va = jnp.split(qkv_a, 3, axis=-1)
    qa = l2norm(heads(qa, H_A)) * (DK_A ** -0.5)
    ka = l2norm(heads(ka, H_A))
    va = heads(va, H_A)
    g_log = -jnp.exp(a_log.astype(f32)) * jax.nn.softplus(alpha_raw.reshape(Bt, T, 2, H_A) + dt_bias.astype(f32))
    g_log = g_log.transpose(2, 0, 3, 1)
    beta = jax.nn.sigmoid(beta_raw.reshape(Bt, T, 2, H_A)).transpose(2, 0, 3, 1)
    sd0 = s_delta0.astype(f32)
    o_f, sd_f = gated_delta_chunked(qa, ka, va, g_log[0], beta[0], sd0[:, 0])
    o_bw, sd_b = gated_delta_chunked(flip_t(qa), flip_t(ka), flip_t(va), flip_t(g_log[1]), flip_t(beta[1]), sd0[:, 1])
    o_a = o_f + flip_t(o_bw)
    o_a = o_a * lax.rsqrt(jnp.mean(jnp.square(o_a), -1, keepdims=True) + 1e-6) * norm_a_w.astype(f32)
    o_a = merge_heads(o_a) * jax.nn.silu(gate_a)

    qb = heads(q_b, H_B) * (DK_B ** -0.5)
    kb = heads(k_b, H_B)
    vb = heads(v_b, H_B)
    if use_rope:
        qb = axial_rope(qb)
        kb = axial_rope(kb)
    lg = retention_log_decay()
    sr0 = s_ret0.astype(f32)
    r_f, sr_f = retention_chunked(qb, kb, vb, lg, sr0[:, 0])
    r_bw, sr_b = retention_chunked(flip_t(qb), flip_t(kb), flip_t(vb), lg[::-1], sr0[:, 1])
    o_r = r_f + flip_t(r_bw)
    o_r = merge_heads(layer_norm(o_r, eps=1e-5)) * gn_w.astype(f32) + gn_b.astype(f32)
    o_r = o_r * jax.nn.silu(gate_b)

    y = jnp.einsum('bte,ed->btd', jnp.concatenate([o_a, o_r], -1), w_o.astype(f32)).astype(h.dtype)
    return y, jnp.stack([sd_f, sd_b], 1), jnp.stack([sr_f, sr_b], 1)


def trunk(x, cond, s_delta, s_ret, use_rope, w_mod, b_mod, w_in, conv_w, a_log, dt_bias, norm_a_w,
          gn_w, gn_b, w_o, ln1_w, ln1_b, w_ff1, b_ff1, w_ff2, b_ff2, ln2_w, ln2_b):
    new_sd, new_sr = [], []
    for l in range(DEPTH):
        mod = jnp.einsum('bd,de->be', jax.nn.silu(cond), w_mod[l]) + b_mod[l]
        sh1, sc1, g1, sh2, sc2, g2 = jnp.split(mod[:, None, :], 6, axis=-1)
        h = layer_norm(x) * (1.0 + sc1) + sh1
        y, sd, sr = mixer(h, s_delta[:, l], s_ret[:, l], use_rope, w_in[l], conv_w[l], a_log[l], dt_bias[l],
                          norm_a_w[l], gn_w[l], gn_b[l], w_o[l])
        x = layer_norm(ALPHA * x + g1 * y, ln1_w[l], ln1_b[l])
        h = layer_norm(x) * (1.0 + sc2) + sh2
        f = jnp.einsum('btf,fd->btd', jnp.square(jax.nn.relu(jnp.einsum('btd,df->btf', h, w_ff1[l]) + b_ff1[l])),
                       w_ff2[l]) + b_ff2[l]
        x = layer_norm(ALPHA * x + g2 * f, ln2_w[l], ln2_b[l])
        new_sd.append(sd)
        new_sr.append(sr)
    return x, jnp.stack(new_sd, 1), jnp.stack(new_sr, 1)


def setup_inputs(seed: int = 0) -> dict:
    key = jax.random.key(seed)
    ks = jax.random.split(key, 24)
    nrm = jax.random.normal
    f32 = jnp.float32
    dt = jnp.exp(jax.random.uniform(ks[9], (DEPTH, 2, H_A), f32, np.log(1e-3), np.log(1e-1)))
    return {
        "x_prompt": nrm(ks[0], (BATCH, SEQ, D_MODEL), f32),
        "x_sample": nrm(ks[1], (DEC_BATCH, DEC_SEQ, D_MODEL), f32),
        "c": nrm(ks[2], (DEC_BATCH, D_MODEL), f32),
        "state_delta": 0.1 * nrm(ks[3], (DEC_BATCH, DEPTH, 2, H_A, DK_A, DV_A), f32),
        "state_ret": 0.5 * nrm(ks[4], (DEC_BATCH, DEPTH, 2, H_B, DK_B, DV_B), f32),
        "c_ctx": nrm(ks[5], (D_MODEL,), f32),
        "w_mod": 0.5 * D_MODEL ** -0.5 * nrm(ks[6], (DEPTH, D_MODEL, 6 * D_MODEL), f32),
        "b_mod": 0.02 * nrm(ks[7], (DEPTH, 6 * D_MODEL), f32),
        "w_in": D_MODEL ** -0.5 * nrm(ks[8], (DEPTH, D_MODEL, D_IN), f32),
        "conv_w": CONV_K ** -0.5 * nrm(ks[10], (DEPTH, CONV_K, 3 * D_A), f32),
        "a_log": jnp.log(jax.random.uniform(ks[11], (DEPTH, 2, H_A), f32, 1.0, 16.0)),
        "dt_bias": dt + jnp.log(-jnp.expm1(-dt)),
        "norm_a_w": 1.0 + 0.02 * nrm(ks[12], (DEPTH, DV_A), f32),
        "gn_w": 1.0 + 0.02 * nrm(ks[13], (DEPTH, D_B), f32),
        "gn_b": 0.02 * nrm(ks[14], (DEPTH, D_B), f32),
        "w_o": BETA_DN * D_MIX ** -0.5 * nrm(ks[15], (DEPTH, D_MIX, D_MODEL), f32),
        "ln1_w": 1.0 + 0.02 * nrm(ks[16], (DEPTH, D_MODEL), f32),
        "ln1_b": 0.02 * nrm(ks[17], (DEPTH, D_MODEL), f32),
        "w_ff1": D_MODEL ** -0.5 * nrm(ks[18], (DEPTH, D_MODEL, D_FF), f32),
        "b_ff1": 0.02 * nrm(ks[19], (DEPTH, D_FF), f32),
        "w_ff2": BETA_DN * D_FF ** -0.5 * nrm(ks[20], (DEPTH, D_FF, D_MODEL), f32),
        "b_ff2": 0.02 * nrm(ks[21], (DEPTH, D_MODEL), f32),
        "ln2_w": 1.0 + 0.02 * nrm(ks[22], (DEPTH, D_MODEL), f32),
        "ln2_b": 0.02 * nrm(ks[23], (DEPTH, D_MODEL), f32),
    }


def reference(x_prompt, x_sample, c, state_delta, state_ret, c_ctx, w_mod, b_mod, w_in, conv_w, a_log, dt_bias,
              norm_a_w, gn_w, gn_b, w_o, ln1_w, ln1_b, w_ff1, b_ff1, w_ff2, b_ff2, ln2_w, ln2_b):
    weights = (w_mod, b_mod, w_in, conv_w, a_log, dt_bias, norm_a_w, gn_w, gn_b, w_o,
               ln1_w, ln1_b, w_ff1, b_ff1, w_ff2, b_ff2, ln2_w, ln2_b)
    bp = x_prompt.shape[0]
    zero_sd = jnp.zeros((bp, DEPTH, 2, H_A, DK_A, DV_A), jnp.float32)
    zero_sr = jnp.zeros((bp, DEPTH, 2, H_B, DK_B, DV_B), jnp.float32)
    y_prompt, new_state_delta, new_state_ret = trunk(x_prompt, c_ctx[None, :], zero_sd, zero_sr, False, *weights)
    y_sample, _, _ = trunk(x_sample, c, state_delta, state_ret, True, *weights)
    return (y_prompt, y_sample, new_state_delta.astype(x_prompt.dtype), new_state_ret.astype(x_prompt.dtype))
```

```python
import numpy as np
from contextlib import ExitStack
import concourse.bass as bass
import concourse.mybir as mybir
from concourse.bass_utils import run_bass_kernel_spmd

F32 = mybir.dt.float32
BF16 = mybir.dt.bfloat16
AF = mybir.ActivationFunctionType
ALU = mybir.AluOpType

PE, ACT, DVE, POOL, SP = "tensor", "scalar", "vector", "gpsimd", "sync"

D = 1024
TS = 4096
OWN = 2048
TP = 256
DIN = 3600
DFF = 4096
ALPHA = 2.0 ** 0.25
NEGBIG = -30000.0
NOREORDER = set()
HLFET_BLOCKS = set([1, 4, 5])


class Res:
    __slots__ = ("w", "rs")

    def __init__(self):
        self.w = None
        self.rs = []


class Op:
    __slots__ = ("eng", "fn", "deps", "dma_sem", "token", "signal", "epoch", "cost", "lat", "is_dma", "tag")


class Prog:
    NDMA = 12

    def __init__(self, nc, stack):
        self.nc = nc
        self.ops = []
        self.epoch = 0
        self.esem = {}
        self.ecnt = {}
        for e in (PE, ACT, DVE, POOL):
            self.esem[e] = stack.enter_context(nc.semaphore("s_" + e))
            self.ecnt[e] = 0
        self.dsem = {}
        self.dcnt = {}
        self.dlast = {}
        self.drr = {}
        for q in (SP, ACT, POOL):
            self.dsem[q] = [stack.enter_context(nc.semaphore("d_%s%d" % (q, i))) for i in range(self.NDMA)]
            self.dcnt[q] = [0] * self.NDMA
            self.dlast[q] = [None] * self.NDMA
            self.drr[q] = 0
        self.waited = {e: {} for e in (PE, ACT, DVE, POOL, SP)}
        self.n_inst = 0
        self.reorder = True
        self.sim_total = 0.0

    def op(self, eng, fn, reads=(), writes=(), cost=300.0, lat=None):
        op = Op()
        op.eng = eng
        op.fn = fn
        op.deps = []
        op.dma_sem = None
        op.token = None
        op.signal = False
        op.epoch = self.epoch
        op.cost = cost
        op.lat = cost if lat is None else lat
        op.is_dma = False
        op.tag = ""
        deps = op.deps
        for r in reads:
            if r.w is not None:
                deps.append(r.w)
        for r in writes:
            if r.w is not None:
                deps.append(r.w)
            deps.extend(r.rs)
        for r in reads:
            r.rs.append(op)
        for r in writes:
            r.w = op
            r.rs = []
        self.ops.append(op)
        return op

    def dma(self, q, out, in_, reads=(), writes=(), **kw):
        def fn(e):
            return e.dma_start(out=out, in_=in_, **kw)
        nbytes = 1
        for d_ in out.shape:
            nbytes *= d_
        nbytes *= 2 if out.dtype == BF16 else 4
        op = self.op(q, fn, reads, writes, cost=(400.0 if q == POOL else 80.0), lat=2000.0 + nbytes / 150.0)
        op.tag = "dma:%s<-%s" % (out.name, in_.name)
        op.is_dma = True
        op.signal = True
        return op

    def schedule(self, ops):
        import heapq
        n = len(ops)
        idx = {id(o): i for i, o in enumerate(ops)}
        succs = [[] for _ in range(n)]
        npred = [0] * n
        for i, o in enumerate(ops):
            ps = set()
            for d in o.deps:
                if d.epoch == self.epoch:
                    j = idx[id(d)]
                    if j != i:
                        ps.add(j)
            npred[i] = len(ps)
            for j in ps:
                succs[j].append(i)
        ready_t = [0.0] * n
        engs = (PE, ACT, DVE, POOL, SP)
        blev = [0.0] * n
        if self.epoch in HLFET_BLOCKS:
            for i in range(n - 1, -1, -1):
                m_ = 0.0
                for j in succs[i]:
                    if blev[j] > m_:
                        m_ = blev[j]
                blev[i] = m_ + ops[i].lat + 120.0
        prio = [(-blev[i], i) for i in range(n)]
        pend = {e: [] for e in engs}
        avail = {e: [] for e in engs}
        free_t = {e: 0.0 for e in engs}
        for i in range(n):
            if npred[i] == 0:
                heapq.heappush(avail[ops[i].eng], prio[i])
        order = []
        done = 0
        crit = [-1] * n
        rdy_from = [-1] * n
        st_t = [0.0] * n
        last_on = {e: -1 for e in engs}
        while done < n:
            best = None
            for e in engs:
                pe_, av = pend[e], avail[e]
                while pe_ and pe_[0][0] <= free_t[e]:
                    heapq.heappush(av, prio[heapq.heappop(pe_)[1]])
                if av:
                    cand = (free_t[e], 0, av[0][1], e)
                elif pe_:
                    cand = (pe_[0][0], 1, pe_[0][1], e)
                else:
                    continue
                if best is None or cand < best:
                    best = cand
            start, kind, i, e = best
            if kind == 0:
                heapq.heappop(avail[e])
            else:
                heapq.heappop(pend[e])
            o = ops[i]
            start = max(start, ready_t[i], free_t[e])
            if ready_t[i] >= free_t[e]:
                crit[i] = rdy_from[i]
            else:
                crit[i] = last_on[e]
            last_on[e] = i
            st_t[i] = start
            free_t[e] = start + o.cost
            fin = start + o.lat + 120.0
            order.append(i)
            done += 1
            for j in succs[i]:
                f_ = (start + o.cost) if (e == PE and ops[j].eng == PE) else fin
                if f_ > ready_t[j]:
                    ready_t[j] = f_
                    rdy_from[j] = i
                npred[j] -= 1
                if npred[j] == 0:
                    heapq.heappush(pend[ops[j].eng], (ready_t[j], j))
        self.sim_time = max(free_t.values())
        if getattr(self, "debug_crit", False) and order:
            import collections
            i = order[-1]
            agg = collections.Counter(); cnt_ = collections.Counter()
            prev_t = st_t[i] + ops[i].cost
            while i >= 0:
                key = ops[i].eng[:3] + ":" + ops[i].tag
                agg[key] += prev_t - st_t[i]
                cnt_[key] += 1
                prev_t = st_t[i]
                i = crit[i]
            for k_, v_ in agg.most_common(25):
                print("[crit] %-60s %8.1f us  n=%d" % (k_, v_ / 1e3, cnt_[k_]))
        tot = {e: 0.0 for e in engs}
        cnt = {e: 0 for e in engs}
        for o in ops:
            tot[o.eng] += o.cost
            cnt[o.eng] += 1
        print("[prog]   busy us: " + " ".join("%s=%.0f(%d)" % (e, tot[e] / 1e3, cnt[e]) for e in engs))
        return [ops[i] for i in order]

    def flush(self):
        nc = self.nc
        ops = self.schedule(self.ops) if (self.reorder and self.epoch not in NOREORDER) else self.ops
        ep = self.epoch
        for op in ops:
            if op.is_dma:
                q = op.eng
                i = self.drr[q]
                self.drr[q] = (i + 1) % self.NDMA
                prev = self.dlast[q][i]
                if prev is not None:
                    op.deps.append(prev)
                self.dcnt[q][i] += 16
                op.dma_sem = self.dsem[q][i]
                op.token = (op.dma_sem, self.dcnt[q][i])
                self.dlast[q][i] = op
        for op in ops:
            nd = []
            for d in op.deps:
                if d.epoch != ep:
                    continue
                if d.eng == PE and op.eng == PE and d.dma_sem is None and op.dma_sem is None:
                    continue
                nd.append(d)
                if d.dma_sem is None:
                    d.signal = True
            op.deps = nd
        for op in ops:
            if op.dma_sem is None and op.signal:
                self.ecnt[op.eng] += 1
                op.token = (self.esem[op.eng], self.ecnt[op.eng])
        by_eng = {e: [] for e in (PE, ACT, DVE, POOL, SP)}
        for op in ops:
            by_eng[op.eng].append(op)
        self.n_inst += len(ops)

        def emit(ename):
            lst = by_eng[ename]
            waited = self.waited[ename]

            def body(e):
                for op in lst:
                    for d in op.deps:
                        sem, val = d.token
                        k = id(sem)
                        if waited.get(k, 0) < val:
                            e.wait_ge(sem, val)
                            waited[k] = val
                    inst = op.fn(e)
                    if op.signal:
                        if op.dma_sem is not None:
                            inst.then_inc(op.dma_sem, 16)
                        else:
                            inst.then_inc(self.esem[ename], 1)
                if ename in self.dsem:
                    for i, s in enumerate(self.dsem[ename]):
                        v = self.dcnt[ename][i]
                        if v > 0 and waited.get(id(s), 0) < v:
                            e.wait_ge(s, v)
                            waited[id(s)] = v
            return body

        with nc.Block() as blk:
            for ename in (SP, POOL, ACT, DVE, PE):
                if by_eng[ename] or ename in self.dsem:
                    getattr(blk, ename)(emit(ename))
        self.sim_total += getattr(self, "sim_time", 0.0)
        print("[prog] block %d: %d ops, sim %.0f us" % (self.epoch, len(ops), getattr(self, "sim_time", 0.0) / 1e3))
        self.ops = []
        self.epoch += 1


class Ring:
    def __init__(self, items):
        self.items = items
        self.i = 0

    def get(self):
        t = self.items[self.i % len(self.items)]
        self.i += 1
        return t


def build_program(debug=False):
    nc = bass.Bass("TRN2", target_bir_lowering=False)

    def din(name, shape, dt=F32):
        return nc.dram_tensor(name, list(shape), dt, kind="ExternalInput").ap()

    def dout(name, shape, dt=F32):
        return nc.dram_tensor(name, list(shape), dt, kind="ExternalOutput").ap()

    def dscr(name, shape, dt):
        return nc.dram_tensor(name, list(shape), dt, kind="Internal").ap()

    xs = din("xs", [TS, D])
    xp = din("xp", [2 * TP, D])
    cond = din("cond", [2, D])
    sd0 = din("sd0", [2, 4 * 128, 128])
    sr0 = din("sr0", [2, 4 * 64, 128])
    w_mod = din("w_mod", [D, 6 * D])
    b_mod = din("b_mod", [6 * D])
    w_in = din("w_in", [D, DIN])
    conv_w = din("conv_w", [5, 1536])
    a_log = din("a_log", [8])
    dt_bias = din("dt_bias", [8])
    norm_a_w = din("norm_a_w", [128])
    gn_w = din("gn_w", [512])
    gn_b = din("gn_b", [512])
    w_o = din("w_o", [D, D])
    ln1_w = din("ln1_w", [D])
    ln1_b = din("ln1_b", [D])
    w_ff1 = din("w_ff1", [D, DFF])
    b_ff1 = din("b_ff1", [DFF])
    w_ff2 = din("w_ff2", [DFF, D])
    b_ff2 = din("b_ff2", [D])
    ln2_w = din("ln2_w", [D])
    ln2_b = din("ln2_b", [D])
    rope_cos = din("rope_cos", [TS, 32])
    rope_sin = din("rope_sin", [TS, 32])
    c_ut8 = din("c_ut8", [2, 64, 512])
    c_neg = din("c_neg", [2, 64, 512])
    c_dmt = din("c_dmt", [2, 128, 512])
    c_xi = din("c_xi", [2, 64, 512])
    c_zeta = din("c_zeta", [128, 8])
    c_gch = din("c_gch", [64, 8])

    ys = dout("ys", [OWN, D])
    yp = dout("yp", [2 * TP, D])
    nsd = dout("nsd", [2, 2, 4 * 128, 128])
    nsr = dout("nsr", [2, 2, 4 * 64, 128])

    seqs = []
    for si, (nm, T, own) in enumerate((("s", TS, OWN), ("p0", TP, TP), ("p1", TP, TP))):
        q = dict(i=si, name=nm, T=T, own=own, rope=(si == 0), cond=(0 if si == 0 else 1))
        q["x"] = xs if si == 0 else xp[(si - 1) * TP:si * TP, :]
        q["y"] = ys if si == 0 else yp[(si - 1) * TP:si * TP, :]
        q["PQ"] = dscr("PQ" + nm, [1536, T], BF16)
        q["QK"] = dscr("QK" + nm, [1024, T], BF16)
        q["KV"] = dscr("KV" + nm, [T, 1024], BF16)
        q["TM"] = dscr("TM" + nm, [T, 1536], BF16)
        q["RQK"] = dscr("RQK" + nm, [512, T], BF16)
        q["RK"] = dscr("RK" + nm, [T, 256], BF16)
        q["OA"] = dscr("OA" + nm, [2, own, 512], F32)
        q["OR"] = dscr("OR" + nm, [2, own, 512], F32)
        q["X1"] = dscr("X1" + nm, [own, D], F32)
        q["S6Q"] = dscr("S6Q" + nm, [2, T // 64, 64, 512], BF16)
        q["nch"] = T // 64
        seqs.append(q)

    with ExitStack() as st0:
        P = Prog(nc, st0)

        def I(eng, meth, reads, writes, *a, **k):
            o_ = k.get("out", a[0] if a else None)
            fr = 1
            for d_ in o_.shape[1:]:
                fr *= d_
            if eng == PE:
                l_ = k.get("lhsT", a[1] if len(a) > 1 else None)
                c_ = (max(64, fr) + 8) / 2.4 * (4.0 if (l_ is not None and l_.dtype == F32) else 1.0)
                lat = c_ + 150.0
            elif eng == ACT:
                c_ = (224 + fr) / 1.2
                lat = c_
            elif eng == DVE:
                c_ = (150 + fr) / 0.96
                lat = c_
            else:
                c_ = (150 + 2 * fr) / 1.2
                lat = c_
            op_ = P.op(eng, lambda e: getattr(e, meth)(*a, **k), reads, writes, cost=c_, lat=lat)
            op_.tag = "%s:%s" % (meth, o_.name)

        def sbt(stack, name, shape, dt=F32):
            return stack.enter_context(nc.sbuf_tensor(name, list(shape), dt))

        def pst(stack, name, shape, dt=F32):
            return stack.enter_context(nc.psum_tensor(name, list(shape), dt))

        def ring(stack, name, shape, dt, n, psum=False):
            items = []
            for i in range(n):
                t = (pst if psum else sbt)(stack, "%s%d" % (name, i), shape, dt)
                items.append((t, Res()))
            return Ring(items)

        identf = sbt(st0, "identf", [128, 128]); r_identf = Res()
        identb = sbt(st0, "identb", [128, 128], BF16); r_identb = Res()
        onesb = sbt(st0, "onesb", [128, 128], BF16); r_onesb = Res()
        onesq = sbt(st0, "onesq", [128, 128], BF16); r_onesq = Res()
        onesf = sbt(st0, "onesf", [64, 128]); r_onesf = Res()
        modc = sbt(st0, "modc", [128, 6, 8, 2]); r_modc = Res()
        gates = sbt(st0, "gates", [128, 2, 2, D]); r_gates = Res()
        dtb = sbt(st0, "dtb", [8, 1]); negA = sbt(st0, "negA", [8, 1]); r_gc = Res()
        stG = ExitStack()
        G = []
        for q in seqs:
            G.append((sbt(stG, "G" + q["name"], [64, q["nch"], 24]), Res()))
        G128 = []
        for q in seqs:
            G128.append((sbt(stG, "GG" + q["name"], [128, q["nch"] // 2, 24]), Res()))

        I(POOL, "memset", [], [r_identf], identf[:], 0.0)
        I(POOL, "affine_select", [r_identf], [r_identf], out=identf[:], in_=identf[:], pattern=[[-1, 128]],
                                             compare_op=ALU.not_equal, fill=1.0, base=0, channel_multiplier=1)
        I(DVE, "tensor_copy", [r_identf], [r_identb], out=identb[:], in_=identf[:])
        I(POOL, "memset", [], [r_onesb], onesb[:], 1.0)
        I(POOL, "memset", [], [r_onesq], onesq[:], 128.0)
        I(POOL, "memset", [], [r_onesf], onesf[:], 1.0)
        P.dma(SP, dtb[:], dt_bias.rearrange("(p o) -> p o", o=1), writes=[r_gc])
        P.dma(SP, negA[:], a_log.rearrange("(p o) -> p o", o=1), writes=[r_gc])
        I(ACT, "activation", [r_gc], [r_gc], out=negA[:], in_=negA[:], func=AF.Exp)
        I(DVE, "tensor_scalar", [r_gc], [r_gc], out=negA[:], in0=negA[:], scalar1=-1.0, scalar2=None, op0=ALU.mult)

        stW = ExitStack()
        w_in_sb = sbt(stW, "w_in_sb", [128, 8, DIN], BF16); r_win = [Res() for _ in range(8)]
        for kc in range(8):
            P.dma(POOL, w_in_sb[:, kc, :], w_in[kc * 128:(kc + 1) * 128, :], writes=[r_win[kc]])
        with ExitStack() as st:
            psA = ring(st, "ps0_", [128, 512], F32, 4, psum=True)
            crow = sbt(st, "crow", [16, 128]); r_crow = Res()
            scT = sbt(st, "scT", [128, 16]); r_scT = Res()
            brow = sbt(st, "brow", [48, 128]); r_brow = Res()
            bcol = sbt(st, "bcol", [128, 48]); r_bcol = Res()
            wm = ring(st, "wm", [128, 8, D], F32, 2)
            wm_res = {}
            gbt = ring(st, "gbt", [128, 128], F32, 2)
            P.dma(SP, crow[:], cond.rearrange("c (k p) -> (c k) p", p=128), writes=[r_crow])
            P.dma(SP, brow[:], b_mod.rearrange("(a p) -> a p", p=128), writes=[r_brow])
            ps, rps = psA.get()
            I(PE, "matmul", [r_crow, r_identf], [rps], ps[:, 0:16], lhsT=crow[:], rhs=identf[0:16, 0:16], start=True, stop=True)
            I(ACT, "activation", [rps], [r_scT], out=scT[:], in_=ps[:, 0:16], func=AF.Exp, scale=-1.0)
            I(DVE, "tensor_scalar", [r_scT], [r_scT], out=scT[:], in0=scT[:], scalar1=1.0, scalar2=None, op0=ALU.add)
            I(DVE, "reciprocal", [r_scT], [r_scT], out=scT[:], in_=scT[:])
            I(DVE, "tensor_tensor", [r_scT, rps], [r_scT], out=scT[:], in0=scT[:], in1=ps[:, 0:16], op=ALU.mult)
            ps, rps = psA.get()
            I(PE, "matmul", [r_brow, r_identf], [rps], ps[:, 0:48], lhsT=brow[:], rhs=identf[0:48, 0:48], start=True, stop=True)
            I(DVE, "tensor_copy", [rps], [r_bcol], out=bcol[:], in_=ps[:, 0:48])
            scv = scT[:].rearrange("p (c k) -> p c k", c=2)
            for blk in range(6):
                wt, rw0 = wm.get()
                if id(rw0) not in wm_res:
                    wm_res[id(rw0)] = [Res() for _ in range(8)]
                rw = wm_res[id(rw0)]
                for kc in range(8):
                    P.dma(SP if kc % 2 == 0 else ACT, wt[:, kc, :], w_mod[kc * 128:(kc + 1) * 128, blk * D:(blk + 1) * D], writes=[rw[kc]])
                ps, rps = psA.get()
                for ft in range(8):
                    for kc in range(8):
                        I(PE, "matmul", [rw[kc], r_scT], [rps], ps[:, ft * 2:ft * 2 + 2], lhsT=wt[:, kc, ft * 128:(ft + 1) * 128], rhs=scv[:, :, kc],
                            start=(kc == 0), stop=(kc == 7))
                I(DVE, "tensor_tensor", [rps, r_bcol], [r_modc], out=modc[:, blk, :, :], in0=ps[:, 0:16].rearrange("p (f c) -> p f c", c=2),
                    in1=bcol[:, blk * 8:(blk + 1) * 8].unsqueeze(2).broadcast_to([128, 8, 2]), op=ALU.add)
                if blk in (1, 4):
                    I(DVE, "tensor_scalar", [r_modc], [r_modc], out=modc[:, blk, :, :], in0=modc[:, blk, :, :], scalar1=1.0, scalar2=None, op0=ALU.add)
            for wi, blk in enumerate((2, 5)):
                for c in range(2):
                    for half in range(2):
                        ps, rps = psA.get()
                        for f4 in range(4):
                            ft = half * 4 + f4
                            gb, rgb = gbt.get()
                            I(DVE, "tensor_copy", [r_modc], [rgb], out=gb[:], in_=modc[:, blk, ft, c:c + 1].broadcast_to([128, 128]))
                            I(PE, "matmul", [rgb, r_identf], [rps], ps[:, f4 * 128:(f4 + 1) * 128], lhsT=gb[:], rhs=identf[:], start=True, stop=True)
                        I(ACT, "copy", [rps], [r_gates], out=gates[:, wi, c, half * 512:(half + 1) * 512], in_=ps[:])
            P.flush()

        with ExitStack() as st:
            cosT = sbt(st, "cosT", [128, TS // 128, 32]); sinT = sbt(st, "sinT", [128, TS // 128, 32]); r_rope = Res()
            P.dma(SP, cosT[:], rope_cos.rearrange("(n p) f -> p n f", p=128), writes=[r_rope])
            P.dma(SP, sinT[:], rope_sin.rearrange("(n p) f -> p n f", p=128), writes=[r_rope])
            psA = ring(st, "psA_", [128, 512], F32, 6, psum=True)
            psT = ring(st, "psT_", [128, 1024], BF16, 2, psum=True)
            xt_r = ring(st, "xt", [128, D], F32, 3)
            xn_r = ring(st, "xn", [128, D], BF16, 2)
            st_r = ring(st, "bst", [128, 2, 6], F32, 2)
            mv_r = ring(st, "bmv", [128, 4], F32, 2)
            hT_r = ring(st, "hT", [128, 8, 512], BF16, 2)
            pq_r = ring(st, "pq", [128, 512], BF16, 4)
            tm_r = ring(st, "tm", [128, 1536], BF16, 2)
            qkf_r = ring(st, "qkf", [128, 512], F32, 2)
            rt_r = ring(st, "rt", [128, 4, 256], F32, 1)
            rqk_r = ring(st, "rqk", [128, 512], BF16, 3)
            rqT_r = ring(st, "rqT", [128, 4, 128], BF16, 3)
            gt_r = ring(st, "gt", [8, 5, 512], F32, 1)

            def layernorm_stats(xt, rx, mv, rmv, stt, rst, eps):
                for c2 in range(2):
                    I(DVE, "bn_stats", [rx], [rst], out=stt[:, c2, :], in_=xt[:, c2 * 512:(c2 + 1) * 512])
                I(DVE, "bn_aggr", [rst], [rmv], out=mv[:, 0:2], in_=stt[:])
                I(ACT, "activation", [rmv], [rmv], out=mv[:, 2:3], in_=mv[:, 1:2], func=AF.Ln, bias=eps)
                I(ACT, "activation", [rmv], [rmv], out=mv[:, 2:3], in_=mv[:, 2:3], func=AF.Exp, scale=-0.5)

            cwr = sbt(st, "cwr", [60, 128]); r_cwr = Res()
            cwc = sbt(st, "cwc", [128, 60]); r_cwc = Res()
            Dg = sbt(st, "Dg", [128, 60, 128], BF16); r_Dg = Res()
            P.dma(SP, cwr[:], conv_w.rearrange("k (c p) -> (k c) p", p=128), writes=[r_cwr])
            ps, rps = psA.get()
            I(PE, "matmul", [r_cwr, r_identf], [rps], ps[:, 0:60], lhsT=cwr[:], rhs=identf[0:60, 0:60], start=True, stop=True)
            I(DVE, "tensor_copy", [rps], [r_cwc], out=cwc[:], in_=ps[:, 0:60])
            for i in range(60):
                I(DVE, "tensor_scalar", [r_identf, r_cwc], [r_Dg], out=Dg[:, i, :], in0=identf[:], scalar1=cwc[:, i:i + 1], scalar2=None, op0=ALU.mult)
            pw_r = ring(st, "pw", [128, 516], BF16, 4)
            cs_r = ring(st, "cs", [128, 512], F32, 3)
            sq_r = ring(st, "sq", [128, 512], BF16, 3)
            rs_r = ring(st, "rs", [128, 512], F32, 3)
            xh_r = ring(st, "xh", [128, 512], BF16, 4)
            kt_r = ring(st, "kt", [128, 4, 128], BF16, 4)
            def phaseA_tile(q, t0):
                T = q["T"]
                TT = min(512, T)
                nsub = TT // 128
                Gt, rG = G[q["i"]]
                cnd = q["cond"]
                foreign = t0 >= q["own"]
                hT, rhT = hT_r.get()
                for j in range(nsub):
                    ts = t0 + j * 128
                    xt, rx = xt_r.get()
                    P.dma(SP, xt[:], q["x"][ts:ts + 128, :], writes=[rx])
                    stt, rst = st_r.get(); mv, rmv = mv_r.get()
                    layernorm_stats(xt, rx, mv, rmv, stt, rst, 1e-6)
                    xn, rxn = xn_r.get()
                    I(DVE, "tensor_scalar", [rx, rmv], [rxn], out=xn[:], in0=xt[:], scalar1=mv[:, 0:1], scalar2=mv[:, 2:3],
                                                                           op0=ALU.subtract, op1=ALU.mult)
                    pt, rpt = psT.get()
                    for kc in range(8):
                        I(PE, "transpose", [rxn, r_identb], [rpt], pt[:, kc * 128:(kc + 1) * 128], xn[:, kc * 128:(kc + 1) * 128], identb[:])
                    for kc in range(8):
                        I(ACT, "activation", [rpt, r_modc], [rhT], out=hT[:, kc, j * 128:(j + 1) * 128], in_=pt[:, kc * 128:(kc + 1) * 128], func=AF.Identity,
                            scale=modc[:, 1, kc, cnd:cnd + 1], bias=modc[:, 0, kc, cnd:cnd + 1])
                for ct in range(12):
                    if foreign and ct < 4 and t0 != q["own"]:
                        continue
                    ps, rps = psA.get()
                    for kc in range(8):
                        I(PE, "matmul", [r_win[kc], rhT], [rps], ps[:, 0:TT], lhsT=w_in_sb[:, kc, ct * 128:(ct + 1) * 128], rhs=hT[:, kc, 0:TT],
                                                                            start=(kc == 0), stop=(kc == 7))
                    pq, rpq = pq_r.get()
                    if ct % 2 == 0:
                        I(ACT, "copy", [rps], [rpq], out=pq[:, 0:TT], in_=ps[:, 0:TT])
                    else:
                        I(DVE, "tensor_copy", [rps], [rpq], out=pq[:, 0:TT], in_=ps[:, 0:TT])
                    rPQ[(q["i"], t0, ct)] = Res()
                    P.dma(SP, q["PQ"][ct * 128:(ct + 1) * 128, t0:t0 + TT], pq[:, 0:TT], reads=[rpq], writes=[rPQ[(q["i"], t0, ct)]])
                psa_, rpa = psA.get()
                psb_, rpb = psA.get()
                for which, (pp, rp) in enumerate(((psa_, rpa), (psb_, rpb))):
                    c0 = 2048 + which * 8
                    for kc in range(8):
                        I(PE, "matmul", [r_win[kc], rhT], [rp], pp[0:8, 0:TT], lhsT=w_in_sb[:, kc, c0:c0 + 8], rhs=hT[:, kc, 0:TT], start=(kc == 0), stop=(kc == 7))
                pa = psa_[0:8, 0:TT]; pb = psb_[0:8, 0:TT]
                gt, rgt = gt_r.get()
                I(ACT, "activation", [rpa, r_gc], [rgt], out=gt[:, 0, 0:TT], in_=pa, func=AF.Exp, bias=dtb[:, 0:1])
                I(ACT, "activation", [rgt], [rgt], out=gt[:, 0, 0:TT], in_=gt[:, 0, 0:TT], func=AF.Ln, bias=1.0)
                I(DVE, "tensor_scalar", [rgt, r_gc], [rgt], out=gt[:, 0, 0:TT], in0=gt[:, 0, 0:TT], scalar1=negA[:, 0:1], scalar2=None, op0=ALU.mult)
                I(ACT, "activation", [rpb], [rgt], out=gt[:, 3, 0:TT], in_=pb, func=AF.Exp, scale=-1.0)
                I(ACT, "activation", [rgt], [rgt], out=gt[:, 2, 0:TT], in_=gt[:, 3, 0:TT], func=AF.Ln, bias=1.0)
                I(DVE, "tensor_scalar", [rgt], [rgt], out=gt[:, 3, 0:TT], in0=gt[:, 3, 0:TT], scalar1=1.0, scalar2=None, op0=ALU.add)
                I(DVE, "reciprocal", [rgt], [rgt], out=gt[:, 1, 0:TT], in_=gt[:, 3, 0:TT])
                ps, rps = psA.get()
                nc64 = TT // 64
                for c64 in range(nc64):
                    for a in range(3):
                        I(PE, "matmul", [rgt, r_identf], [rps], ps[0:64, c64 * 24 + a * 8:c64 * 24 + a * 8 + 8],
                                                                             lhsT=gt[:, a, c64 * 64:(c64 + 1) * 64], rhs=identf[0:8, 0:8], start=True, stop=True)
                I(DVE, "tensor_copy", [rps], [rG], out=Gt[:, t0 // 64:t0 // 64 + nc64, :], in_=ps[0:64, 0:nc64 * 24].rearrange("p (c a) -> p c a", a=24))
                ps, rps = psA.get()
                Gt2, rG2 = G128[q["i"]]
                for c128 in range(nsub):
                    for a in range(3):
                        I(PE, "matmul", [rgt, r_identf], [rps], ps[:, c128 * 24 + a * 8:c128 * 24 + a * 8 + 8],
                          lhsT=gt[:, a, c128 * 128:(c128 + 1) * 128], rhs=identf[0:8, 0:8], start=True, stop=True)
                I(DVE, "tensor_copy", [rps], [rG2], out=Gt2[:, t0 // 128:t0 // 128 + nsub, :], in_=ps[:, 0:nsub * 24].rearrange("p (c a) -> p c a", a=24))
                for j in range(nsub):
                    ts = t0 + j * 128
                    tm, rtm = tm_r.get()
                    for gi, c0 in enumerate((1536, 2064, 2576, 3088)):
                        if foreign and gi in (0, 3):
                            continue
                        ps, rps = psA.get()
                        for kc in range(8):
                            I(PE, "matmul", [r_win[kc], rhT], [rps], ps[:], lhsT=hT[:, kc, j * 128:(j + 1) * 128], rhs=w_in_sb[:, kc, c0:c0 + 512],
                                                                                  start=(kc == 0), stop=(kc == 7))
                        if gi == 0:
                            I(ACT, "activation", [rps], [rtm], out=tm[:, 0:512], in_=ps[:], func=AF.Silu)
                        elif gi == 2:
                            I(DVE, "tensor_copy", [rps], [rtm], out=tm[:, 512:1024], in_=ps[:])
                        elif gi == 3:
                            I(ACT, "activation", [rps], [rtm], out=tm[:, 1024:1536], in_=ps[:], func=AF.Silu)
                        else:
                            rqk, rrqk = rqk_r.get()
                            if q["rope"]:
                                qkf, rqkf = qkf_r.get()
                                I(ACT, "copy", [rps], [rqkf], out=qkf[:], in_=ps[:])
                                rt, rrt = rt_r.get()
                                xv = qkf[:].rearrange("p (a h f) -> p a h f", a=8, h=2)
                                ov = rqk[:].rearrange("p (a h f) -> p a h f", a=8, h=2)
                                nt = ts // 128
                                cb = cosT[:, nt, :].unsqueeze(1).broadcast_to([128, 8, 32])
                                sb_ = sinT[:, nt, :].unsqueeze(1).broadcast_to([128, 8, 32])
                                rv = [rt[:, k, :].rearrange("p (a f) -> p a f", a=8) for k in range(4)]
                                I(DVE, "tensor_tensor", [rqkf, r_rope], [rrt], out=rv[0], in0=xv[:, :, 0, :], in1=cb, op=ALU.mult)
                                I(DVE, "tensor_tensor", [rqkf, r_rope], [rrt], out=rv[1], in0=xv[:, :, 1, :], in1=sb_, op=ALU.mult)
                                I(DVE, "tensor_tensor", [rqkf, r_rope], [rrt], out=rv[2], in0=xv[:, :, 0, :], in1=sb_, op=ALU.mult)
                                I(DVE, "tensor_tensor", [rqkf, r_rope], [rrt], out=rv[3], in0=xv[:, :, 1, :], in1=cb, op=ALU.mult)
                                I(DVE, "tensor_tensor", [rrt], [rrqk], out=ov[:, :, 0, :], in0=rv[0], in1=rv[1], op=ALU.subtract)
                                I(DVE, "tensor_tensor", [rrt], [rrqk], out=ov[:, :, 1, :], in0=rv[2], in1=rv[3], op=ALU.add)
                            else:
                                I(ACT, "copy", [rps], [rrqk], out=rqk[:], in_=ps[:])
                            P.dma(SP, q["RK"][ts:ts + 128, :], rqk[:, 256:512], reads=[rrqk])
                            pt, rpt = psT.get()
                            for b4 in range(4):
                                I(PE, "transpose", [rrqk, r_identb], [rpt], pt[:, b4 * 128:(b4 + 1) * 128], rqk[:, b4 * 128:(b4 + 1) * 128], identb[:])
                            rqT, rrqT = rqT_r.get()
                            I(DVE, "tensor_copy", [rpt], [rrqT], out=rqT[:].rearrange("p a t -> p (a t)"), in_=pt[:, 0:512])
                            P.dma(SP, q["RQK"][:, ts:ts + 128].rearrange("(a p) t -> p a t", p=128), rqT[:], reads=[rrqT])
                    if foreign:
                        P.dma(SP, q["TM"][ts:ts + 128, 512:1024], tm[:, 512:1024], reads=[rtm])
                    else:
                        P.dma(SP, q["TM"][ts:ts + 128, :], tm[:], reads=[rtm])

            def phaseA2_tile(q, t0):
                T = q["T"]
                TT = min(512, T)
                nsub = TT // 128
                for ct in range(12):
                    if t0 >= q["own"] and ct < 4:
                        continue
                    kind = ct // 4
                    h = ct % 4
                    pw, rpw = pw_r.get()
                    lo = max(t0 - 2, 0); hi = min(t0 + TT + 2, T)
                    if lo > t0 - 2:
                        I(DVE, "memset", [], [rpw], pw[:, 0:2], 0.0)
                    if hi < t0 + TT + 2:
                        I(DVE, "memset", [], [rpw], pw[:, TT + 2:TT + 4], 0.0)
                    rdeps = [rPQ[(q["i"], t_, ct)] for t_ in (t0 - TT, t0, t0 + TT) if (q["i"], t_, ct) in rPQ]
                    P.dma(SP, pw[:, lo - t0 + 2:hi - t0 + 2], q["PQ"][ct * 128:(ct + 1) * 128, lo:hi], reads=rdeps, writes=[rpw])
                    ps, rps = psA.get()
                    for k in range(5):
                        I(PE, "matmul", [r_Dg, rpw], [rps], ps[:, 0:TT], lhsT=Dg[:, k * 12 + ct, :], rhs=pw[:, k:k + TT], start=(k == 0), stop=(k == 4))
                    xh, rxh = xh_r.get()
                    if kind == 2:
                        I(ACT, "activation", [rps], [rxh], out=xh[:, 0:TT], in_=ps[:, 0:TT], func=AF.Silu)
                    else:
                        cs, rcs = cs_r.get()
                        I(ACT, "activation", [rps], [rcs], out=cs[:, 0:TT], in_=ps[:, 0:TT], func=AF.Silu)
                        sq, rsq = sq_r.get()
                        I(DVE, "tensor_tensor", [rcs], [rsq], out=sq[:, 0:TT], in0=cs[:, 0:TT], in1=cs[:, 0:TT], op=ALU.mult)
                        ps2, rps2 = psA.get()
                        ones_ = onesq if kind == 0 else onesb
                        r_ones = r_onesq if kind == 0 else r_onesb
                        I(PE, "matmul", [rsq, r_ones], [rps2], ps2[:, 0:TT], lhsT=ones_[:], rhs=sq[:, 0:TT], start=True, stop=True)
                        rs, rrs = rs_r.get()
                        eps = 1e-6 * (128.0 if kind == 0 else 1.0)
                        I(ACT, "activation", [rps2], [rrs], out=rs[:, 0:TT], in_=ps2[:, 0:TT], func=AF.Ln, bias=eps)
                        I(ACT, "activation", [rrs], [rrs], out=rs[:, 0:TT], in_=rs[:, 0:TT], func=AF.Exp, scale=-0.5)
                        I(DVE, "tensor_tensor", [rcs, rrs], [rxh], out=xh[:, 0:TT], in0=cs[:, 0:TT], in1=rs[:, 0:TT], op=ALU.mult)
                        rb = h * 2 + (1 if kind == 0 else 0)
                        P.dma(SP, q["QK"][rb * 128:(rb + 1) * 128, t0:t0 + TT], xh[:, 0:TT], reads=[rxh])
                    if kind >= 1:
                        pt, rpt = psT.get()
                        for j in range(nsub):
                            I(PE, "transpose", [rxh, r_identb], [rpt], pt[:, j * 128:(j + 1) * 128], xh[:, j * 128:(j + 1) * 128], identb[:])
                        kt, rkt = kt_r.get()
                        if kind == 1:
                            I(DVE, "tensor_copy", [rpt], [rkt], out=kt[:, 0:nsub, :].rearrange("p a t -> p (a t)"), in_=pt[:, 0:nsub * 128])
                        else:
                            I(ACT, "copy", [rpt], [rkt], out=kt[:, 0:nsub, :].rearrange("p a t -> p (a t)"), in_=pt[:, 0:nsub * 128])
                        c0 = (kind - 1) * 512 + h * 128
                        P.dma(SP, q["KV"][t0:t0 + TT, c0:c0 + 128].rearrange("(j p) c -> p j c", p=128), kt[:, 0:nsub, :], reads=[rkt])

            rPQ = {}
            tilesA = []
            for q in seqs:
                TT_ = min(512, q["T"])
                for t0 in range(0, q["T"], TT_):
                    tilesA.append((q, t0, TT_))
            pend = []
            for (q, t0, TT_) in tilesA:
                phaseA_tile(q, t0)
                pend.append((q, t0, TT_))
                for (q2, t2, TT2) in list(pend):
                    nxt = t2 + TT2
                    if nxt >= q2["T"] or (q2["i"], nxt, 11) in rPQ:
                        phaseA2_tile(q2, t2)
                        pend.remove((q2, t2, TT2))
            assert not pend
            P.flush()
        stW.close()

        with ExitStack() as st2:


            with ExitStack() as st:
                psA = ring(st, "psC_", [128, 512], F32, 7, psum=True)
                psR = psA
                psT = ring(st, "psCT_", [128, 1024], BF16, 1, psum=True)
                ut8 = sbt(st, "ut8", [64, 2, 512]); negm = sbt(st, "negm", [64, 2, 512]); r_cst = Res()
                ut8b = sbt(st, "ut8b", [64, 2, 512], BF16); negb = sbt(st, "negb", [64, 2, 512], BF16)
                idb4 = sbt(st, "idb4", [64, 4, 64], BF16)
                for d in range(2):
                    P.dma(SP, ut8[:, d, :], c_ut8[d], writes=[r_cst])
                    P.dma(SP, negm[:, d, :], c_neg[d], writes=[r_cst])
                I(DVE, "tensor_copy", [r_cst], [r_cst], out=ut8b[:], in_=ut8[:])
                I(DVE, "tensor_copy", [r_cst], [r_cst], out=negb[:], in_=negm[:])
                I(DVE, "tensor_copy", [r_identb], [r_cst], out=idb4[:], in_=identb[0:64, 0:64].unsqueeze(1).broadcast_to([64, 4, 64]))
                qk_r = ring(st, "qkc", [128, 8, 64], BF16, 5)
                chains = []
                for q in seqs:
                    Gt, rG = G[q["i"]]
                    nch = q["nch"]
                    for d in range(2):
                        nm = "%s%d" % (q["name"], d)
                        S = [sbt(st, "S%s_%d" % (nm, k), [128, 4, 128]) for k in range(2)]
                        rS = [Res(), Res()]
                        Sb = sbt(st, "Sb" + nm, [128, 4, 128], BF16); rSb = Res()
                        pre = sbt(st, "pre" + nm, [64, nch, 28]); rpre = Res()
                        egl = sbt(st, "egl" + nm, [128, nch, 4]); regl = Res()
                        if q["i"] == 0:
                            P.dma(SP, S[0][:], sd0[d].rearrange("(h k) v -> k h v", k=128), writes=[rS[0]])
                        else:
                            I(POOL, "memset", [], [rS[0]], S[0][:], 0.0)
                        I(ACT, "copy", [rS[0]], [rSb], out=Sb[:], in_=S[0][:])
                        I(DVE, "tensor_copy", [rG], [rpre], out=pre[:, :, 0:4], in_=Gt[:, :, d * 4:d * 4 + 4])
                        nn = nch * 4
                        gsd = sbt(st, "gsd" + nm, [64, nn]); rgsd = Res()
                        I(DVE, "tensor_copy", [rG], [rgsd], out=gsd[:].rearrange("p (c h) -> p c h", h=4), in_=Gt[:, :, d * 4:d * 4 + 4])
                        ps, rps = psA.get()
                        ps2, rps2 = psA.get()
                        I(PE, "matmul", [r_cst, rgsd], [rps], ps[0:64, 0:nn], lhsT=ut8[:, d, 0:64], rhs=gsd[:], start=True, stop=True)
                        I(PE, "matmul", [r_onesf, rgsd], [rps2], ps2[:, 0:nn], lhsT=onesf[:], rhs=gsd[:], start=True, stop=True)
                        gcv = ps[0:64, 0:nn].rearrange("p (c h) -> p c h", h=4)
                        I(DVE, "tensor_copy", [rps], [rpre], out=pre[:, :, 4:8], in_=gcv)
                        b2 = pre[:, :, 8:16].rearrange("p c (h a) -> p c h a", a=2)
                        I(DVE, "tensor_tensor", [rps, rG], [rpre], out=b2[:, :, :, 0], in0=gcv, in1=Gt[:, :, 16 + d * 4:20 + d * 4], op=ALU.add)
                        I(DVE, "tensor_copy", [rps], [rpre], out=b2[:, :, :, 1], in_=gcv)
                        I(ACT, "activation", [rps], [rpre], out=pre[:, :, 20:24], in_=gcv, func=AF.Exp)
                        I(DVE, "tensor_scalar", [rpre], [rpre], out=pre[:, :, 16:20], in0=pre[:, :, 20:24], scalar1=-1.0, scalar2=None, op0=ALU.mult)
                        glv = ps2[0:64, 0:nn].rearrange("p (c h) -> p c h", h=4)
                        I(DVE, "tensor_tensor", [rps2, rpre], [rpre], out=pre[:, :, 24:28], in0=glv, in1=pre[:, :, 4:8], op=ALU.subtract)
                        I(ACT, "activation", [rpre], [rpre], out=pre[:, :, 24:28], in_=pre[:, :, 24:28], func=AF.Exp)
                        I(DVE, "tensor_tensor", [rpre, rG], [rpre], out=pre[:, :, 24:28], in0=pre[:, :, 24:28], in1=Gt[:, :, 8 + d * 4:12 + d * 4], op=ALU.mult)
                        I(ACT, "activation", [rps2], [regl], out=egl[:].rearrange("p c h -> p (c h)"), in_=ps2[:, 0:nn], func=AF.Exp)
                        nown = q["own"] // 64
                        if d == 0:
                            order = list(range(nown))
                        else:
                            order = list(range(nch - 1, -1, -1))
                        npair = nch // 2
                        if d == 0:
                            porder = list(range(nown // 2))
                        else:
                            porder = list(range(npair - 1, -1, -1))
                        chains.append(dict(q=q, d=d, S=S, rS=rS, Sb=Sb, rSb=rSb, pre=pre, rpre=rpre, egl=egl, regl=regl, order=order, pos=0, cur=0, nown=nown,
                                           porder=porder, nm=nm, nch=nch))

                stB0 = ExitStack()
                ut8w = sbt(stB0, "ut8w", [128, 2, 512]); negw = sbt(stB0, "negw", [128, 2, 512]); r_cw = Res()
                ut8wb = sbt(stB0, "ut8wb", [128, 2, 512], BF16); negwb = sbt(stB0, "negwb", [128, 2, 512], BF16)
                idw4 = sbt(stB0, "idw4", [128, 4, 64], BF16)
                for d in range(2):
                    for e_ in range(2):
                        P.dma(SP, ut8w[e_ * 64:(e_ + 1) * 64, d, :], c_ut8[d], writes=[r_cw])
                        P.dma(SP, negw[e_ * 64:(e_ + 1) * 64, d, :], c_neg[d], writes=[r_cw])
                I(DVE, "tensor_copy", [r_cw], [r_cw], out=ut8wb[:], in_=ut8w[:])
                I(DVE, "tensor_copy", [r_cw], [r_cw], out=negwb[:], in_=negw[:])
                I(DVE, "tensor_copy", [r_identb], [r_cw], out=idw4[0:64], in_=identb[0:64, 0:64].unsqueeze(1).broadcast_to([64, 4, 64]))
                I(DVE, "tensor_copy", [r_identb], [r_cw], out=idw4[64:128], in_=identb[64:128, 64:128].unsqueeze(1).broadcast_to([64, 4, 64]))
                for ch in chains:
                    q = ch["q"]; d = ch["d"]; nm = ch["nm"]; nch = ch["nch"]
                    Gt2, rG2 = G128[q["i"]]
                    npair = nch // 2
                    n2 = npair * 4
                    pw_ = sbt(stB0, "prw" + nm, [128, npair, 12]); rpw_ = Res()
                    g2c = sbt(stB0, "g2c" + nm, [128, n2]); rg2c = Res()
                    I(DVE, "tensor_copy", [rG2], [rpw_], out=pw_[:, :, 0:4], in_=Gt2[:, :, d * 4:d * 4 + 4])
                    I(DVE, "tensor_copy", [rG2], [rg2c], out=g2c[:].rearrange("p (c h) -> p c h", h=4), in_=Gt2[:, :, d * 4:d * 4 + 4])
                    ps3, rps3 = psA.get()
                    I(PE, "matmul", [r_cw, rg2c], [rps3], ps3[0:64, 0:n2], lhsT=ut8w[0:64, d, 0:64], rhs=g2c[0:64, :], start=True, stop=True)
                    I(PE, "matmul", [r_cw, rg2c], [rps3], ps3[64:128, 0:n2], lhsT=ut8w[64:128, d, 0:64], rhs=g2c[64:128, :], start=True, stop=True, tile_position=(64, 64))
                    gcw = ps3[:, 0:n2].rearrange("p (c h) -> p c h", h=4)
                    b2w = pw_[:, :, 4:12].rearrange("p c (h a) -> p c h a", a=2)
                    I(DVE, "tensor_tensor", [rps3, rG2], [rpw_], out=b2w[:, :, :, 0], in0=gcw, in1=Gt2[:, :, 16 + d * 4:20 + d * 4], op=ALU.add)
                    I(DVE, "tensor_tensor", [rps3, rG2], [rpw_], out=b2w[:, :, :, 1], in0=gcw, in1=Gt2[:, :, 16 + d * 4:20 + d * 4], op=ALU.add)

                    ch["pw"] = pw_; ch["rpw"] = rpw_
                qkp_r = ring(stB0, "qkp", [128, 8, 128], BF16, 4)
                gu_r = ring(stB0, "gu", [128, 512], BF16, 4)
                dd_r = ring(stB0, "dd", [128, 512], F32, 4)
                ee_r = ring(stB0, "ee", [128, 512], F32, 4)
                pc_r = ring(stB0, "pc", [128, 4, 2, 64], BF16, 8)
                sc_r = ring(stB0, "sc", [128, 4, 64], BF16, 8)
                stg_r = ring(stB0, "stg", [128, 512], BF16, 4)

                def prep_step(ch, m):
                    q = ch["q"]; d = ch["d"]
                    own = (2 * m) < ch["nown"]
                    pw_, rpw_ = ch["pw"], ch["rpw"]
                    t0 = m * 128
                    HV = ((0, None), (64, (64, 64)))
                    qk, rqk = qkp_r.get()
                    if own:
                        P.dma(SP, qk[:], q["QK"][:, t0:t0 + 128].rearrange("(a p) t -> p a t", p=128), writes=[rqk])
                    else:
                        P.dma(SP, qk[:].rearrange("p (h two) t -> p h two t", two=2)[:, :, 0, :],
                              q["QK"][:, t0:t0 + 128].rearrange("(h two p) t -> p h two t", two=2, p=128)[:, :, 0, :], writes=[rqk])
                    stg, rstg = stg_r.get()
                    gu, rgu = gu_r.get()
                    I(DVE, "tensor_tensor", [r_cw, rpw_], [rgu], out=gu[:].rearrange("p (h x) -> p h x", h=4), in0=ut8wb[:, d, :].rearrange("p (h x) -> p h x", h=4),
                      in1=pw_[:, m, 0:4].unsqueeze(2).broadcast_to([128, 4, 128]), op=ALU.mult)
                    dps, rdps = psA.get()
                    for b_, tp in HV:
                        kw = {} if tp is None else {"tile_position": tp}
                        I(PE, "matmul", [r_onesb, rgu], [rdps], dps[b_:b_ + 64, :], lhsT=onesb[b_:b_ + 64, 0:64], rhs=gu[b_:b_ + 64, :], start=True, stop=False, **kw)
                        I(PE, "matmul", [r_identb, r_cw], [rdps], dps[b_:b_ + 64, :], lhsT=identb[b_:b_ + 64, b_:b_ + 64], rhs=negwb[b_:b_ + 64, d, :], start=False, stop=True, **kw)
                    dd, rdd = dd_r.get()
                    I(DVE, "tensor_tensor", [rdps, rpw_], [rdd], out=dd[:].rearrange("p (a x) -> p a x", a=8), in0=dps[:, :].rearrange("p (a x) -> p a x", a=8),
                      in1=pw_[:, m, 4:12].unsqueeze(2).broadcast_to([128, 8, 64]), op=ALU.subtract)
                    ee, ree = ee_r.get()
                    I(ACT, "activation", [rdd], [ree], out=ee[:], in_=dd[:], func=AF.Exp)
                    kps, rkps = psA.get()
                    for e_, (b_, tp) in enumerate(HV):
                        kw = {} if tp is None else {"tile_position": (0, 64)}
                        for h in range(4):
                            if own:
                                I(PE, "matmul", [rqk], [rkps], kps[b_:b_ + 64, h * 128:(h + 1) * 128], lhsT=qk[:, 2 * h, b_:b_ + 64], rhs=qk[:, 2 * h:2 * h + 2, b_:b_ + 64],
                                  start=True, stop=True, **kw)
                            else:
                                I(PE, "matmul", [rqk], [rkps], kps[b_:b_ + 64, h * 128:h * 128 + 64], lhsT=qk[:, 2 * h, b_:b_ + 64], rhs=qk[:, 2 * h, b_:b_ + 64], start=True, stop=True, **kw)
                    pc, rpc = pc_r.get()
                    kv4 = kps[:, :].rearrange("p (h a x) -> p h a x", h=4, a=2)
                    ev4 = ee[:].rearrange("p (h a x) -> p h a x", h=4, a=2)
                    I(DVE, "tensor_tensor", [rkps, ree], [rpc], out=pc[:, :, 0, :], in0=kv4[:, :, 0, :], in1=ev4[:, :, 0, :], op=ALU.mult)
                    if own:
                        I(DVE, "tensor_tensor", [rkps, ree], [rstg], out=stg[:, 256:512].rearrange("p (h x) -> p h x", h=4), in0=kv4[:, :, 1, :], in1=ev4[:, :, 1, :], op=ALU.mult)
                    pt, rpt = psT.get()
                    for b_, tp in HV:
                        kw = {} if tp is None else {"tile_position": tp}
                        for h in range(4):
                            I(PE, "transpose", [rpc, r_identb], [rpt], pt[b_:b_ + 64, h * 64:(h + 1) * 64], pc[b_:b_ + 64, h, 0, :], identb[b_:b_ + 64, b_:b_ + 64], **kw)
                    I(ACT, "copy", [rpt], [rpc], out=pc[:, :, 1, :], in_=pt[:, 0:256].rearrange("p (h x) -> p h x", h=4))
                    sc, rsc = sc_r.get()
                    I(DVE, "tensor_tensor", [rpc, r_cw], [rsc], out=sc[:], in0=idw4[:], in1=pc[:, :, 0, :], op=ALU.subtract)
                    yield
                    for lvl in range(5):
                        xps, rxps = psA.get()
                        for b_, tp in HV:
                            kw = {} if tp is None else {"tile_position": tp}
                            for h in range(4):
                                if lvl < 4:
                                    I(PE, "matmul", [rpc], [rxps], xps[b_:b_ + 64, h * 128:h * 128 + 64], lhsT=pc[b_:b_ + 64, h, 1, :], rhs=pc[b_:b_ + 64, h, 0, :], start=True, stop=True, **kw)
                                I(PE, "matmul", [rpc], [rxps], xps[b_:b_ + 64, h * 128 + 64:h * 128 + 128], lhsT=pc[b_:b_ + 64, h, 0, :], rhs=pc[b_:b_ + 64, h, 1, :], start=True, stop=True, **kw)
                        pcn, rpcn = pc_r.get()
                        if lvl < 4:
                            I(ACT, "copy", [rxps], [rpcn], out=pcn[:].rearrange("p h a x -> p (h a x)"), in_=xps[:, :])
                        else:
                            I(ACT, "copy", [rxps], [rpcn], out=pcn[:, :, 1, :], in_=xps[:, :].rearrange("p (h a x) -> p h a x", h=4, a=2)[:, :, 1, :])
                        pc, rpc = pcn, rpcn
                        yps, ryps = psA.get()
                        for b_, tp in HV:
                            kw = {} if tp is None else {"tile_position": tp}
                            for h in range(4):
                                I(PE, "matmul", [rpc, rsc], [ryps], yps[b_:b_ + 64, h * 64:(h + 1) * 64], lhsT=pc[b_:b_ + 64, h, 1, :], rhs=sc[b_:b_ + 64, h, :], start=True, stop=True, **kw)
                        if lvl == 4:
                            I(DVE, "tensor_tensor", [ryps, rsc], [rstg], out=stg[:, 0:256].rearrange("p (h x) -> p h x", h=4), in0=yps[:, 0:256].rearrange("p (h x) -> p h x", h=4), in1=sc[:], op=ALU.add)
                        else:
                            scn, rscn = sc_r.get()
                            I(DVE, "tensor_tensor", [ryps, rsc], [rscn], out=scn[:], in0=yps[:, 0:256].rearrange("p (h x) -> p h x", h=4), in1=sc[:], op=ALU.add)
                            sc, rsc = scn, rscn
                        yield
                    dst = q["S6Q"][d, 2 * m:2 * m + 2].rearrange("e p c -> (e p) c")
                    if own:
                        P.dma(SP, dst, stg[:], reads=[rstg])
                    else:
                        P.dma(SP, dst[:, 0:256], stg[:, 0:256], reads=[rstg])

                def rec_step(ch):
                    q = ch["q"]; d = ch["d"]; n = ch["order"][ch["pos"]]
                    own = n < ch["nown"]
                    Gt, rG = G[q["i"]]
                    pre, rpre = ch["pre"], ch["rpre"]
                    t0 = n * 64
                    qk, rqk = qk_r.get()
                    kv, rkv = kv_r.get()
                    s6q, rs6q = s6q_r.get()
                    if own:
                        P.dma(SP, qk[:], q["QK"][:, t0:t0 + 64].rearrange("(a p) t -> p a t", p=128), writes=[rqk])
                        P.dma(SP, s6q[:], q["S6Q"][d, n], writes=[rs6q])
                    else:
                        P.dma(SP, qk[:].rearrange("p (h two) t -> p h two t", two=2)[:, :, 0, :],
                              q["QK"][:, t0:t0 + 64].rearrange("(h two p) t -> p h two t", two=2, p=128)[:, :, 0, :], writes=[rqk])
                        P.dma(SP, s6q[:, 0:256], q["S6Q"][d, n, :, 0:256], writes=[rs6q])
                    P.dma(SP, kv[:], q["KV"][t0:t0 + 64, :], writes=[rkv])
                    sc = s6q[:, 0:256].rearrange("p (h x) -> p h x", h=4); rsc = rs6q
                    qm = s6q[:, 256:512].rearrange("p (h x) -> p h x", h=4); rqm = rs6q
                    S_old, rS_old = ch["S"][ch["cur"]], ch["rS"][ch["cur"]]
                    S_new, rS_new = ch["S"][1 - ch["cur"]], ch["rS"][1 - ch["cur"]]
                    Sb, rSb = ch["Sb"], ch["rSb"]
                    ksp, rksp = psR.get()
                    for h in range(4):
                        I(PE, "matmul", [rqk, rSb], [rksp], ksp[0:64, h * 128:(h + 1) * 128], lhsT=qk[:, 2 * h, :], rhs=Sb[:, h, :], start=True, stop=True)
                    if own:
                        qsp, rqsp = psR.get()
                        for h in range(4):
                            I(PE, "matmul", [rqk, rSb], [rqsp], qsp[0:64, h * 128:(h + 1) * 128], lhsT=qk[:, 2 * h + 1, :], rhs=Sb[:, h, :], start=True, stop=True)
                    yield
                    rr, rrr = rr_r.get()
                    for h in range(4):
                        I(DVE, "scalar_tensor_tensor", [rksp, rpre, rkv], [rrr], out=rr[:, h, :], in0=ksp[0:64, h * 128:(h + 1) * 128], scalar=pre[:, n, 16 + h:17 + h],
                                                                       in1=kv[:, 512 + h * 128:512 + (h + 1) * 128], op0=ALU.mult, op1=ALU.add)
                    yield
                    trp, rtrp = psR.get()
                    for h in range(4):
                        I(PE, "matmul", [rsc, rrr], [rtrp], trp[0:64, h * 128:(h + 1) * 128], lhsT=sc[:, h, :], rhs=rr[:, h, :], start=True, stop=True)
                    yield
                    vn, rvn = vn_r.get()
                    I(ACT, "copy", [rtrp], [rvn], out=vn[:].rearrange("p h x -> p (h x)"), in_=trp[0:64, :])
                    kd, rkd = kd_r.get()
                    I(DVE, "tensor_tensor", [rkv, rpre], [rkd], out=kd[:], in0=kv[:, 0:512].rearrange("p (h x) -> p h x", h=4),
                                                         in1=pre[:, n, 24:28].unsqueeze(2).broadcast_to([64, 4, 128]), op=ALU.mult)
                    yield
                    if own:
                        oa, roa = oa_r.get()
                        for h in range(4):
                            I(ACT, "activation", [rqsp, rpre], [roa], out=oa[:, h, :], in_=qsp[0:64, h * 128:(h + 1) * 128], func=AF.Copy, scale=pre[:, n, 20 + h:21 + h])
                        obp, robp = psR.get()
                        for h in range(4):
                            I(PE, "matmul", [rqm, rvn], [robp], obp[0:64, h * 128:(h + 1) * 128], lhsT=qm[:, h, :], rhs=vn[:, h, :], start=True, stop=True)
                        oo, roo = oo_r.get()
                        I(DVE, "tensor_tensor", [robp, roa], [roo], out=oo[:], in0=obp[0:64, :], in1=oa[:].rearrange("p h x -> p (h x)"), op=ALU.add)
                        P.dma(SP, q["OA"][d, t0:t0 + 64, :], oo[:], reads=[roo])
                    sup, rsup = psR.get()
                    for h in range(4):
                        I(PE, "matmul", [rkd, rvn], [rsup], sup[:, h * 128:(h + 1) * 128], lhsT=kd[:, h, :], rhs=vn[:, h, :], start=True, stop=True)
                    yield
                    egl = ch["egl"]
                    for h in range(4):
                        I(DVE, "scalar_tensor_tensor", [rS_old, ch["regl"], rsup], [rS_new], out=S_new[:, h, :], in0=S_old[:, h, :], scalar=egl[:, n, h:h + 1], in1=sup[:, h * 128:(h + 1) * 128],
                                                                       op0=ALU.mult, op1=ALU.add)
                    I(ACT, "copy", [rS_new], [rSb], out=Sb[:], in_=S_new[:])
                    ch["cur"] = 1 - ch["cur"]
                    ch["pos"] += 1
                    if ch["pos"] == len(ch["order"]) and q["i"] > 0:
                        P.dma(SP, nsd[q["i"] - 1, d].rearrange("(h k) v -> k h v", k=128), S_new[:], reads=[rS_new])

                def lockstep(gens):
                    gens = list(gens)
                    while gens:
                        for g_ in list(gens):
                            try:
                                next(g_)
                            except StopIteration:
                                gens.remove(g_)

                KLOCK = 3
                tasks = []
                ppos = {id(ch): 0 for ch in chains}
                active = list(chains)
                while active:
                    for ch in list(active):
                        tasks.append((ch, ch["porder"][ppos[id(ch)]]))
                        ppos[id(ch)] += 1
                        if ppos[id(ch)] == len(ch["porder"]):
                            active.remove(ch)
                for i in range(0, len(tasks), KLOCK):
                    lockstep([prep_step(ch, n) for ch, n in tasks[i:i + KLOCK]])
                P.flush()
                stB0.close()
                kv_r = ring(st, "kvc", [64, 1024], BF16, 5)
                rr_r = ring(st, "rr", [64, 4, 128], BF16, 4)
                vn_r = ring(st, "vn", [64, 4, 128], BF16, 4)
                kd_r = ring(st, "kd", [64, 4, 128], BF16, 4)
                oa_r = ring(st, "oa", [64, 4, 128], F32, 4)
                oo_r = ring(st, "oo", [64, 512], F32, 4)
                s6q_r = ring(st, "s6q", [64, 512], BF16, 5)
                dmt = sbt(st, "dmt", [128, 2, 512]); xi = sbt(st, "xi", [64, 2, 512]); zeta = sbt(st, "zeta", [128, 8]); gch = sbt(st, "gch", [64, 8]); r_cst2 = Res()
                for d in range(2):
                    P.dma(SP, dmt[:, d, :], c_dmt[d], writes=[r_cst2])
                    P.dma(SP, xi[:, d, :], c_xi[d], writes=[r_cst2])
                P.dma(SP, zeta[:], c_zeta, writes=[r_cst2])
                P.dma(SP, gch[:], c_gch, writes=[r_cst2])
                rq_r = ring(st, "rqc", [64, 8, 128], BF16, 3)
                rk_r = ring(st, "rkc", [128, 256], BF16, 3)
                vb_r = ring(st, "vbc", [128, 512], BF16, 3)
                sm_r = ring(st, "smc", [128, 4, 128], BF16, 2)
                qx_r = ring(st, "qxc", [64, 4, 128], BF16, 2)
                kz_r = ring(st, "kzc", [128, 4, 64], BF16, 2)
                or_r = ring(st, "orc", [128, 512], F32, 2)
                rchains2 = []
                for q in seqs:
                    nch = q["T"] // 128
                    nown = q["own"] // 128
                    for d in range(2):
                        nm = "%s%d" % (q["name"], d)
                        S = [sbt(st, "R%s_%d" % (nm, k), [64, 4, 128]) for k in range(2)]
                        rS = [Res(), Res()]
                        Sb = sbt(st, "Rb" + nm, [64, 4, 128], BF16); rSb = Res()
                        if q["i"] == 0:
                            P.dma(SP, S[0][:], sr0[d].rearrange("(h k) v -> k h v", k=64), writes=[rS[0]])
                        else:
                            I(POOL, "memset", [], [rS[0]], S[0][:], 0.0)
                        I(ACT, "copy", [rS[0]], [rSb], out=Sb[:], in_=S[0][:])
                        order = list(range(nown)) if d == 0 else list(range(nch - 1, -1, -1))
                        rchains2.append(dict(q=q, d=d, S=S, rS=rS, Sb=Sb, rSb=rSb, order=order, pos=0, cur=0, nown=nown))

                def ret_step(ch):
                    q = ch["q"]; d = ch["d"]; n = ch["order"][ch["pos"]]
                    own = n < ch["nown"]
                    t0 = n * 128
                    rq, rrq = rq_r.get(); rk, rrk = rk_r.get(); vb, rvb = vb_r.get()
                    if own:
                        P.dma(SP, rq[:], q["RQK"][:, t0:t0 + 128].rearrange("(a p) t -> p a t", p=64), writes=[rrq])
                    P.dma(SP, rk[:], q["RK"][t0:t0 + 128, :], writes=[rrk])
                    P.dma(SP, vb[:], q["TM"][t0:t0 + 128, 512:1024], writes=[rvb])
                    S_old, rS_old = ch["S"][ch["cur"]], ch["rS"][ch["cur"]]
                    S_new, rS_new = ch["S"][1 - ch["cur"]], ch["rS"][1 - ch["cur"]]
                    Sb, rSb = ch["Sb"], ch["rSb"]
                    if own:
                        scp, rscp = psA.get()
                        for h in range(4):
                            I(PE, "matmul", [rrq], [rscp], scp[:, h * 128:(h + 1) * 128], lhsT=rq[:, 4 + h, :], rhs=rq[:, h, :], start=True, stop=True)
                        sm, rsm = sm_r.get()
                        I(DVE, "tensor_tensor", [rscp, r_cst2], [rsm], out=sm[:].rearrange("p h x -> p (h x)"), in0=scp[:], in1=dmt[:, d, :], op=ALU.mult)
                        qx, rqx = qx_r.get()
                        I(DVE, "tensor_tensor", [rrq, r_cst2], [rqx], out=qx[:].rearrange("p h x -> p (h x)"), in0=rq[:, 0:4, :].rearrange("p h x -> p (h x)"), in1=xi[:, d, :], op=ALU.mult)
                        orp, rorp = psA.get()
                        for h in range(4):
                            I(PE, "matmul", [rsm, rvb], [rorp], orp[:, h * 128:(h + 1) * 128], lhsT=sm[:, h, :], rhs=vb[:, h * 128:(h + 1) * 128], start=True, stop=False)
                            I(PE, "matmul", [rqx, rSb], [rorp], orp[:, h * 128:(h + 1) * 128], lhsT=qx[:, h, :], rhs=Sb[:, h, :], start=False, stop=True)
                        oc, roc = or_r.get()
                        I(ACT, "copy", [rorp], [roc], out=oc[:], in_=orp[:])
                        P.dma(SP, q["OR"][d, t0:t0 + 128, :], oc[:], reads=[roc])
                    kz, rkz = kz_r.get()
                    I(DVE, "tensor_tensor", [rrk, r_cst2], [rkz], out=kz[:], in0=rk[:].rearrange("p (h x) -> p h x", h=4), in1=zeta[:, d * 4:d * 4 + 4].unsqueeze(2).broadcast_to([128, 4, 64]), op=ALU.mult)
                    dsp, rdsp = psA.get()
                    for h in range(4):
                        I(PE, "matmul", [rkz, rvb], [rdsp], dsp[0:64, h * 128:(h + 1) * 128], lhsT=kz[:, h, :], rhs=vb[:, h * 128:(h + 1) * 128], start=True, stop=True)
                    for h in range(4):
                        I(DVE, "scalar_tensor_tensor", [rS_old, r_cst2, rdsp], [rS_new], out=S_new[:, h, :], in0=S_old[:, h, :], scalar=gch[:, d * 4 + h:d * 4 + h + 1], in1=dsp[0:64, h * 128:(h + 1) * 128],
                                                                       op0=ALU.mult, op1=ALU.add)
                    I(ACT, "copy", [rS_new], [rSb], out=Sb[:], in_=S_new[:])
                    ch["cur"] = 1 - ch["cur"]
                    ch["pos"] += 1
                    if ch["pos"] == len(ch["order"]) and q["i"] > 0:
                        P.dma(SP, nsr[q["i"] - 1, d].rearrange("(h k) v -> k h v", k=64), S_new[:], reads=[rS_new])


                rchains = sorted(chains, key=lambda c: -len(c["order"]))
                rchains2 = sorted(rchains2, key=lambda c: -len(c["order"]))
                rnd = 0
                while True:
                    act = [ch for ch in rchains if ch["pos"] < len(ch["order"])][:KLOCK]
                    act2 = [ch for ch in rchains2 if ch["pos"] < len(ch["order"])][:2]
                    if not act and not act2:
                        break
                    if act:
                        lockstep([rec_step(ch) for ch in act])
                    if act2 and (rnd % 2 == 1 or not act):
                        for ch in act2:
                            ret_step(ch)
                    rnd += 1
                P.flush()
            stG.close()

            w1_sb = sbt(st2, "w1_sb", [128, 8, DFF], BF16); r_w1 = [Res() for _ in range(8)]
            for kc in range(8):
                P.dma(POOL, w1_sb[:, kc, :], w_ff1[kc * 128:(kc + 1) * 128, :], writes=[r_w1[kc]])

            def ln_stats(xt, rx, mv, rmv, stt, rst, eps):
                for c2 in range(2):
                    I(DVE, "bn_stats", [rx], [rst], out=stt[:, c2, :], in_=xt[:, c2 * 512:(c2 + 1) * 512])
                I(DVE, "bn_aggr", [rst], [rmv], out=mv[:, 0:2], in_=stt[:])
                I(ACT, "activation", [rmv], [rmv], out=mv[:, 2:3], in_=mv[:, 1:2], func=AF.Ln, bias=eps)
                I(ACT, "activation", [rmv], [rmv], out=mv[:, 2:3], in_=mv[:, 2:3], func=AF.Exp, scale=-0.5)

            def bcast_row(stack, name, src, n):
                t = sbt(stack, name, [128, n]); r = Res()
                P.dma(SP, t[:], src.partition_broadcast(128), writes=[r])
                return t, r

            with ExitStack() as st:
                psY = ring(st, "psE_", [128, 1024], F32, 2, psum=True)
                psT = ring(st, "psET_", [128, 1024], BF16, 2, psum=True)
                wo_sb = sbt(st, "wo_sb", [128, 8, D], BF16); r_wo = [Res() for _ in range(8)]
                for kc in range(8):
                    P.dma(POOL, wo_sb[:, kc, :], w_o[kc * 128:(kc + 1) * 128, :], writes=[r_wo[kc]])
                l1w, r_l1w = bcast_row(st, "l1w", ln1_w, D)
                l1b, r_l1b = bcast_row(st, "l1b", ln1_b, D)
                naw, r_naw = bcast_row(st, "naw", norm_a_w, 128)
                gnw, r_gnw = bcast_row(st, "gnw", gn_w, 512)
                gnb, r_gnb = bcast_row(st, "gnb", gn_b, 512)
                o0_r = ring(st, "o0", [128, 512], F32, 3)
                o1_r = ring(st, "o1", [128, 512], F32, 3)
                jk_r = ring(st, "jk", [128, 512], F32, 2)
                tm_r = ring(st, "tmc", [128, 1536], BF16, 2)
                ss_r = ring(st, "ss", [128, 8], F32, 2)
                bs_r = ring(st, "bs", [128, 4, 6], F32, 2)
                bm_r = ring(st, "bm", [128, 4, 3], F32, 2)
                mix_r = ring(st, "mix", [128, D], BF16, 2)
                mT_r = ring(st, "mT", [128, 8, 128], BF16, 2)
                xt_r = ring(st, "xtc", [128, D], F32, 2)
                tt_r = ring(st, "ttc", [128, D], F32, 2)
                st_r = ring(st, "bstc", [128, 2, 6], F32, 2)
                mv_r = ring(st, "bmvc", [128, 4], F32, 2)
                for q in seqs:
                    cnd = q["cond"]
                    for ts in range(0, q["own"], 128):
                        tm, rtm = tm_r.get()
                        P.dma(ACT, tm[:], q["TM"][ts:ts + 128, :], writes=[rtm])
                        mix, rmix = mix_r.get()
                        o0, ro0 = o0_r.get(); o1, ro1 = o1_r.get()
                        P.dma(SP, o0[:], q["OA"][0, ts:ts + 128, :], writes=[ro0])
                        P.dma(SP, o1[:], q["OA"][1, ts:ts + 128, :], writes=[ro1])
                        I(POOL, "tensor_tensor", [ro0, ro1], [ro0], out=o0[:], in0=o0[:], in1=o1[:], op=ALU.add)
                        ss, rss = ss_r.get(); jk, rjk = jk_r.get()
                        I(POOL, "memset", [], [rss], ss[:], 0.0)
                        for h in range(4):
                            I(ACT, "activation", [ro0], [rjk, rss], out=jk[:, h * 128:(h + 1) * 128], in_=o0[:, h * 128:(h + 1) * 128], func=AF.Square, accum_out=ss[:, h:h + 1])
                        I(ACT, "activation", [rss], [rss], out=ss[:, 4:8], in_=ss[:, 0:4], func=AF.Ln, scale=1.0 / 128.0, bias=1e-6)
                        I(ACT, "activation", [rss], [rss], out=ss[:, 4:8], in_=ss[:, 4:8], func=AF.Exp, scale=-0.5)
                        I(DVE, "tensor_tensor", [ro0, rss], [ro0], out=o0[:].rearrange("p (h x) -> p h x", h=4), in0=o0[:].rearrange("p (h x) -> p h x", h=4),
                                                                       in1=ss[:, 4:8].unsqueeze(2).broadcast_to([128, 4, 128]), op=ALU.mult)
                        I(POOL, "tensor_tensor", [ro0, r_naw], [ro0], out=o0[:].rearrange("p (h x) -> p h x", h=4), in0=o0[:].rearrange("p (h x) -> p h x", h=4),
                                                                  in1=naw[:].unsqueeze(1).broadcast_to([128, 4, 128]), op=ALU.mult)
                        I(DVE, "tensor_tensor", [ro0, rtm], [rmix], out=mix[:, 0:512], in0=o0[:], in1=tm[:, 0:512], op=ALU.mult)
                        p0, rp0 = o0_r.get(); p1, rp1 = o1_r.get()
                        P.dma(SP, p0[:], q["OR"][0, ts:ts + 128, :], writes=[rp0])
                        P.dma(SP, p1[:], q["OR"][1, ts:ts + 128, :], writes=[rp1])
                        I(POOL, "tensor_tensor", [rp0, rp1], [rp0], out=p0[:], in0=p0[:], in1=p1[:], op=ALU.add)
                        bs, rbs = bs_r.get(); bm, rbm = bm_r.get()
                        for h in range(4):
                            I(DVE, "bn_stats", [rp0], [rbs], out=bs[:, h, :], in_=p0[:, h * 128:(h + 1) * 128])
                        for h in range(4):
                            I(DVE, "bn_aggr", [rbs], [rbm], out=bm[:, h, 0:2], in_=bs[:, h, :])
                        I(ACT, "activation", [rbm], [rbm], out=bm[:, :, 2], in_=bm[:, :, 1], func=AF.Ln, bias=1e-5)
                        I(ACT, "activation", [rbm], [rbm], out=bm[:, :, 2], in_=bm[:, :, 2], func=AF.Exp, scale=-0.5)
                        for h in range(4):
                            I(DVE, "tensor_scalar", [rp0, rbm], [rp0], out=p0[:, h * 128:(h + 1) * 128], in0=p0[:, h * 128:(h + 1) * 128], scalar1=bm[:, h, 0:1], scalar2=bm[:, h, 2:3],
                                                                                  op0=ALU.subtract, op1=ALU.mult)
                        I(POOL, "tensor_tensor", [rp0, r_gnw], [rp0], out=p0[:], in0=p0[:], in1=gnw[:], op=ALU.mult)
                        I(POOL, "tensor_tensor", [rp0, r_gnb], [rp0], out=p0[:], in0=p0[:], in1=gnb[:], op=ALU.add)
                        I(DVE, "tensor_tensor", [rp0, rtm], [rmix], out=mix[:, 512:1024], in0=p0[:], in1=tm[:, 1024:1536], op=ALU.mult)
                        pt, rpt = psT.get()
                        for kc in range(8):
                            I(PE, "transpose", [rmix, r_identb], [rpt], pt[:, kc * 128:(kc + 1) * 128], mix[:, kc * 128:(kc + 1) * 128], identb[:])
                        mT, rmT = mT_r.get()
                        I(ACT, "copy", [rpt], [rmT], out=mT[:].rearrange("p a t -> p (a t)"), in_=pt[:])
                        py, rpy = psY.get()
                        for cg in range(2):
                            for kc in range(8):
                                I(PE, "matmul", [rmT, r_wo[kc]], [rpy], py[:, cg * 512:(cg + 1) * 512], lhsT=mT[:, kc, :], rhs=wo_sb[:, kc, cg * 512:(cg + 1) * 512],
                                                                                  start=(kc == 0), stop=(kc == 7))
                        xt, rx = xt_r.get()
                        P.dma(ACT, xt[:], q["x"][ts:ts + 128, :], writes=[rx])
                        tt, rtt = tt_r.get()
                        I(DVE, "tensor_tensor", [rpy, r_gates], [rtt], out=tt[:], in0=py[:], in1=gates[:, 0, cnd, :], op=ALU.mult)
                        I(DVE, "scalar_tensor_tensor", [rx, rtt], [rtt], out=tt[:], in0=xt[:], scalar=ALPHA, in1=tt[:], op0=ALU.mult, op1=ALU.add)
                        stt, rst = st_r.get(); mv, rmv = mv_r.get()
                        ln_stats(tt, rtt, mv, rmv, stt, rst, 1e-6)
                        I(DVE, "tensor_scalar", [rtt, rmv], [rtt], out=tt[:], in0=tt[:], scalar1=mv[:, 0:1], scalar2=mv[:, 2:3], op0=ALU.subtract, op1=ALU.mult)
                        I(POOL, "tensor_tensor", [rtt, r_l1w], [rtt], out=tt[:], in0=tt[:], in1=l1w[:], op=ALU.mult)
                        I(DVE, "tensor_tensor", [rtt, r_l1b], [rtt], out=tt[:], in0=tt[:], in1=l1b[:], op=ALU.add)
                        P.dma(SP, q["X1"][ts:ts + 128, :], tt[:], reads=[rtt])
                P.flush()

            with ExitStack() as st:
                w2_sb = sbt(st, "w2_sb", [128, 32, D], BF16); r_w2 = [Res() for _ in range(32)]
                for kc in range(32):
                    P.dma(POOL, w2_sb[:, kc, :], w_ff2[kc * 128:(kc + 1) * 128, :], writes=[r_w2[kc]])
                psA = ring(st, "psF_", [128, 512], F32, 3, psum=True)
                psY = ring(st, "psFY_", [128, 1024], F32, 2, psum=True)
                psT = ring(st, "psFT_", [128, 1024], BF16, 1, psum=True)
                l2w, r_l2w = bcast_row(st, "l2w", ln2_w, D)
                l2b, r_l2b = bcast_row(st, "l2b", ln2_b, D)
                bf2, r_bf2 = bcast_row(st, "bf2", b_ff2, D)
                b1r = sbt(st, "b1r", [32, 128]); b1c = sbt(st, "b1c", [128, 32]); r_b1 = Res()
                P.dma(SP, b1r[:], b_ff1.rearrange("(a p) -> a p", p=128), writes=[r_b1])
                ps, rps = psA.get()
                I(PE, "matmul", [r_b1, r_identf], [rps], ps[:, 0:32], lhsT=b1r[:], rhs=identf[0:32, 0:32], start=True, stop=True)
                I(DVE, "tensor_copy", [rps], [r_b1], out=b1c[:], in_=ps[:, 0:32])
                x1_r = ring(st, "x1", [128, D], F32, 4)
                xn_r = ring(st, "xnf", [128, D], BF16, 2)
                h2_r = ring(st, "h2T", [128, 8, 256], BF16, 2)
                aT_r = ring(st, "aT", [128, 8, 256], BF16, 2)
                rl_r = ring(st, "rl", [128, 256], F32, 3)
                tt_r = ring(st, "ttf", [128, D], F32, 2)
                st_r = ring(st, "bstf", [128, 2, 6], F32, 2)
                mv_r = ring(st, "bmvf", [128, 4], F32, 2)
                for q in seqs:
                    cnd = q["cond"]
                    TT = 256
                    for t0 in range(0, q["own"], TT):
                        h2, rh2 = h2_r.get()
                        x1s = []
                        for j in range(2):
                            ts = t0 + j * 128
                            x1, rx1 = x1_r.get()
                            x1s.append((x1, rx1))
                            P.dma(SP, x1[:], q["X1"][ts:ts + 128, :], writes=[rx1])
                            stt, rst = st_r.get(); mv, rmv = mv_r.get()
                            ln_stats(x1, rx1, mv, rmv, stt, rst, 1e-6)
                            xn, rxn = xn_r.get()
                            I(DVE, "tensor_scalar", [rx1, rmv], [rxn], out=xn[:], in0=x1[:], scalar1=mv[:, 0:1], scalar2=mv[:, 2:3], op0=ALU.subtract, op1=ALU.mult)
                            pt, rpt = psT.get()
                            for kc in range(8):
                                I(PE, "transpose", [rxn, r_identb], [rpt], pt[:, kc * 128:(kc + 1) * 128], xn[:, kc * 128:(kc + 1) * 128], identb[:])
                            for kc in range(8):
                                I(ACT, "activation", [rpt, r_modc], [rh2], out=h2[:, kc, j * 128:(j + 1) * 128], in_=pt[:, kc * 128:(kc + 1) * 128], func=AF.Identity,
                                                                                                scale=modc[:, 4, kc, cnd:cnd + 1], bias=modc[:, 3, kc, cnd:cnd + 1])
                        pys = [psY.get() for _ in range(2)]
                        for g in range(4):
                            aT, raT = aT_r.get()
                            for f in range(8):
                                ft = g * 8 + f
                                ps, rps = psA.get()
                                for kc in range(8):
                                    I(PE, "matmul", [r_w1[kc], rh2], [rps], ps[:, 0:TT], lhsT=w1_sb[:, kc, ft * 128:(ft + 1) * 128], rhs=h2[:, kc, :], start=(kc == 0), stop=(kc == 7))
                                rl, rrl = rl_r.get()
                                I(ACT, "activation", [rps, r_b1], [rrl], out=rl[:], in_=ps[:, 0:TT], func=AF.Relu, bias=b1c[:, ft:ft + 1])
                                I(POOL if ft % 2 else DVE, "tensor_tensor", [rrl], [raT], out=aT[:, f, :], in0=rl[:], in1=rl[:], op=ALU.mult)
                            for j in range(2):
                                py, rpy = pys[j]
                                for cg in range(2):
                                    for f in range(8):
                                        ft = g * 8 + f
                                        I(PE, "matmul", [raT, r_w2[ft]], [rpy], py[:, cg * 512:(cg + 1) * 512], lhsT=aT[:, f, j * 128:(j + 1) * 128], rhs=w2_sb[:, ft, cg * 512:(cg + 1) * 512],
                                          start=(ft == 0), stop=(ft == 31))
                        for j in range(2):
                            ts = t0 + j * 128
                            x1, rx1 = x1s[j]
                            py, rpy = pys[j]
                            tt, rtt = tt_r.get()
                            I(DVE, "tensor_tensor", [rpy, r_bf2], [rtt], out=tt[:], in0=py[:], in1=bf2[:], op=ALU.add)
                            I(POOL, "tensor_tensor", [rtt, r_gates], [rtt], out=tt[:], in0=tt[:], in1=gates[:, 1, cnd, :], op=ALU.mult)
                            I(DVE, "scalar_tensor_tensor", [rx1, rtt], [rtt], out=tt[:], in0=x1[:], scalar=ALPHA, in1=tt[:], op0=ALU.mult, op1=ALU.add)
                            stt, rst = st_r.get(); mv, rmv = mv_r.get()
                            ln_stats(tt, rtt, mv, rmv, stt, rst, 1e-6)
                            I(DVE, "tensor_scalar", [rtt, rmv], [rtt], out=tt[:], in0=tt[:], scalar1=mv[:, 0:1], scalar2=mv[:, 2:3], op0=ALU.subtract, op1=ALU.mult)
                            I(POOL, "tensor_tensor", [rtt, r_l2w], [rtt], out=tt[:], in0=tt[:], in1=l2w[:], op=ALU.mult)
                            I(DVE, "tensor_tensor", [rtt, r_l2b], [rx1], out=x1[:], in0=tt[:], in1=l2b[:], op=ALU.add)
                            P.dma(SP, q["y"][ts:ts + 128, :], x1[:], reads=[rx1])
                P.flush()
    return nc


def _consts(odd):
    k = np.arange(64)
    ut = np.zeros((2, 64, 64), np.float32)
    ut[0] = (k[:, None] <= k[None, :])
    ut[1] = (k[:, None] >= k[None, :])
    ut8 = np.broadcast_to(ut[:, :, None, None, :], (2, 64, 4, 2, 64)).reshape(2, 64, 512)
    neg = np.zeros((2, 64, 2, 64), np.float32)
    j = k[:, None]; i = k[None, :]
    neg[0, :, 0] = np.where(i > j, 0.0, NEGBIG); neg[0, :, 1] = np.where(i >= j, 0.0, NEGBIG)
    neg[1, :, 0] = np.where(i < j, 0.0, NEGBIG); neg[1, :, 1] = np.where(i <= j, 0.0, NEGBIG)
    neg8 = np.broadcast_to(neg[:, :, None, :, :], (2, 64, 4, 2, 64)).reshape(2, 64, 512)
    lg = np.log1p(-np.exp2(-5.0 - np.arange(4, dtype=np.float64)))
    C = 128
    p = np.arange(C, dtype=np.float64)
    dmt = np.zeros((2, C, 4, C)); xi = np.zeros((2, 4, C)); zeta = np.zeros((C, 8)); gch = np.zeros((8,))
    for d in range(2):
        td = d ^ odd
        lgd = lg if td == 0 else lg[::-1]
        for h in range(4):
            g = lgd[h]
            jj = p[:, None]; ii = p[None, :]
            if d == 0:
                dmt[d, :, h, :] = np.where(ii >= jj, np.exp(g * np.maximum(ii - jj, 0)), 0.0)
                xi[d, h] = np.exp(g * (p + 1)); zeta[:, d * 4 + h] = np.exp(g * (C - 1 - p))
            else:
                dmt[d, :, h, :] = np.where(ii <= jj, np.exp(g * np.maximum(jj - ii, 0)), 0.0)
                xi[d, h] = np.exp(g * (C - p)); zeta[:, d * 4 + h] = np.exp(g * p)
            gch[d * 4 + h] = np.exp(g * C)
    dmt *= 0.125; xi *= 0.125
    xi64 = np.broadcast_to(xi.reshape(2, 1, 512), (2, 64, 512))
    gch64 = np.broadcast_to(gch[None, :], (64, 8))
    r = np.repeat(np.arange(64, dtype=np.float32), 64); col = np.tile(np.arange(64, dtype=np.float32), 64)
    inv = (np.float32(10000.0) ** (-np.arange(16, dtype=np.float32) / np.float32(16))).astype(np.float32)
    ang = np.concatenate([r[:, None] * inv, col[:, None] * inv], -1).astype(np.float32)
    cos, sin = np.cos(ang), np.sin(ang)
    if odd:
        cos, sin = cos[::-1], sin[::-1]
    f = lambda a: np.ascontiguousarray(a, dtype=np.float32)
    return dict(c_ut8=f(ut8), c_neg=f(neg8), c_dmt=f(dmt.reshape(2, C, 512)), c_xi=f(xi64), c_zeta=f(zeta), c_gch=f(gch64),
                rope_cos=f(cos), rope_sin=f(sin))


_NC_CACHE = {}


def kernel(x_prompt, x_sample, c, state_delta, state_ret, c_ctx, w_mod, b_mod, w_in, conv_w, a_log, dt_bias,
           norm_a_w, gn_w, gn_b, w_o, ln1_w, ln1_b, w_ff1, b_ff1, w_ff2, b_ff2, ln2_w, ln2_b):
    f = lambda a: np.ascontiguousarray(np.asarray(a), dtype=np.float32)
    x_prompt, x_sample, c, state_delta, state_ret, c_ctx = map(f, (x_prompt, x_sample, c, state_delta, state_ret, c_ctx))
    w_in0 = f(w_in)[0]
    perm = np.arange(DIN)
    perm[2048:2052], perm[2052:2056] = np.arange(2052, 2056), np.arange(2048, 2052)
    perm[2056:2060], perm[2060:2064] = np.arange(2060, 2064), np.arange(2056, 2060)
    common = dict(w_mod=f(w_mod)[0], b_mod=f(b_mod)[0], norm_a_w=f(norm_a_w)[0], gn_w=f(gn_w)[0], gn_b=f(gn_b)[0], w_o=f(w_o)[0],
                  ln1_w=f(ln1_w)[0], ln1_b=f(ln1_b)[0], w_ff1=f(w_ff1)[0], b_ff1=f(b_ff1)[0], w_ff2=f(w_ff2)[0], b_ff2=f(b_ff2)[0],
                  ln2_w=f(ln2_w)[0], ln2_b=f(ln2_b)[0])
    per_par = []
    for odd in range(2):
        dd = dict(common)
        dd.update(_consts(odd))
        dd["w_in"] = f(w_in0[:, perm]) if odd else w_in0
        dd["conv_w"] = f(f(conv_w)[0][::-1]) if odd else f(conv_w)[0]
        dd["a_log"] = f(f(a_log)[0][::-1] if odd else f(a_log)[0]).reshape(8)
        dd["dt_bias"] = f(f(dt_bias)[0][::-1] if odd else f(dt_bias)[0]).reshape(8)
        per_par.append(dd)
    in_maps = []
    for core in range(8):
        s, odd = core // 2, core % 2
        m = dict(per_par[odd])
        xs_ = x_sample[s]
        xp_ = x_prompt[2 * core:2 * core + 2]
        sd = state_delta[s, 0]
        sr = state_ret[s, 0]
        if odd:
            xs_ = xs_[::-1]; xp_ = xp_[:, ::-1]; sd = sd[::-1]; sr = sr[::-1]
        m["xs"] = f(xs_)
        m["xp"] = f(xp_).reshape(2 * TP, D)
        m["cond"] = f(np.stack([c[s], c_ctx]))
        m["sd0"] = f(sd).reshape(2, 512, 128)
        m["sr0"] = f(sr).reshape(2, 256, 128)
        in_maps.append(m)
    if "nc" not in _NC_CACHE:
        _NC_CACHE["nc"] = build_program()
    res = run_bass_kernel_spmd(_NC_CACHE["nc"], in_maps, core_ids=list(range(8)))
    y_prompt = np.zeros((16, TP, D), np.float32)
    y_sample = np.zeros((4, TS, D), np.float32)
    new_sd = np.zeros((16, 1, 2, 4, 128, 128), np.float32)
    new_sr = np.zeros((16, 1, 2, 4, 64, 128), np.float32)
    for core in range(8):
        s, odd = core // 2, core % 2
        r = res.results[core]
        ys_ = np.asarray(r["ys"], dtype=np.float32)
        yp_ = np.asarray(r["yp"], dtype=np.float32).reshape(2, TP, D)
        sd_ = np.asarray(r["nsd"], dtype=np.float32).reshape(2, 2, 4, 128, 128)
        sr_ = np.asarray(r["nsr"], dtype=np.float32).reshape(2, 2, 4, 64, 128)
        if odd:
            y_sample[s, OWN:] = ys_[::-1]
            y_prompt[2 * core:2 * core + 2] = yp_[:, ::-1]
            new_sd[2 * core:2 * core + 2, 0] = sd_[:, ::-1]
            new_sr[2 * core:2 * core + 2, 0] = sr_[:, ::-1]
        else:
            y_sample[s, :OWN] = ys_
            y_prompt[2 * core:2 * core + 2] = yp_
            new_sd[2 * core:2 * core + 2, 0] = sd_
            new_sr[2 * core:2 * core + 2, 0] = sr_
    return (y_prompt, y_sample, new_sd, new_sr)
```

```python
import numpy as np
from contextlib import ExitStack
import concourse.bass as bass
import concourse.mybir as mybir
from concourse.bass_utils import run_bass_kernel_spmd

F32 = mybir.dt.float32
BF16 = mybir.dt.bfloat16
AF = mybir.ActivationFunctionType
ALU = mybir.AluOpType

PE, ACT, DVE, POOL, SP = "tensor", "scalar", "vector", "gpsimd", "sync"

D = 1024
TS = 4096
OWN = 2048
TP = 256
DIN = 3600
DFF = 4096
ALPHA = 2.0 ** 0.25
NEGBIG = -30000.0
NOREORDER = set()


class Res:
    __slots__ = ("w", "rs")

    def __init__(self):
        self.w = None
        self.rs = []


class Op:
    __slots__ = ("eng", "fn", "deps", "dma_sem", "token", "signal", "epoch", "cost", "lat", "is_dma", "tag")


class Prog:
    NDMA = 12

    def __init__(self, nc, stack):
        self.nc = nc
        self.ops = []
        self.epoch = 0
        self.esem = {}
        self.ecnt = {}
        for e in (PE, ACT, DVE, POOL):
            self.esem[e] = stack.enter_context(nc.semaphore("s_" + e))
            self.ecnt[e] = 0
        self.dsem = {}
        self.dcnt = {}
        self.dlast = {}
        self.drr = {}
        for q in (SP, ACT, POOL):
            self.dsem[q] = [stack.enter_context(nc.semaphore("d_%s%d" % (q, i))) for i in range(self.NDMA)]
            self.dcnt[q] = [0] * self.NDMA
            self.dlast[q] = [None] * self.NDMA
            self.drr[q] = 0
        self.waited = {e: {} for e in (PE, ACT, DVE, POOL, SP)}
        self.n_inst = 0
        self.reorder = True
        self.sim_total = 0.0

    def op(self, eng, fn, reads=(), writes=(), cost=300.0, lat=None):
        op = Op()
        op.eng = eng
        op.fn = fn
        op.deps = []
        op.dma_sem = None
        op.token = None
        op.signal = False
        op.epoch = self.epoch
        op.cost = cost
        op.lat = cost if lat is None else lat
        op.is_dma = False
        op.tag = ""
        deps = op.deps
        for r in reads:
            if r.w is not None:
                deps.append(r.w)
        for r in writes:
            if r.w is not None:
                deps.append(r.w)
            deps.extend(r.rs)
        for r in reads:
            r.rs.append(op)
        for r in writes:
            r.w = op
            r.rs = []
        self.ops.append(op)
        return op

    def dma(self, q, out, in_, reads=(), writes=(), **kw):
        def fn(e):
            return e.dma_start(out=out, in_=in_, **kw)
        nbytes = 1
        for d_ in out.shape:
            nbytes *= d_
        nbytes *= 2 if out.dtype == BF16 else 4
        op = self.op(q, fn, reads, writes, cost=(400.0 if q == POOL else 80.0), lat=2000.0 + nbytes / 150.0)
        op.tag = "dma:%s<-%s" % (out.name, in_.name)
        op.is_dma = True
        op.signal = True
        return op

    def schedule(self, ops):
        import heapq
        n = len(ops)
        idx = {id(o): i for i, o in enumerate(ops)}
        succs = [[] for _ in range(n)]
        npred = [0] * n
        for i, o in enumerate(ops):
            ps = set()
            for d in o.deps:
                if d.epoch == self.epoch:
                    j = idx[id(d)]
                    if j != i:
                        ps.add(j)
            npred[i] = len(ps)
            for j in ps:
                succs[j].append(i)
        ready_t = [0.0] * n
        engs = (PE, ACT, DVE, POOL, SP)
        pend = {e: [] for e in engs}
        avail = {e: [] for e in engs}
        free_t = {e: 0.0 for e in engs}
        for i in range(n):
            if npred[i] == 0:
                heapq.heappush(avail[ops[i].eng], i)
        order = []
        done = 0
        crit = [-1] * n
        rdy_from = [-1] * n
        st_t = [0.0] * n
        last_on = {e: -1 for e in engs}
        while done < n:
            best = None
            for e in engs:
                pe_, av = pend[e], avail[e]
                while pe_ and pe_[0][0] <= free_t[e]:
                    heapq.heappush(av, heapq.heappop(pe_)[1])
                if av:
                    cand = (free_t[e], 0, av[0], e)
                elif pe_:
                    cand = (pe_[0][0], 1, pe_[0][1], e)
                else:
                    continue
                if best is None or cand < best:
                    best = cand
            start, kind, i, e = best
            if kind == 0:
                heapq.heappop(avail[e])
            else:
                heapq.heappop(pend[e])
            o = ops[i]
            start = max(start, ready_t[i], free_t[e])
            if ready_t[i] >= free_t[e]:
                crit[i] = rdy_from[i]
            else:
                crit[i] = last_on[e]
            last_on[e] = i
            st_t[i] = start
            free_t[e] = start + o.cost
            fin = start + o.lat + 120.0
            order.append(i)
            done += 1
            for j in succs[i]:
                f_ = (start + o.cost) if (e == PE and ops[j].eng == PE) else fin
                if f_ > ready_t[j]:
                    ready_t[j] = f_
                    rdy_from[j] = i
                npred[j] -= 1
                if npred[j] == 0:
                    heapq.heappush(pend[ops[j].eng], (ready_t[j], j))
        self.sim_time = max(free_t.values())
        if getattr(self, "debug_crit", False) and order:
            import collections
            i = order[-1]
            agg = collections.Counter(); cnt_ = collections.Counter()
            prev_t = st_t[i] + ops[i].cost
            while i >= 0:
                key = ops[i].eng[:3] + ":" + ops[i].tag
                agg[key] += prev_t - st_t[i]
                cnt_[key] += 1
                prev_t = st_t[i]
                i = crit[i]
            for k_, v_ in agg.most_common(25):
                print("[crit] %-60s %8.1f us  n=%d" % (k_, v_ / 1e3, cnt_[k_]))
        tot = {e: 0.0 for e in engs}
        cnt = {e: 0 for e in engs}
        for o in ops:
            tot[o.eng] += o.cost
            cnt[o.eng] += 1
        print("[prog]   busy us: " + " ".join("%s=%.0f(%d)" % (e, tot[e] / 1e3, cnt[e]) for e in engs))
        return [ops[i] for i in order]

    def flush(self):
        nc = self.nc
        ops = self.schedule(self.ops) if (self.reorder and self.epoch not in NOREORDER) else self.ops
        ep = self.epoch
        for op in ops:
            if op.is_dma:
                q = op.eng
                i = self.drr[q]
                self.drr[q] = (i + 1) % self.NDMA
                prev = self.dlast[q][i]
                if prev is not None:
                    op.deps.append(prev)
                self.dcnt[q][i] += 16
                op.dma_sem = self.dsem[q][i]
                op.token = (op.dma_sem, self.dcnt[q][i])
                self.dlast[q][i] = op
        for op in ops:
            nd = []
            for d in op.deps:
                if d.epoch != ep:
                    continue
                if d.eng == PE and op.eng == PE and d.dma_sem is None and op.dma_sem is None:
                    continue
                nd.append(d)
                if d.dma_sem is None:
                    d.signal = True
            op.deps = nd
        for op in ops:
            if op.dma_sem is None and op.signal:
                self.ecnt[op.eng] += 1
                op.token = (self.esem[op.eng], self.ecnt[op.eng])
        by_eng = {e: [] for e in (PE, ACT, DVE, POOL, SP)}
        for op in ops:
            by_eng[op.eng].append(op)
        self.n_inst += len(ops)

        def emit(ename):
            lst = by_eng[ename]
            waited = self.waited[ename]

            def body(e):
                for op in lst:
                    for d in op.deps:
                        sem, val = d.token
                        k = id(sem)
                        if waited.get(k, 0) < val:
                            e.wait_ge(sem, val)
                            waited[k] = val
                    inst = op.fn(e)
                    if op.signal:
                        if op.dma_sem is not None:
                            inst.then_inc(op.dma_sem, 16)
                        else:
                            inst.then_inc(self.esem[ename], 1)
                if ename in self.dsem:
                    for i, s in enumerate(self.dsem[ename]):
                        v = self.dcnt[ename][i]
                        if v > 0 and waited.get(id(s), 0) < v:
                            e.wait_ge(s, v)
                            waited[id(s)] = v
            return body

        with nc.Block() as blk:
            for ename in (SP, POOL, ACT, DVE, PE):
                if by_eng[ename] or ename in self.dsem:
                    getattr(blk, ename)(emit(ename))
        self.sim_total += getattr(self, "sim_time", 0.0)
        print("[prog] block %d: %d ops, sim %.0f us" % (self.epoch, len(ops), getattr(self, "sim_time", 0.0) / 1e3))
        self.ops = []
        self.epoch += 1


class Ring:
    def __init__(self, items):
        self.items = items
        self.i = 0

    def get(self):
        t = self.items[self.i % len(self.items)]
        self.i += 1
        return t


def build_program(debug=False):
    nc = bass.Bass("TRN2", target_bir_lowering=False)

    def din(name, shape, dt=F32):
        return nc.dram_tensor(name, list(shape), dt, kind="ExternalInput").ap()

    def dout(name, shape, dt=F32):
        return nc.dram_tensor(name, list(shape), dt, kind="ExternalOutput").ap()

    def dscr(name, shape, dt):
        return nc.dram_tensor(name, list(shape), dt, kind="Internal").ap()

    xs = din("xs", [TS, D])
    xp = din("xp", [2 * TP, D])
    cond = din("cond", [2, D])
    sd0 = din("sd0", [2, 4 * 128, 128])
    sr0 = din("sr0", [2, 4 * 64, 128])
    w_mod = din("w_mod", [D, 6 * D])
    b_mod = din("b_mod", [6 * D])
    w_in = din("w_in", [D, DIN])
    conv_w = din("conv_w", [5, 1536])
    a_log = din("a_log", [8])
    dt_bias = din("dt_bias", [8])
    norm_a_w = din("norm_a_w", [128])
    gn_w = din("gn_w", [512])
    gn_b = din("gn_b", [512])
    w_o = din("w_o", [D, D])
    ln1_w = din("ln1_w", [D])
    ln1_b = din("ln1_b", [D])
    w_ff1 = din("w_ff1", [D, DFF])
    b_ff1 = din("b_ff1", [DFF])
    w_ff2 = din("w_ff2", [DFF, D])
    b_ff2 = din("b_ff2", [D])
    ln2_w = din("ln2_w", [D])
    ln2_b = din("ln2_b", [D])
    rope_cos = din("rope_cos", [TS, 32])
    rope_sin = din("rope_sin", [TS, 32])
    c_ut8 = din("c_ut8", [2, 64, 512])
    c_neg = din("c_neg", [2, 64, 512])
    c_dmt = din("c_dmt", [2, 128, 512])
    c_xi = din("c_xi", [2, 64, 512])
    c_zeta = din("c_zeta", [128, 8])
    c_gch = din("c_gch", [64, 8])

    ys = dout("ys", [OWN, D])
    yp = dout("yp", [2 * TP, D])
    nsd = dout("nsd", [2, 2, 4 * 128, 128])
    nsr = dout("nsr", [2, 2, 4 * 64, 128])

    seqs = []
    for si, (nm, T, own) in enumerate((("s", TS, OWN), ("p0", TP, TP), ("p1", TP, TP))):
        q = dict(i=si, name=nm, T=T, own=own, rope=(si == 0), cond=(0 if si == 0 else 1))
        q["x"] = xs if si == 0 else xp[(si - 1) * TP:si * TP, :]
        q["y"] = ys if si == 0 else yp[(si - 1) * TP:si * TP, :]
        q["PQ"] = dscr("PQ" + nm, [1536, T], BF16)
        q["QK"] = dscr("QK" + nm, [1024, T], BF16)
        q["KV"] = dscr("KV" + nm, [T, 1024], BF16)
        q["TM"] = dscr("TM" + nm, [T, 1536], BF16)
        q["RQK"] = dscr("RQK" + nm, [512, T], BF16)
        q["RK"] = dscr("RK" + nm, [T, 256], BF16)
        q["OA"] = dscr("OA" + nm, [2, own, 512], F32)
        q["OR"] = dscr("OR" + nm, [2, own, 512], F32)
        q["X1"] = dscr("X1" + nm, [own, D], F32)
        q["S6Q"] = dscr("S6Q" + nm, [2, T // 64, 64, 512], BF16)
        q["nch"] = T // 64
        seqs.append(q)

    with ExitStack() as st0:
        P = Prog(nc, st0)

        def I(eng, meth, reads, writes, *a, **k):
            o_ = k.get("out", a[0] if a else None)
            fr = 1
            for d_ in o_.shape[1:]:
                fr *= d_
            if eng == PE:
                l_ = k.get("lhsT", a[1] if len(a) > 1 else None)
                c_ = (max(64, fr) + 8) / 2.4 * (4.0 if (l_ is not None and l_.dtype == F32) else 1.0)
                lat = c_ + 150.0
            elif eng == ACT:
                c_ = (224 + fr) / 1.2
                lat = c_
            elif eng == DVE:
                c_ = (150 + fr) / 0.96
                lat = c_
            else:
                c_ = (150 + 2 * fr) / 1.2
                lat = c_
            op_ = P.op(eng, lambda e: getattr(e, meth)(*a, **k), reads, writes, cost=c_, lat=lat)
            op_.tag = "%s:%s" % (meth, o_.name)

        def sbt(stack, name, shape, dt=F32):
            return stack.enter_context(nc.sbuf_tensor(name, list(shape), dt))

        def pst(stack, name, shape, dt=F32):
            return stack.enter_context(nc.psum_tensor(name, list(shape), dt))

        def ring(stack, name, shape, dt, n, psum=False):
            items = []
            for i in range(n):
                t = (pst if psum else sbt)(stack, "%s%d" % (name, i), shape, dt)
                items.append((t, Res()))
            return Ring(items)

        identf = sbt(st0, "identf", [128, 128]); r_identf = Res()
        identb = sbt(st0, "identb", [128, 128], BF16); r_identb = Res()
        onesb = sbt(st0, "onesb", [128, 128], BF16); r_onesb = Res()
        onesq = sbt(st0, "onesq", [128, 128], BF16); r_onesq = Res()
        onesf = sbt(st0, "onesf", [64, 128]); r_onesf = Res()
        modc = sbt(st0, "modc", [128, 6, 8, 2]); r_modc = Res()
        gates = sbt(st0, "gates", [128, 2, 2, D]); r_gates = Res()
        dtb = sbt(st0, "dtb", [8, 1]); negA = sbt(st0, "negA", [8, 1]); r_gc = Res()
        stG = ExitStack()
        G = []
        for q in seqs:
            G.append((sbt(stG, "G" + q["name"], [64, q["nch"], 24]), Res()))
        G128 = []
        for q in seqs:
            G128.append((sbt(stG, "GG" + q["name"], [128, q["nch"] // 2, 24]), Res()))

        I(POOL, "memset", [], [r_identf], identf[:], 0.0)
        I(POOL, "affine_select", [r_identf], [r_identf], out=identf[:], in_=identf[:], pattern=[[-1, 128]],
                                             compare_op=ALU.not_equal, fill=1.0, base=0, channel_multiplier=1)
        I(DVE, "tensor_copy", [r_identf], [r_identb], out=identb[:], in_=identf[:])
        I(POOL, "memset", [], [r_onesb], onesb[:], 1.0)
        I(POOL, "memset", [], [r_onesq], onesq[:], 128.0)
        I(POOL, "memset", [], [r_onesf], onesf[:], 1.0)
        P.dma(SP, dtb[:], dt_bias.rearrange("(p o) -> p o", o=1), writes=[r_gc])
        P.dma(SP, negA[:], a_log.rearrange("(p o) -> p o", o=1), writes=[r_gc])
        I(ACT, "activation", [r_gc], [r_gc], out=negA[:], in_=negA[:], func=AF.Exp)
        I(DVE, "tensor_scalar", [r_gc], [r_gc], out=negA[:], in0=negA[:], scalar1=-1.0, scalar2=None, op0=ALU.mult)

        stW = ExitStack()
        w_in_sb = sbt(stW, "w_in_sb", [128, 8, DIN], BF16); r_win = [Res() for _ in range(8)]
        for kc in range(8):
            P.dma(POOL, w_in_sb[:, kc, :], w_in[kc * 128:(kc + 1) * 128, :], writes=[r_win[kc]])
        with ExitStack() as st:
            psA = ring(st, "ps0_", [128, 512], F32, 4, psum=True)
            crow = sbt(st, "crow", [16, 128]); r_crow = Res()
            scT = sbt(st, "scT", [128, 16]); r_scT = Res()
            brow = sbt(st, "brow", [48, 128]); r_brow = Res()
            bcol = sbt(st, "bcol", [128, 48]); r_bcol = Res()
            wm = ring(st, "wm", [128, 8, D], F32, 2)
            wm_res = {}
            gbt = ring(st, "gbt", [128, 128], F32, 2)
            P.dma(SP, crow[:], cond.rearrange("c (k p) -> (c k) p", p=128), writes=[r_crow])
            P.dma(SP, brow[:], b_mod.rearrange("(a p) -> a p", p=128), writes=[r_brow])
            ps, rps = psA.get()
            I(PE, "matmul", [r_crow, r_identf], [rps], ps[:, 0:16], lhsT=crow[:], rhs=identf[0:16, 0:16], start=True, stop=True)
            I(ACT, "activation", [rps], [r_scT], out=scT[:], in_=ps[:, 0:16], func=AF.Exp, scale=-1.0)
            I(DVE, "tensor_scalar", [r_scT], [r_scT], out=scT[:], in0=scT[:], scalar1=1.0, scalar2=None, op0=ALU.add)
            I(DVE, "reciprocal", [r_scT], [r_scT], out=scT[:], in_=scT[:])
            I(DVE, "tensor_tensor", [r_scT, rps], [r_scT], out=scT[:], in0=scT[:], in1=ps[:, 0:16], op=ALU.mult)
            ps, rps = psA.get()
            I(PE, "matmul", [r_brow, r_identf], [rps], ps[:, 0:48], lhsT=brow[:], rhs=identf[0:48, 0:48], start=True, stop=True)
            I(DVE, "tensor_copy", [rps], [r_bcol], out=bcol[:], in_=ps[:, 0:48])
            scv = scT[:].rearrange("p (c k) -> p c k", c=2)
            for blk in range(6):
                wt, rw0 = wm.get()
                if id(rw0) not in wm_res:
                    wm_res[id(rw0)] = [Res() for _ in range(8)]
                rw = wm_res[id(rw0)]
                for kc in range(8):
                    P.dma(SP if kc % 2 == 0 else ACT, wt[:, kc, :], w_mod[kc * 128:(kc + 1) * 128, blk * D:(blk + 1) * D], writes=[rw[kc]])
                ps, rps = psA.get()
                for ft in range(8):
                    for kc in range(8):
                        I(PE, "matmul", [rw[kc], r_scT], [rps], ps[:, ft * 2:ft * 2 + 2], lhsT=wt[:, kc, ft * 128:(ft + 1) * 128], rhs=scv[:, :, kc],
                            start=(kc == 0), stop=(kc == 7))
                I(DVE, "tensor_tensor", [rps, r_bcol], [r_modc], out=modc[:, blk, :, :], in0=ps[:, 0:16].rearrange("p (f c) -> p f c", c=2),
                    in1=bcol[:, blk * 8:(blk + 1) * 8].unsqueeze(2).broadcast_to([128, 8, 2]), op=ALU.add)
                if blk in (1, 4):
                    I(DVE, "tensor_scalar", [r_modc], [r_modc], out=modc[:, blk, :, :], in0=modc[:, blk, :, :], scalar1=1.0, scalar2=None, op0=ALU.add)
            for wi, blk in enumerate((2, 5)):
                for c in range(2):
                    for half in range(2):
                        ps, rps = psA.get()
                        for f4 in range(4):
                            ft = half * 4 + f4
                            gb, rgb = gbt.get()
                            I(DVE, "tensor_copy", [r_modc], [rgb], out=gb[:], in_=modc[:, blk, ft, c:c + 1].broadcast_to([128, 128]))
                            I(PE, "matmul", [rgb, r_identf], [rps], ps[:, f4 * 128:(f4 + 1) * 128], lhsT=gb[:], rhs=identf[:], start=True, stop=True)
                        I(ACT, "copy", [rps], [r_gates], out=gates[:, wi, c, half * 512:(half + 1) * 512], in_=ps[:])
            P.flush()

        with ExitStack() as st:
            cosT = sbt(st, "cosT", [128, TS // 128, 32]); sinT = sbt(st, "sinT", [128, TS // 128, 32]); r_rope = Res()
            P.dma(SP, cosT[:], rope_cos.rearrange("(n p) f -> p n f", p=128), writes=[r_rope])
            P.dma(SP, sinT[:], rope_sin.rearrange("(n p) f -> p n f", p=128), writes=[r_rope])
            psA = ring(st, "psA_", [128, 512], F32, 6, psum=True)
            psT = ring(st, "psT_", [128, 1024], BF16, 2, psum=True)
            xt_r = ring(st, "xt", [128, D], F32, 3)
            xn_r = ring(st, "xn", [128, D], BF16, 2)
            st_r = ring(st, "bst", [128, 2, 6], F32, 2)
            mv_r = ring(st, "bmv", [128, 4], F32, 2)
            hT_r = ring(st, "hT", [128, 8, 512], BF16, 2)
            pq_r = ring(st, "pq", [128, 512], BF16, 4)
            tm_r = ring(st, "tm", [128, 1536], BF16, 2)
            qkf_r = ring(st, "qkf", [128, 512], F32, 2)
            rt_r = ring(st, "rt", [128, 4, 256], F32, 1)
            rqk_r = ring(st, "rqk", [128, 512], BF16, 3)
            rqT_r = ring(st, "rqT", [128, 4, 128], BF16, 3)
            gt_r = ring(st, "gt", [8, 5, 512], F32, 1)

            def layernorm_stats(xt, rx, mv, rmv, stt, rst, eps):
                for c2 in range(2):
                    I(DVE, "bn_stats", [rx], [rst], out=stt[:, c2, :], in_=xt[:, c2 * 512:(c2 + 1) * 512])
                I(DVE, "bn_aggr", [rst], [rmv], out=mv[:, 0:2], in_=stt[:])
                I(ACT, "activation", [rmv], [rmv], out=mv[:, 2:3], in_=mv[:, 1:2], func=AF.Ln, bias=eps)
                I(ACT, "activation", [rmv], [rmv], out=mv[:, 2:3], in_=mv[:, 2:3], func=AF.Exp, scale=-0.5)

            cwr = sbt(st, "cwr", [60, 128]); r_cwr = Res()
            cwc = sbt(st, "cwc", [128, 60]); r_cwc = Res()
            Dg = sbt(st, "Dg", [128, 60, 128], BF16); r_Dg = Res()
            P.dma(SP, cwr[:], conv_w.rearrange("k (c p) -> (k c) p", p=128), writes=[r_cwr])
            ps, rps = psA.get()
            I(PE, "matmul", [r_cwr, r_identf], [rps], ps[:, 0:60], lhsT=cwr[:], rhs=identf[0:60, 0:60], start=True, stop=True)
            I(DVE, "tensor_copy", [rps], [r_cwc], out=cwc[:], in_=ps[:, 0:60])
            for i in range(60):
                I(DVE, "tensor_scalar", [r_identf, r_cwc], [r_Dg], out=Dg[:, i, :], in0=identf[:], scalar1=cwc[:, i:i + 1], scalar2=None, op0=ALU.mult)
            pw_r = ring(st, "pw", [128, 516], BF16, 3)
            cs_r = ring(st, "cs", [128, 512], F32, 2)
            sq_r = ring(st, "sq", [128, 512], BF16, 3)
            rs_r = ring(st, "rs", [128, 512], F32, 3)
            xh_r = ring(st, "xh", [128, 512], BF16, 3)
            kt_r = ring(st, "kt", [128, 4, 512], BF16, 2)
            def phaseA_tile(q, t0):
                T = q["T"]
                TT = min(512, T)
                nsub = TT // 128
                Gt, rG = G[q["i"]]
                cnd = q["cond"]
                foreign = t0 >= q["own"]
                hT, rhT = hT_r.get()
                for j in range(nsub):
                    ts = t0 + j * 128
                    xt, rx = xt_r.get()
                    P.dma(SP, xt[:], q["x"][ts:ts + 128, :], writes=[rx])
                    stt, rst = st_r.get(); mv, rmv = mv_r.get()
                    layernorm_stats(xt, rx, mv, rmv, stt, rst, 1e-6)
                    xn, rxn = xn_r.get()
                    I(DVE, "tensor_scalar", [rx, rmv], [rxn], out=xn[:], in0=xt[:], scalar1=mv[:, 0:1], scalar2=mv[:, 2:3],
                                                                           op0=ALU.subtract, op1=ALU.mult)
                    pt, rpt = psT.get()
                    for kc in range(8):
                        I(PE, "transpose", [rxn, r_identb], [rpt], pt[:, kc * 128:(kc + 1) * 128], xn[:, kc * 128:(kc + 1) * 128], identb[:])
                    for kc in range(8):
                        I(ACT, "activation", [rpt, r_modc], [rhT], out=hT[:, kc, j * 128:(j + 1) * 128], in_=pt[:, kc * 128:(kc + 1) * 128], func=AF.Identity,
                            scale=modc[:, 1, kc, cnd:cnd + 1], bias=modc[:, 0, kc, cnd:cnd + 1])
                for ct in range(12):
                    if foreign and ct < 4 and t0 != q["own"]:
                        continue
                    ps, rps = psA.get()
                    for kc in range(8):
                        I(PE, "matmul", [r_win[kc], rhT], [rps], ps[:, 0:TT], lhsT=w_in_sb[:, kc, ct * 128:(ct + 1) * 128], rhs=hT[:, kc, 0:TT],
                                                                            start=(kc == 0), stop=(kc == 7))
                    pq, rpq = pq_r.get()
                    if ct % 2 == 0:
                        I(ACT, "copy", [rps], [rpq], out=pq[:, 0:TT], in_=ps[:, 0:TT])
                    else:
                        I(DVE, "tensor_copy", [rps], [rpq], out=pq[:, 0:TT], in_=ps[:, 0:TT])
                    rPQ[(q["i"], t0, ct)] = Res()
                    P.dma(SP, q["PQ"][ct * 128:(ct + 1) * 128, t0:t0 + TT], pq[:, 0:TT], reads=[rpq], writes=[rPQ[(q["i"], t0, ct)]])
                psa_, rpa = psA.get()
                psb_, rpb = psA.get()
                for which, (pp, rp) in enumerate(((psa_, rpa), (psb_, rpb))):
                    c0 = 2048 + which * 8
                    for kc in range(8):
                        I(PE, "matmul", [r_win[kc], rhT], [rp], pp[0:8, 0:TT], lhsT=w_in_sb[:, kc, c0:c0 + 8], rhs=hT[:, kc, 0:TT], start=(kc == 0), stop=(kc == 7))
                pa = psa_[0:8, 0:TT]; pb = psb_[0:8, 0:TT]
                gt, rgt = gt_r.get()
                I(ACT, "activation", [rpa, r_gc], [rgt], out=gt[:, 0, 0:TT], in_=pa, func=AF.Exp, bias=dtb[:, 0:1])
                I(ACT, "activation", [rgt], [rgt], out=gt[:, 0, 0:TT], in_=gt[:, 0, 0:TT], func=AF.Ln, bias=1.0)
                I(DVE, "tensor_scalar", [rgt, r_gc], [rgt], out=gt[:, 0, 0:TT], in0=gt[:, 0, 0:TT], scalar1=negA[:, 0:1], scalar2=None, op0=ALU.mult)
                I(ACT, "activation", [rpb], [rgt], out=gt[:, 3, 0:TT], in_=pb, func=AF.Exp, scale=-1.0)
                I(ACT, "activation", [rgt], [rgt], out=gt[:, 2, 0:TT], in_=gt[:, 3, 0:TT], func=AF.Ln, bias=1.0)
                I(DVE, "tensor_scalar", [rgt], [rgt], out=gt[:, 3, 0:TT], in0=gt[:, 3, 0:TT], scalar1=1.0, scalar2=None, op0=ALU.add)
                I(DVE, "reciprocal", [rgt], [rgt], out=gt[:, 1, 0:TT], in_=gt[:, 3, 0:TT])
                ps, rps = psA.get()
                nc64 = TT // 64
                for c64 in range(nc64):
                    for a in range(3):
                        I(PE, "matmul", [rgt, r_identf], [rps], ps[0:64, c64 * 24 + a * 8:c64 * 24 + a * 8 + 8],
                                                                             lhsT=gt[:, a, c64 * 64:(c64 + 1) * 64], rhs=identf[0:8, 0:8], start=True, stop=True)
                I(DVE, "tensor_copy", [rps], [rG], out=Gt[:, t0 // 64:t0 // 64 + nc64, :], in_=ps[0:64, 0:nc64 * 24].rearrange("p (c a) -> p c a", a=24))
                ps, rps = psA.get()
                Gt2, rG2 = G128[q["i"]]
                for c128 in range(nsub):
                    for a in range(3):
                        I(PE, "matmul", [rgt, r_identf], [rps], ps[:, c128 * 24 + a * 8:c128 * 24 + a * 8 + 8],
                          lhsT=gt[:, a, c128 * 128:(c128 + 1) * 128], rhs=identf[0:8, 0:8], start=True, stop=True)
                I(DVE, "tensor_copy", [rps], [rG2], out=Gt2[:, t0 // 128:t0 // 128 + nsub, :], in_=ps[:, 0:nsub * 24].rearrange("p (c a) -> p c a", a=24))
                for j in range(nsub):
                    ts = t0 + j * 128
                    tm, rtm = tm_r.get()
                    for gi, c0 in enumerate((1536, 2064, 2576, 3088)):
                        if foreign and gi in (0, 3):
                            continue
                        ps, rps = psA.get()
                        for kc in range(8):
                            I(PE, "matmul", [r_win[kc], rhT], [rps], ps[:], lhsT=hT[:, kc, j * 128:(j + 1) * 128], rhs=w_in_sb[:, kc, c0:c0 + 512],
                                                                                  start=(kc == 0), stop=(kc == 7))
                        if gi == 0:
                            I(ACT, "activation", [rps], [rtm], out=tm[:, 0:512], in_=ps[:], func=AF.Silu)
                        elif gi == 2:
                            I(DVE, "tensor_copy", [rps], [rtm], out=tm[:, 512:1024], in_=ps[:])
                        elif gi == 3:
                            I(ACT, "activation", [rps], [rtm], out=tm[:, 1024:1536], in_=ps[:], func=AF.Silu)
                        else:
                            rqk, rrqk = rqk_r.get()
                            if q["rope"]:
                                qkf, rqkf = qkf_r.get()
                                I(ACT, "copy", [rps], [rqkf], out=qkf[:], in_=ps[:])
                                rt, rrt = rt_r.get()
                                xv = qkf[:].rearrange("p (a h f) -> p a h f", a=8, h=2)
                                ov = rqk[:].rearrange("p (a h f) -> p a h f", a=8, h=2)
                                nt = ts // 128
                                cb = cosT[:, nt, :].unsqueeze(1).broadcast_to([128, 8, 32])
                                sb_ = sinT[:, nt, :].unsqueeze(1).broadcast_to([128, 8, 32])
                                rv = [rt[:, k, :].rearrange("p (a f) -> p a f", a=8) for k in range(4)]
                                I(DVE, "tensor_tensor", [rqkf, r_rope], [rrt], out=rv[0], in0=xv[:, :, 0, :], in1=cb, op=ALU.mult)
                                I(DVE, "tensor_tensor", [rqkf, r_rope], [rrt], out=rv[1], in0=xv[:, :, 1, :], in1=sb_, op=ALU.mult)
                                I(DVE, "tensor_tensor", [rqkf, r_rope], [rrt], out=rv[2], in0=xv[:, :, 0, :], in1=sb_, op=ALU.mult)
                                I(DVE, "tensor_tensor", [rqkf, r_rope], [rrt], out=rv[3], in0=xv[:, :, 1, :], in1=cb, op=ALU.mult)
                                I(DVE, "tensor_tensor", [rrt], [rrqk], out=ov[:, :, 0, :], in0=rv[0], in1=rv[1], op=ALU.subtract)
                                I(DVE, "tensor_tensor", [rrt], [rrqk], out=ov[:, :, 1, :], in0=rv[2], in1=rv[3], op=ALU.add)
                            else:
                                I(ACT, "copy", [rps], [rrqk], out=rqk[:], in_=ps[:])
                            P.dma(SP, q["RK"][ts:ts + 128, :], rqk[:, 256:512], reads=[rrqk])
                            pt, rpt = psT.get()
                            for b4 in range(4):
                                I(PE, "transpose", [rrqk, r_identb], [rpt], pt[:, b4 * 128:(b4 + 1) * 128], rqk[:, b4 * 128:(b4 + 1) * 128], identb[:])
                            rqT, rrqT = rqT_r.get()
                            I(DVE, "tensor_copy", [rpt], [rrqT], out=rqT[:].rearrange("p a t -> p (a t)"), in_=pt[:, 0:512])
                            P.dma(SP, q["RQK"][:, ts:ts + 128].rearrange("(a p) t -> p a t", p=128), rqT[:], reads=[rrqT])
                    if foreign:
                        P.dma(SP, q["TM"][ts:ts + 128, 512:1024], tm[:, 512:1024], reads=[rtm])
                    else:
                        P.dma(SP, q["TM"][ts:ts + 128, :], tm[:], reads=[rtm])

            def phaseA2_tile(q, t0):
                T = q["T"]
                TT = min(512, T)
                nsub = TT // 128
                ktw = {}
                for ct in range(12):
                    if t0 >= q["own"] and ct < 4:
                        continue
                    kind = ct // 4
                    h = ct % 4
                    pw, rpw = pw_r.get()
                    lo = max(t0 - 2, 0); hi = min(t0 + TT + 2, T)
                    if lo > t0 - 2:
                        I(DVE, "memset", [], [rpw], pw[:, 0:2], 0.0)
                    if hi < t0 + TT + 2:
                        I(DVE, "memset", [], [rpw], pw[:, TT + 2:TT + 4], 0.0)
                    rdeps = [rPQ[(q["i"], t_, ct)] for t_ in (t0 - TT, t0, t0 + TT) if (q["i"], t_, ct) in rPQ]
                    P.dma(SP, pw[:, lo - t0 + 2:hi - t0 + 2], q["PQ"][ct * 128:(ct + 1) * 128, lo:hi], reads=rdeps, writes=[rpw])
                    ps, rps = psA.get()
                    for k in range(5):
                        I(PE, "matmul", [r_Dg, rpw], [rps], ps[:, 0:TT], lhsT=Dg[:, k * 12 + ct, :], rhs=pw[:, k:k + TT], start=(k == 0), stop=(k == 4))
                    xh, rxh = xh_r.get()
                    if kind == 2:
                        I(ACT, "activation", [rps], [rxh], out=xh[:, 0:TT], in_=ps[:, 0:TT], func=AF.Silu)
                    else:
                        cs, rcs = cs_r.get()
                        I(ACT, "activation", [rps], [rcs], out=cs[:, 0:TT], in_=ps[:, 0:TT], func=AF.Silu)
                        sq, rsq = sq_r.get()
                        I(DVE, "tensor_tensor", [rcs], [rsq], out=sq[:, 0:TT], in0=cs[:, 0:TT], in1=cs[:, 0:TT], op=ALU.mult)
                        ps2, rps2 = psA.get()
                        ones_ = onesq if kind == 0 else onesb
                        r_ones = r_onesq if kind == 0 else r_onesb
                        I(PE, "matmul", [rsq, r_ones], [rps2], ps2[:, 0:TT], lhsT=ones_[:], rhs=sq[:, 0:TT], start=True, stop=True)
                        rs, rrs = rs_r.get()
                        eps = 1e-6 * (128.0 if kind == 0 else 1.0)
                        I(ACT, "activation", [rps2], [rrs], out=rs[:, 0:TT], in_=ps2[:, 0:TT], func=AF.Ln, bias=eps)
                        I(ACT, "activation", [rrs], [rrs], out=rs[:, 0:TT], in_=rs[:, 0:TT], func=AF.Exp, scale=-0.5)
                        I(DVE, "tensor_tensor", [rcs, rrs], [rxh], out=xh[:, 0:TT], in0=cs[:, 0:TT], in1=rs[:, 0:TT], op=ALU.mult)
                        rb = h * 2 + (1 if kind == 0 else 0)
                        P.dma(SP, q["QK"][rb * 128:(rb + 1) * 128, t0:t0 + TT], xh[:, 0:TT], reads=[rxh])
                    if kind >= 1:
                        pt, rpt = psT.get()
                        for j in range(nsub):
                            I(PE, "transpose", [rxh, r_identb], [rpt], pt[:, j * 128:(j + 1) * 128], xh[:, j * 128:(j + 1) * 128], identb[:])
                        if h == 0:
                            ktw[kind] = kt_r.get()
                        kt, rkt = ktw[kind]
                        src = pt[:, 0:nsub * 128].rearrange("p (a t) -> p a t", t=128)
                        if kind == 1:
                            I(DVE, "tensor_copy", [rpt], [rkt], out=kt[:, 0:nsub, h * 128:(h + 1) * 128], in_=src)
                        else:
                            I(ACT, "copy", [rpt], [rkt], out=kt[:, 0:nsub, h * 128:(h + 1) * 128], in_=src)
                        if h == 3:
                            c0 = (kind - 1) * 512
                            P.dma(SP, q["KV"][t0:t0 + TT, c0:c0 + 512].rearrange("(j p) c -> p j c", p=128), kt[:, 0:nsub, :], reads=[rkt])

            rPQ = {}
            tilesA = []
            for q in seqs:
                TT_ = min(512, q["T"])
                for t0 in range(0, q["T"], TT_):
                    tilesA.append((q, t0, TT_))
            pend = []
            for (q, t0, TT_) in tilesA:
                phaseA_tile(q, t0)
                pend.append((q, t0, TT_))
                for (q2, t2, TT2) in list(pend):
                    nxt = t2 + TT2
                    if nxt >= q2["T"] or (q2["i"], nxt, 11) in rPQ:
                        phaseA2_tile(q2, t2)
                        pend.remove((q2, t2, TT2))
            assert not pend
            P.flush()
        stW.close()

        with ExitStack() as st2:


            with ExitStack() as st:
                psA = ring(st, "psC_", [128, 512], F32, 7, psum=True)
                psR = psA
                psT = ring(st, "psCT_", [128, 1024], BF16, 1, psum=True)
                ut8 = sbt(st, "ut8", [64, 2, 512]); negm = sbt(st, "negm", [64, 2, 512]); r_cst = Res()
                ut8b = sbt(st, "ut8b", [64, 2, 512], BF16); negb = sbt(st, "negb", [64, 2, 512], BF16)
                idb4 = sbt(st, "idb4", [64, 4, 64], BF16)
                for d in range(2):
                    P.dma(SP, ut8[:, d, :], c_ut8[d], writes=[r_cst])
                    P.dma(SP, negm[:, d, :], c_neg[d], writes=[r_cst])
                I(DVE, "tensor_copy", [r_cst], [r_cst], out=ut8b[:], in_=ut8[:])
                I(DVE, "tensor_copy", [r_cst], [r_cst], out=negb[:], in_=negm[:])
                I(DVE, "tensor_copy", [r_identb], [r_cst], out=idb4[:], in_=identb[0:64, 0:64].unsqueeze(1).broadcast_to([64, 4, 64]))
                qk_r = ring(st, "qkc", [128, 8, 64], BF16, 5)
                chains = []
                for q in seqs:
                    Gt, rG = G[q["i"]]
                    nch = q["nch"]
                    for d in range(2):
                        nm = "%s%d" % (q["name"], d)
                        S = [sbt(st, "S%s_%d" % (nm, k), [128, 4, 128]) for k in range(2)]
                        rS = [Res(), Res()]
                        Sb = sbt(st, "Sb" + nm, [128, 4, 128], BF16); rSb = Res()
                        pre = sbt(st, "pre" + nm, [64, nch, 28]); rpre = Res()
                        egl = sbt(st, "egl" + nm, [128, nch, 4]); regl = Res()
                        if q["i"] == 0:
                            P.dma(SP, S[0][:], sd0[d].rearrange("(h k) v -> k h v", k=128), writes=[rS[0]])
                        else:
                            I(POOL, "memset", [], [rS[0]], S[0][:], 0.0)
                        I(ACT, "copy", [rS[0]], [rSb], out=Sb[:], in_=S[0][:])
                        I(DVE, "tensor_copy", [rG], [rpre], out=pre[:, :, 0:4], in_=Gt[:, :, d * 4:d * 4 + 4])
                        nn = nch * 4
                        gsd = sbt(st, "gsd" + nm, [64, nn]); rgsd = Res()
                        I(DVE, "tensor_copy", [rG], [rgsd], out=gsd[:].rearrange("p (c h) -> p c h", h=4), in_=Gt[:, :, d * 4:d * 4 + 4])
                        ps, rps = psA.get()
                        ps2, rps2 = psA.get()
                        I(PE, "matmul", [r_cst, rgsd], [rps], ps[0:64, 0:nn], lhsT=ut8[:, d, 0:64], rhs=gsd[:], start=True, stop=True)
                        I(PE, "matmul", [r_onesf, rgsd], [rps2], ps2[:, 0:nn], lhsT=onesf[:], rhs=gsd[:], start=True, stop=True)
                        gcv = ps[0:64, 0:nn].rearrange("p (c h) -> p c h", h=4)
                        I(DVE, "tensor_copy", [rps], [rpre], out=pre[:, :, 4:8], in_=gcv)
                        b2 = pre[:, :, 8:16].rearrange("p c (h a) -> p c h a", a=2)
                        I(DVE, "tensor_tensor", [rps, rG], [rpre], out=b2[:, :, :, 0], in0=gcv, in1=Gt[:, :, 16 + d * 4:20 + d * 4], op=ALU.add)
                        I(DVE, "tensor_copy", [rps], [rpre], out=b2[:, :, :, 1], in_=gcv)
                        I(ACT, "activation", [rps], [rpre], out=pre[:, :, 20:24], in_=gcv, func=AF.Exp)
                        I(DVE, "tensor_scalar", [rpre], [rpre], out=pre[:, :, 16:20], in0=pre[:, :, 20:24], scalar1=-1.0, scalar2=None, op0=ALU.mult)
                        glv = ps2[0:64, 0:nn].rearrange("p (c h) -> p c h", h=4)
                        I(DVE, "tensor_tensor", [rps2, rpre], [rpre], out=pre[:, :, 24:28], in0=glv, in1=pre[:, :, 4:8], op=ALU.subtract)
                        I(ACT, "activation", [rpre], [rpre], out=pre[:, :, 24:28], in_=pre[:, :, 24:28], func=AF.Exp)
                        I(DVE, "tensor_tensor", [rpre, rG], [rpre], out=pre[:, :, 24:28], in0=pre[:, :, 24:28], in1=Gt[:, :, 8 + d * 4:12 + d * 4], op=ALU.mult)
                        I(ACT, "activation", [rps2], [regl], out=egl[:].rearrange("p c h -> p (c h)"), in_=ps2[:, 0:nn], func=AF.Exp)
                        nown = q["own"] // 64
                        if d == 0:
                            order = list(range(nown))
                        else:
                            order = list(range(nch - 1, -1, -1))
                        npair = nch // 2
                        if d == 0:
                            porder = list(range(nown // 2))
                        else:
                            porder = list(range(npair - 1, -1, -1))
                        chains.append(dict(q=q, d=d, S=S, rS=rS, Sb=Sb, rSb=rSb, pre=pre, rpre=rpre, egl=egl, regl=regl, order=order, pos=0, cur=0, nown=nown,
                                           porder=porder, nm=nm, nch=nch))

                stB0 = ExitStack()
                ut8w = sbt(stB0, "ut8w", [128, 2, 512]); negw = sbt(stB0, "negw", [128, 2, 512]); r_cw = Res()
                ut8wb = sbt(stB0, "ut8wb", [128, 2, 512], BF16); negwb = sbt(stB0, "negwb", [128, 2, 512], BF16)
                idw4 = sbt(stB0, "idw4", [128, 4, 64], BF16)
                for d in range(2):
                    for e_ in range(2):
                        P.dma(SP, ut8w[e_ * 64:(e_ + 1) * 64, d, :], c_ut8[d], writes=[r_cw])
                        P.dma(SP, negw[e_ * 64:(e_ + 1) * 64, d, :], c_neg[d], writes=[r_cw])
                I(DVE, "tensor_copy", [r_cw], [r_cw], out=ut8wb[:], in_=ut8w[:])
                I(DVE, "tensor_copy", [r_cw], [r_cw], out=negwb[:], in_=negw[:])
                I(DVE, "tensor_copy", [r_identb], [r_cw], out=idw4[0:64], in_=identb[0:64, 0:64].unsqueeze(1).broadcast_to([64, 4, 64]))
                I(DVE, "tensor_copy", [r_identb], [r_cw], out=idw4[64:128], in_=identb[64:128, 64:128].unsqueeze(1).broadcast_to([64, 4, 64]))
                for ch in chains:
                    q = ch["q"]; d = ch["d"]; nm = ch["nm"]; nch = ch["nch"]
                    Gt2, rG2 = G128[q["i"]]
                    npair = nch // 2
                    n2 = npair * 4
                    pw_ = sbt(stB0, "prw" + nm, [128, npair, 12]); rpw_ = Res()
                    g2c = sbt(stB0, "g2c" + nm, [128, n2]); rg2c = Res()
                    I(DVE, "tensor_copy", [rG2], [rpw_], out=pw_[:, :, 0:4], in_=Gt2[:, :, d * 4:d * 4 + 4])
                    I(DVE, "tensor_copy", [rG2], [rg2c], out=g2c[:].rearrange("p (c h) -> p c h", h=4), in_=Gt2[:, :, d * 4:d * 4 + 4])
                    ps3, rps3 = psA.get()
                    I(PE, "matmul", [r_cw, rg2c], [rps3], ps3[0:64, 0:n2], lhsT=ut8w[0:64, d, 0:64], rhs=g2c[0:64, :], start=True, stop=True)
                    I(PE, "matmul", [r_cw, rg2c], [rps3], ps3[64:128, 0:n2], lhsT=ut8w[64:128, d, 0:64], rhs=g2c[64:128, :], start=True, stop=True, tile_position=(64, 64))
                    gcw = ps3[:, 0:n2].rearrange("p (c h) -> p c h", h=4)
                    b2w = pw_[:, :, 4:12].rearrange("p c (h a) -> p c h a", a=2)
                    I(DVE, "tensor_tensor", [rps3, rG2], [rpw_], out=b2w[:, :, :, 0], in0=gcw, in1=Gt2[:, :, 16 + d * 4:20 + d * 4], op=ALU.add)
                    I(DVE, "tensor_tensor", [rps3, rG2], [rpw_], out=b2w[:, :, :, 1], in0=gcw, in1=Gt2[:, :, 16 + d * 4:20 + d * 4], op=ALU.add)

                    ch["pw"] = pw_; ch["rpw"] = rpw_
                qkp_r = ring(stB0, "qkp", [128, 8, 128], BF16, 4)
                gu_r = ring(stB0, "gu", [128, 512], BF16, 4)
                dd_r = ring(stB0, "dd", [128, 512], F32, 4)
                ee_r = ring(stB0, "ee", [128, 512], F32, 4)
                pc_r = ring(stB0, "pc", [128, 4, 2, 64], BF16, 8)
                sc_r = ring(stB0, "sc", [128, 4, 64], BF16, 8)
                stg_r = ring(stB0, "stg", [128, 512], BF16, 4)

                def prep_step(ch, m):
                    q = ch["q"]; d = ch["d"]
                    own = (2 * m) < ch["nown"]
                    pw_, rpw_ = ch["pw"], ch["rpw"]
                    t0 = m * 128
                    HV = ((0, None), (64, (64, 64)))
                    qk, rqk = qkp_r.get()
                    if own:
                        P.dma(SP, qk[:], q["QK"][:, t0:t0 + 128].rearrange("(a p) t -> p a t", p=128), writes=[rqk])
                    else:
                        P.dma(SP, qk[:].rearrange("p (h two) t -> p h two t", two=2)[:, :, 0, :],
                              q["QK"][:, t0:t0 + 128].rearrange("(h two p) t -> p h two t", two=2, p=128)[:, :, 0, :], writes=[rqk])
                    stg, rstg = stg_r.get()
                    gu, rgu = gu_r.get()
                    I(DVE, "tensor_tensor", [r_cw, rpw_], [rgu], out=gu[:].rearrange("p (h x) -> p h x", h=4), in0=ut8wb[:, d, :].rearrange("p (h x) -> p h x", h=4),
                      in1=pw_[:, m, 0:4].unsqueeze(2).broadcast_to([128, 4, 128]), op=ALU.mult)
                    dps, rdps = psA.get()
                    for b_, tp in HV:
                        kw = {} if tp is None else {"tile_position": tp}
                        I(PE, "matmul", [r_onesb, rgu], [rdps], dps[b_:b_ + 64, :], lhsT=onesb[b_:b_ + 64, 0:64], rhs=gu[b_:b_ + 64, :], start=True, stop=False, **kw)
                        I(PE, "matmul", [r_identb, r_cw], [rdps], dps[b_:b_ + 64, :], lhsT=identb[b_:b_ + 64, b_:b_ + 64], rhs=negwb[b_:b_ + 64, d, :], start=False, stop=True, **kw)
                    dd, rdd = dd_r.get()
                    I(DVE, "tensor_tensor", [rdps, rpw_], [rdd], out=dd[:].rearrange("p (a x) -> p a x", a=8), in0=dps[:, :].rearrange("p (a x) -> p a x", a=8),
                      in1=pw_[:, m, 4:12].unsqueeze(2).broadcast_to([128, 8, 64]), op=ALU.subtract)
                    ee, ree = ee_r.get()
                    I(ACT, "activation", [rdd], [ree], out=ee[:], in_=dd[:], func=AF.Exp)
                    kps, rkps = psA.get()
                    for e_, (b_, tp) in enumerate(HV):
                        kw = {} if tp is None else {"tile_position": (0, 64)}
                        for h in range(4):
                            if own:
                                I(PE, "matmul", [rqk], [rkps], kps[b_:b_ + 64, h * 128:(h + 1) * 128], lhsT=qk[:, 2 * h, b_:b_ + 64], rhs=qk[:, 2 * h:2 * h + 2, b_:b_ + 64],
                                  start=True, stop=True, **kw)
                            else:
                                I(PE, "matmul", [rqk], [rkps], kps[b_:b_ + 64, h * 128:h * 128 + 64], lhsT=qk[:, 2 * h, b_:b_ + 64], rhs=qk[:, 2 * h, b_:b_ + 64], start=True, stop=True, **kw)
                    pc, rpc = pc_r.get()
                    kv4 = kps[:, :].rearrange("p (h a x) -> p h a x", h=4, a=2)
                    ev4 = ee[:].rearrange("p (h a x) -> p h a x", h=4, a=2)
                    I(DVE, "tensor_tensor", [rkps, ree], [rpc], out=pc[:, :, 0, :], in0=kv4[:, :, 0, :], in1=ev4[:, :, 0, :], op=ALU.mult)
                    if own:
                        I(DVE, "tensor_tensor", [rkps, ree], [rstg], out=stg[:, 256:512].rearrange("p (h x) -> p h x", h=4), in0=kv4[:, :, 1, :], in1=ev4[:, :, 1, :], op=ALU.mult)
                    pt, rpt = psT.get()
                    for b_, tp in HV:
                        kw = {} if tp is None else {"tile_position": tp}
                        for h in range(4):
                            I(PE, "transpose", [rpc, r_identb], [rpt], pt[b_:b_ + 64, h * 64:(h + 1) * 64], pc[b_:b_ + 64, h, 0, :], identb[b_:b_ + 64, b_:b_ + 64], **kw)
                    I(ACT, "copy", [rpt], [rpc], out=pc[:, :, 1, :], in_=pt[:, 0:256].rearrange("p (h x) -> p h x", h=4))
                    sc, rsc = sc_r.get()
                    I(DVE, "tensor_tensor", [rpc, r_cw], [rsc], out=sc[:], in0=idw4[:], in1=pc[:, :, 0, :], op=ALU.subtract)
                    yield
                    for lvl in range(5):
                        xps, rxps = psA.get()
                        for b_, tp in HV:
                            kw = {} if tp is None else {"tile_position": tp}
                            for h in range(4):
                                if lvl < 4:
                                    I(PE, "matmul", [rpc], [rxps], xps[b_:b_ + 64, h * 128:h * 128 + 64], lhsT=pc[b_:b_ + 64, h, 1, :], rhs=pc[b_:b_ + 64, h, 0, :], start=True, stop=True, **kw)
                                I(PE, "matmul", [rpc], [rxps], xps[b_:b_ + 64, h * 128 + 64:h * 128 + 128], lhsT=pc[b_:b_ + 64, h, 0, :], rhs=pc[b_:b_ + 64, h, 1, :], start=True, stop=True, **kw)
                        pcn, rpcn = pc_r.get()
                        if lvl < 4:
                            I(ACT, "copy", [rxps], [rpcn], out=pcn[:].rearrange("p h a x -> p (h a x)"), in_=xps[:, :])
                        else:
                            I(ACT, "copy", [rxps], [rpcn], out=pcn[:, :, 1, :], in_=xps[:, :].rearrange("p (h a x) -> p h a x", h=4, a=2)[:, :, 1, :])
                        pc, rpc = pcn, rpcn
                        yps, ryps = psA.get()
                        for b_, tp in HV:
                            kw = {} if tp is None else {"tile_position": tp}
                            for h in range(4):
                                I(PE, "matmul", [rpc, rsc], [ryps], yps[b_:b_ + 64, h * 64:(h + 1) * 64], lhsT=pc[b_:b_ + 64, h, 1, :], rhs=sc[b_:b_ + 64, h, :], start=True, stop=True, **kw)
                        if lvl == 4:
                            I(DVE, "tensor_tensor", [ryps, rsc], [rstg], out=stg[:, 0:256].rearrange("p (h x) -> p h x", h=4), in0=yps[:, 0:256].rearrange("p (h x) -> p h x", h=4), in1=sc[:], op=ALU.add)
                        else:
                            scn, rscn = sc_r.get()
                            I(DVE, "tensor_tensor", [ryps, rsc], [rscn], out=scn[:], in0=yps[:, 0:256].rearrange("p (h x) -> p h x", h=4), in1=sc[:], op=ALU.add)
                            sc, rsc = scn, rscn
                        yield
                    dst = q["S6Q"][d, 2 * m:2 * m + 2].rearrange("e p c -> (e p) c")
                    if own:
                        P.dma(SP, dst, stg[:], reads=[rstg])
                    else:
                        P.dma(SP, dst[:, 0:256], stg[:, 0:256], reads=[rstg])

                def rec_step(ch):
                    q = ch["q"]; d = ch["d"]; n = ch["order"][ch["pos"]]
                    own = n < ch["nown"]
                    Gt, rG = G[q["i"]]
                    pre, rpre = ch["pre"], ch["rpre"]
                    t0 = n * 64
                    qk, rqk = qk_r.get()
                    kv, rkv = kv_r.get()
                    s6q, rs6q = s6q_r.get()
                    if own:
                        P.dma(SP, qk[:], q["QK"][:, t0:t0 + 64].rearrange("(a p) t -> p a t", p=128), writes=[rqk])
                        P.dma(SP, s6q[:], q["S6Q"][d, n], writes=[rs6q])
                    else:
                        P.dma(SP, qk[:].rearrange("p (h two) t -> p h two t", two=2)[:, :, 0, :],
                              q["QK"][:, t0:t0 + 64].rearrange("(h two p) t -> p h two t", two=2, p=128)[:, :, 0, :], writes=[rqk])
                        P.dma(SP, s6q[:, 0:256], q["S6Q"][d, n, :, 0:256], writes=[rs6q])
                    P.dma(SP, kv[:], q["KV"][t0:t0 + 64, :], writes=[rkv])
                    sc = s6q[:, 0:256].rearrange("p (h x) -> p h x", h=4); rsc = rs6q
                    qm = s6q[:, 256:512].rearrange("p (h x) -> p h x", h=4); rqm = rs6q
                    S_old, rS_old = ch["S"][ch["cur"]], ch["rS"][ch["cur"]]
                    S_new, rS_new = ch["S"][1 - ch["cur"]], ch["rS"][1 - ch["cur"]]
                    Sb, rSb = ch["Sb"], ch["rSb"]
                    ksp, rksp = psR.get()
                    for h in range(4):
                        I(PE, "matmul", [rqk, rSb], [rksp], ksp[0:64, h * 128:(h + 1) * 128], lhsT=qk[:, 2 * h, :], rhs=Sb[:, h, :], start=True, stop=True)
                    if own:
                        qsp, rqsp = psR.get()
                        for h in range(4):
                            I(PE, "matmul", [rqk, rSb], [rqsp], qsp[0:64, h * 128:(h + 1) * 128], lhsT=qk[:, 2 * h + 1, :], rhs=Sb[:, h, :], start=True, stop=True)
                    yield
                    rr, rrr = rr_r.get()
                    for h in range(4):
                        I(DVE, "scalar_tensor_tensor", [rksp, rpre, rkv], [rrr], out=rr[:, h, :], in0=ksp[0:64, h * 128:(h + 1) * 128], scalar=pre[:, n, 16 + h:17 + h],
                                                                       in1=kv[:, 512 + h * 128:512 + (h + 1) * 128], op0=ALU.mult, op1=ALU.add)
                    yield
                    trp, rtrp = psR.get()
                    for h in range(4):
                        I(PE, "matmul", [rsc, rrr], [rtrp], trp[0:64, h * 128:(h + 1) * 128], lhsT=sc[:, h, :], rhs=rr[:, h, :], start=True, stop=True)
                    yield
                    vn, rvn = vn_r.get()
                    I(ACT, "copy", [rtrp], [rvn], out=vn[:].rearrange("p h x -> p (h x)"), in_=trp[0:64, :])
                    kd, rkd = kd_r.get()
                    I(DVE, "tensor_tensor", [rkv, rpre], [rkd], out=kd[:], in0=kv[:, 0:512].rearrange("p (h x) -> p h x", h=4),
                                                         in1=pre[:, n, 24:28].unsqueeze(2).broadcast_to([64, 4, 128]), op=ALU.mult)
                    yield
                    if own:
                        oa, roa = oa_r.get()
                        for h in range(4):
                            I(ACT, "activation", [rqsp, rpre], [roa], out=oa[:, h, :], in_=qsp[0:64, h * 128:(h + 1) * 128], func=AF.Copy, scale=pre[:, n, 20 + h:21 + h])
                        obp, robp = psR.get()
                        for h in range(4):
                            I(PE, "matmul", [rqm, rvn], [robp], obp[0:64, h * 128:(h + 1) * 128], lhsT=qm[:, h, :], rhs=vn[:, h, :], start=True, stop=True)
                        oo, roo = oo_r.get()
                        I(DVE, "tensor_tensor", [robp, roa], [roo], out=oo[:], in0=obp[0:64, :], in1=oa[:].rearrange("p h x -> p (h x)"), op=ALU.add)
                        P.dma(SP, q["OA"][d, t0:t0 + 64, :], oo[:], reads=[roo])
                    sup, rsup = psR.get()
                    for h in range(4):
                        I(PE, "matmul", [rkd, rvn], [rsup], sup[:, h * 128:(h + 1) * 128], lhsT=kd[:, h, :], rhs=vn[:, h, :], start=True, stop=True)
                    yield
                    egl = ch["egl"]
                    for h in range(4):
                        I(DVE, "scalar_tensor_tensor", [rS_old, ch["regl"], rsup], [rS_new], out=S_new[:, h, :], in0=S_old[:, h, :], scalar=egl[:, n, h:h + 1], in1=sup[:, h * 128:(h + 1) * 128],
                                                                       op0=ALU.mult, op1=ALU.add)
                    I(ACT, "copy", [rS_new], [rSb], out=Sb[:], in_=S_new[:])
                    ch["cur"] = 1 - ch["cur"]
                    ch["pos"] += 1
                    if ch["pos"] == len(ch["order"]) and q["i"] > 0:
                        P.dma(SP, nsd[q["i"] - 1, d].rearrange("(h k) v -> k h v", k=128), S_new[:], reads=[rS_new])

                def lockstep(gens):
                    gens = list(gens)
                    while gens:
                        for g_ in list(gens):
                            try:
                                next(g_)
                            except StopIteration:
                                gens.remove(g_)

                KLOCK = 3
                tasks = []
                ppos = {id(ch): 0 for ch in chains}
                active = list(chains)
                while active:
                    for ch in list(active):
                        tasks.append((ch, ch["porder"][ppos[id(ch)]]))
                        ppos[id(ch)] += 1
                        if ppos[id(ch)] == len(ch["porder"]):
                            active.remove(ch)
                for i in range(0, len(tasks), KLOCK):
                    lockstep([prep_step(ch, n) for ch, n in tasks[i:i + KLOCK]])
                P.flush()
                stB0.close()
                kv_r = ring(st, "kvc", [64, 1024], BF16, 5)
                rr_r = ring(st, "rr", [64, 4, 128], BF16, 4)
                vn_r = ring(st, "vn", [64, 4, 128], BF16, 4)
                kd_r = ring(st, "kd", [64, 4, 128], BF16, 4)
                oa_r = ring(st, "oa", [64, 4, 128], F32, 4)
                oo_r = ring(st, "oo", [64, 512], F32, 4)
                s6q_r = ring(st, "s6q", [64, 512], BF16, 5)
                dmt = sbt(st, "dmt", [128, 2, 512]); xi = sbt(st, "xi", [64, 2, 512]); zeta = sbt(st, "zeta", [128, 8]); gch = sbt(st, "gch", [64, 8]); r_cst2 = Res()
                for d in range(2):
                    P.dma(SP, dmt[:, d, :], c_dmt[d], writes=[r_cst2])
                    P.dma(SP, xi[:, d, :], c_xi[d], writes=[r_cst2])
                P.dma(SP, zeta[:], c_zeta, writes=[r_cst2])
                P.dma(SP, gch[:], c_gch, writes=[r_cst2])
                rq_r = ring(st, "rqc", [64, 8, 128], BF16, 3)
                rk_r = ring(st, "rkc", [128, 256], BF16, 3)
                vb_r = ring(st, "vbc", [128, 512], BF16, 3)
                sm_r = ring(st, "smc", [128, 4, 128], BF16, 2)
                qx_r = ring(st, "qxc", [64, 4, 128], BF16, 2)
                kz_r = ring(st, "kzc", [128, 4, 64], BF16, 2)
                or_r = ring(st, "orc", [128, 512], F32, 2)
                rchains2 = []
                for q in seqs:
                    nch = q["T"] // 128
                    nown = q["own"] // 128
                    for d in range(2):
                        nm = "%s%d" % (q["name"], d)
                        S = [sbt(st, "R%s_%d" % (nm, k), [64, 4, 128]) for k in range(2)]
                        rS = [Res(), Res()]
                        Sb = sbt(st, "Rb" + nm, [64, 4, 128], BF16); rSb = Res()
                        if q["i"] == 0:
                            P.dma(SP, S[0][:], sr0[d].rearrange("(h k) v -> k h v", k=64), writes=[rS[0]])
                        else:
                            I(POOL, "memset", [], [rS[0]], S[0][:], 0.0)
                        I(ACT, "copy", [rS[0]], [rSb], out=Sb[:], in_=S[0][:])
                        order = list(range(nown)) if d == 0 else list(range(nch - 1, -1, -1))
                        rchains2.append(dict(q=q, d=d, S=S, rS=rS, Sb=Sb, rSb=rSb, order=order, pos=0, cur=0, nown=nown))

                def ret_step(ch):
                    q = ch["q"]; d = ch["d"]; n = ch["order"][ch["pos"]]
                    own = n < ch["nown"]
                    t0 = n * 128
                    rq, rrq = rq_r.get(); rk, rrk = rk_r.get(); vb, rvb = vb_r.get()
                    if own:
                        P.dma(SP, rq[:], q["RQK"][:, t0:t0 + 128].rearrange("(a p) t -> p a t", p=64), writes=[rrq])
                    P.dma(SP, rk[:], q["RK"][t0:t0 + 128, :], writes=[rrk])
                    P.dma(SP, vb[:], q["TM"][t0:t0 + 128, 512:1024], writes=[rvb])
                    S_old, rS_old = ch["S"][ch["cur"]], ch["rS"][ch["cur"]]
                    S_new, rS_new = ch["S"][1 - ch["cur"]], ch["rS"][1 - ch["cur"]]
                    Sb, rSb = ch["Sb"], ch["rSb"]
                    if own:
                        scp, rscp = psA.get()
                        for h in range(4):
                            I(PE, "matmul", [rrq], [rscp], scp[:, h * 128:(h + 1) * 128], lhsT=rq[:, 4 + h, :], rhs=rq[:, h, :], start=True, stop=True)
                        sm, rsm = sm_r.get()
                        I(DVE, "tensor_tensor", [rscp, r_cst2], [rsm], out=sm[:].rearrange("p h x -> p (h x)"), in0=scp[:], in1=dmt[:, d, :], op=ALU.mult)
                        qx, rqx = qx_r.get()
                        I(DVE, "tensor_tensor", [rrq, r_cst2], [rqx], out=qx[:].rearrange("p h x -> p (h x)"), in0=rq[:, 0:4, :].rearrange("p h x -> p (h x)"), in1=xi[:, d, :], op=ALU.mult)
                        orp, rorp = psA.get()
                        for h in range(4):
                            I(PE, "matmul", [rsm, rvb], [rorp], orp[:, h * 128:(h + 1) * 128], lhsT=sm[:, h, :], rhs=vb[:, h * 128:(h + 1) * 128], start=True, stop=False)
                            I(PE, "matmul", [rqx, rSb], [rorp], orp[:, h * 128:(h + 1) * 128], lhsT=qx[:, h, :], rhs=Sb[:, h, :], start=False, stop=True)
                        oc, roc = or_r.get()
                        I(ACT, "copy", [rorp], [roc], out=oc[:], in_=orp[:])
                        P.dma(SP, q["OR"][d, t0:t0 + 128, :], oc[:], reads=[roc])
                    kz, rkz = kz_r.get()
                    I(DVE, "tensor_tensor", [rrk, r_cst2], [rkz], out=kz[:], in0=rk[:].rearrange("p (h x) -> p h x", h=4), in1=zeta[:, d * 4:d * 4 + 4].unsqueeze(2).broadcast_to([128, 4, 64]), op=ALU.mult)
                    dsp, rdsp = psA.get()
                    for h in range(4):
                        I(PE, "matmul", [rkz, rvb], [rdsp], dsp[0:64, h * 128:(h + 1) * 128], lhsT=kz[:, h, :], rhs=vb[:, h * 128:(h + 1) * 128], start=True, stop=True)
                    for h in range(4):
                        I(DVE, "scalar_tensor_tensor", [rS_old, r_cst2, rdsp], [rS_new], out=S_new[:, h, :], in0=S_old[:, h, :], scalar=gch[:, d * 4 + h:d * 4 + h + 1], in1=dsp[0:64, h * 128:(h + 1) * 128],
                                                                       op0=ALU.mult, op1=ALU.add)
                    I(ACT, "copy", [rS_new], [rSb], out=Sb[:], in_=S_new[:])
                    ch["cur"] = 1 - ch["cur"]
                    ch["pos"] += 1
                    if ch["pos"] == len(ch["order"]) and q["i"] > 0:
                        P.dma(SP, nsr[q["i"] - 1, d].rearrange("(h k) v -> k h v", k=64), S_new[:], reads=[rS_new])


                rchains = sorted(chains, key=lambda c: -len(c["order"]))
                rchains2 = sorted(rchains2, key=lambda c: -len(c["order"]))
                rnd = 0
                while True:
                    act = [ch for ch in rchains if ch["pos"] < len(ch["order"])][:KLOCK]
                    act2 = [ch for ch in rchains2 if ch["pos"] < len(ch["order"])][:2]
                    if not act and not act2:
                        break
                    if act:
                        lockstep([rec_step(ch) for ch in act])
                    if act2 and (rnd % 2 == 1 or not act):
                        for ch in act2:
                            ret_step(ch)
                    rnd += 1
                P.flush()
            stG.close()

            w1_sb = sbt(st2, "w1_sb", [128, 8, DFF], BF16); r_w1 = [Res() for _ in range(8)]
            for kc in range(8):
                P.dma(POOL, w1_sb[:, kc, :], w_ff1[kc * 128:(kc + 1) * 128, :], writes=[r_w1[kc]])

            def ln_stats(xt, rx, mv, rmv, stt, rst, eps):
                for c2 in range(2):
                    I(DVE, "bn_stats", [rx], [rst], out=stt[:, c2, :], in_=xt[:, c2 * 512:(c2 + 1) * 512])
                I(DVE, "bn_aggr", [rst], [rmv], out=mv[:, 0:2], in_=stt[:])
                I(ACT, "activation", [rmv], [rmv], out=mv[:, 2:3], in_=mv[:, 1:2], func=AF.Ln, bias=eps)
                I(ACT, "activation", [rmv], [rmv], out=mv[:, 2:3], in_=mv[:, 2:3], func=AF.Exp, scale=-0.5)

            def bcast_row(stack, name, src, n):
                t = sbt(stack, name, [128, n]); r = Res()
                P.dma(SP, t[:], src.partition_broadcast(128), writes=[r])
                return t, r

            with ExitStack() as st:
                psY = ring(st, "psE_", [128, 1024], F32, 2, psum=True)
                psT = ring(st, "psET_", [128, 1024], BF16, 2, psum=True)
                wo_sb = sbt(st, "wo_sb", [128, 8, D], BF16); r_wo = [Res() for _ in range(8)]
                for kc in range(8):
                    P.dma(POOL, wo_sb[:, kc, :], w_o[kc * 128:(kc + 1) * 128, :], writes=[r_wo[kc]])
                l1w, r_l1w = bcast_row(st, "l1w", ln1_w, D)
                l1b, r_l1b = bcast_row(st, "l1b", ln1_b, D)
                naw, r_naw = bcast_row(st, "naw", norm_a_w, 128)
                gnw, r_gnw = bcast_row(st, "gnw", gn_w, 512)
                gnb, r_gnb = bcast_row(st, "gnb", gn_b, 512)
                o0_r = ring(st, "o0", [128, 512], F32, 3)
                o1_r = ring(st, "o1", [128, 512], F32, 3)
                jk_r = ring(st, "jk", [128, 512], F32, 2)
                tm_r = ring(st, "tmc", [128, 1536], BF16, 2)
                ss_r = ring(st, "ss", [128, 8], F32, 2)
                bs_r = ring(st, "bs", [128, 4, 6], F32, 2)
                bm_r = ring(st, "bm", [128, 4, 3], F32, 2)
                mix_r = ring(st, "mix", [128, D], BF16, 2)
                mT_r = ring(st, "mT", [128, 8, 128], BF16, 2)
                xt_r = ring(st, "xtc", [128, D], F32, 2)
                tt_r = ring(st, "ttc", [128, D], F32, 2)
                st_r = ring(st, "bstc", [128, 2, 6], F32, 2)
                mv_r = ring(st, "bmvc", [128, 4], F32, 2)
                for q in seqs:
                    cnd = q["cond"]
                    for ts in range(0, q["own"], 128):
                        tm, rtm = tm_r.get()
                        P.dma(ACT, tm[:], q["TM"][ts:ts + 128, :], writes=[rtm])
                        mix, rmix = mix_r.get()
                        o0, ro0 = o0_r.get(); o1, ro1 = o1_r.get()
                        P.dma(SP, o0[:], q["OA"][0, ts:ts + 128, :], writes=[ro0])
                        P.dma(SP, o1[:], q["OA"][1, ts:ts + 128, :], writes=[ro1])
                        I(POOL, "tensor_tensor", [ro0, ro1], [ro0], out=o0[:], in0=o0[:], in1=o1[:], op=ALU.add)
                        ss, rss = ss_r.get(); jk, rjk = jk_r.get()
                        I(POOL, "memset", [], [rss], ss[:], 0.0)
                        for h in range(4):
                            I(ACT, "activation", [ro0], [rjk, rss], out=jk[:, h * 128:(h + 1) * 128], in_=o0[:, h * 128:(h + 1) * 128], func=AF.Square, accum_out=ss[:, h:h + 1])
                        I(ACT, "activation", [rss], [rss], out=ss[:, 4:8], in_=ss[:, 0:4], func=AF.Ln, scale=1.0 / 128.0, bias=1e-6)
                        I(ACT, "activation", [rss], [rss], out=ss[:, 4:8], in_=ss[:, 4:8], func=AF.Exp, scale=-0.5)
                        I(DVE, "tensor_tensor", [ro0, rss], [ro0], out=o0[:].rearrange("p (h x) -> p h x", h=4), in0=o0[:].rearrange("p (h x) -> p h x", h=4),
                                                                       in1=ss[:, 4:8].unsqueeze(2).broadcast_to([128, 4, 128]), op=ALU.mult)
                        I(POOL, "tensor_tensor", [ro0, r_naw], [ro0], out=o0[:].rearrange("p (h x) -> p h x", h=4), in0=o0[:].rearrange("p (h x) -> p h x", h=4),
                                                                  in1=naw[:].unsqueeze(1).broadcast_to([128, 4, 128]), op=ALU.mult)
                        I(DVE, "tensor_tensor", [ro0, rtm], [rmix], out=mix[:, 0:512], in0=o0[:], in1=tm[:, 0:512], op=ALU.mult)
                        p0, rp0 = o0_r.get(); p1, rp1 = o1_r.get()
                        P.dma(SP, p0[:], q["OR"][0, ts:ts + 128, :], writes=[rp0])
                        P.dma(SP, p1[:], q["OR"][1, ts:ts + 128, :], writes=[rp1])
                        I(POOL, "tensor_tensor", [rp0, rp1], [rp0], out=p0[:], in0=p0[:], in1=p1[:], op=ALU.add)
                        bs, rbs = bs_r.get(); bm, rbm = bm_r.get()
                        for h in range(4):
                            I(DVE, "bn_stats", [rp0], [rbs], out=bs[:, h, :], in_=p0[:, h * 128:(h + 1) * 128])
                        for h in range(4):
                            I(DVE, "bn_aggr", [rbs], [rbm], out=bm[:, h, 0:2], in_=bs[:, h, :])
                        I(ACT, "activation", [rbm], [rbm], out=bm[:, :, 2], in_=bm[:, :, 1], func=AF.Ln, bias=1e-5)
                        I(ACT, "activation", [rbm], [rbm], out=bm[:, :, 2], in_=bm[:, :, 2], func=AF.Exp, scale=-0.5)
                        for h in range(4):
                            I(DVE, "tensor_scalar", [rp0, rbm], [rp0], out=p0[:, h * 128:(h + 1) * 128], in0=p0[:, h * 128:(h + 1) * 128], scalar1=bm[:, h, 0:1], scalar2=bm[:, h, 2:3],
                                                                                  op0=ALU.subtract, op1=ALU.mult)
                        I(POOL, "tensor_tensor", [rp0, r_gnw], [rp0], out=p0[:], in0=p0[:], in1=gnw[:], op=ALU.mult)
                        I(POOL, "tensor_tensor", [rp0, r_gnb], [rp0], out=p0[:], in0=p0[:], in1=gnb[:], op=ALU.add)
                        I(DVE, "tensor_tensor", [rp0, rtm], [rmix], out=mix[:, 512:1024], in0=p0[:], in1=tm[:, 1024:1536], op=ALU.mult)
                        pt, rpt = psT.get()
                        for kc in range(8):
                            I(PE, "transpose", [rmix, r_identb], [rpt], pt[:, kc * 128:(kc + 1) * 128], mix[:, kc * 128:(kc + 1) * 128], identb[:])
                        mT, rmT = mT_r.get()
                        I(ACT, "copy", [rpt], [rmT], out=mT[:].rearrange("p a t -> p (a t)"), in_=pt[:])
                        py, rpy = psY.get()
                        for cg in range(2):
                            for kc in range(8):
                                I(PE, "matmul", [rmT, r_wo[kc]], [rpy], py[:, cg * 512:(cg + 1) * 512], lhsT=mT[:, kc, :], rhs=wo_sb[:, kc, cg * 512:(cg + 1) * 512],
                                                                                  start=(kc == 0), stop=(kc == 7))
                        xt, rx = xt_r.get()
                        P.dma(ACT, xt[:], q["x"][ts:ts + 128, :], writes=[rx])
                        tt, rtt = tt_r.get()
                        I(DVE, "tensor_tensor", [rpy, r_gates], [rtt], out=tt[:], in0=py[:], in1=gates[:, 0, cnd, :], op=ALU.mult)
                        I(DVE, "scalar_tensor_tensor", [rx, rtt], [rtt], out=tt[:], in0=xt[:], scalar=ALPHA, in1=tt[:], op0=ALU.mult, op1=ALU.add)
                        stt, rst = st_r.get(); mv, rmv = mv_r.get()
                        ln_stats(tt, rtt, mv, rmv, stt, rst, 1e-6)
                        I(DVE, "tensor_scalar", [rtt, rmv], [rtt], out=tt[:], in0=tt[:], scalar1=mv[:, 0:1], scalar2=mv[:, 2:3], op0=ALU.subtract, op1=ALU.mult)
                        I(POOL, "tensor_tensor", [rtt, r_l1w], [rtt], out=tt[:], in0=tt[:], in1=l1w[:], op=ALU.mult)
                        I(DVE, "tensor_tensor", [rtt, r_l1b], [rtt], out=tt[:], in0=tt[:], in1=l1b[:], op=ALU.add)
                        P.dma(SP, q["X1"][ts:ts + 128, :], tt[:], reads=[rtt])
                P.flush()

            with ExitStack() as st:
                w2_sb = sbt(st, "w2_sb", [128, 32, D], BF16); r_w2 = [Res() for _ in range(32)]
                for kc in range(32):
                    P.dma(POOL, w2_sb[:, kc, :], w_ff2[kc * 128:(kc + 1) * 128, :], writes=[r_w2[kc]])
                psA = ring(st, "psF_", [128, 512], F32, 3, psum=True)
                psY = ring(st, "psFY_", [128, 1024], F32, 2, psum=True)
                psT = ring(st, "psFT_", [128, 1024], BF16, 1, psum=True)
                l2w, r_l2w = bcast_row(st, "l2w", ln2_w, D)
                l2b, r_l2b = bcast_row(st, "l2b", ln2_b, D)
                bf2, r_bf2 = bcast_row(st, "bf2", b_ff2, D)
                b1r = sbt(st, "b1r", [32, 128]); b1c = sbt(st, "b1c", [128, 32]); r_b1 = Res()
                P.dma(SP, b1r[:], b_ff1.rearrange("(a p) -> a p", p=128), writes=[r_b1])
                ps, rps = psA.get()
                I(PE, "matmul", [r_b1, r_identf], [rps], ps[:, 0:32], lhsT=b1r[:], rhs=identf[0:32, 0:32], start=True, stop=True)
                I(DVE, "tensor_copy", [rps], [r_b1], out=b1c[:], in_=ps[:, 0:32])
                x1_r = ring(st, "x1", [128, D], F32, 4)
                xn_r = ring(st, "xnf", [128, D], BF16, 2)
                h2_r = ring(st, "h2T", [128, 8, 256], BF16, 2)
                aT_r = ring(st, "aT", [128, 8, 256], BF16, 2)
                rl_r = ring(st, "rl", [128, 256], F32, 3)
                tt_r = ring(st, "ttf", [128, D], F32, 2)
                st_r = ring(st, "bstf", [128, 2, 6], F32, 2)
                mv_r = ring(st, "bmvf", [128, 4], F32, 2)
                for q in seqs:
                    cnd = q["cond"]
                    TT = 256
                    for t0 in range(0, q["own"], TT):
                        h2, rh2 = h2_r.get()
                        x1s = []
                        for j in range(2):
                            ts = t0 + j * 128
                            x1, rx1 = x1_r.get()
                            x1s.append((x1, rx1))
                            P.dma(SP, x1[:], q["X1"][ts:ts + 128, :], writes=[rx1])
                            stt, rst = st_r.get(); mv, rmv = mv_r.get()
                            ln_stats(x1, rx1, mv, rmv, stt, rst, 1e-6)
                            xn, rxn = xn_r.get()
                            I(DVE, "tensor_scalar", [rx1, rmv], [rxn], out=xn[:], in0=x1[:], scalar1=mv[:, 0:1], scalar2=mv[:, 2:3], op0=ALU.subtract, op1=ALU.mult)
                            pt, rpt = psT.get()
                            for kc in range(8):
                                I(PE, "transpose", [rxn, r_identb], [rpt], pt[:, kc * 128:(kc + 1) * 128], xn[:, kc * 128:(kc + 1) * 128], identb[:])
                            for kc in range(8):
                                I(ACT, "activation", [rpt, r_modc], [rh2], out=h2[:, kc, j * 128:(j + 1) * 128], in_=pt[:, kc * 128:(kc + 1) * 128], func=AF.Identity,
                                                                                                scale=modc[:, 4, kc, cnd:cnd + 1], bias=modc[:, 3, kc, cnd:cnd + 1])
                        pys = [psY.get() for _ in range(2)]
                        for g in range(4):
                            aT, raT = aT_r.get()
                            for f in range(8):
                                ft = g * 8 + f
                                ps, rps = psA.get()
                                for kc in range(8):
                                    I(PE, "matmul", [r_w1[kc], rh2], [rps], ps[:, 0:TT], lhsT=w1_sb[:, kc, ft * 128:(ft + 1) * 128], rhs=h2[:, kc, :], start=(kc == 0), stop=(kc == 7))
                                rl, rrl = rl_r.get()
                                I(ACT, "activation", [rps, r_b1], [rrl], out=rl[:], in_=ps[:, 0:TT], func=AF.Relu, bias=b1c[:, ft:ft + 1])
                                I(POOL if ft % 2 else DVE, "tensor_tensor", [rrl], [raT], out=aT[:, f, :], in0=rl[:], in1=rl[:], op=ALU.mult)
                            for j in range(2):
                                py, rpy = pys[j]
                                for cg in range(2):
                                    for f in range(8):
                                        ft = g * 8 + f
                                        I(PE, "matmul", [raT, r_w2[ft]], [rpy], py[:, cg * 512:(cg + 1) * 512], lhsT=aT[:, f, j * 128:(j + 1) * 128], rhs=w2_sb[:, ft, cg * 512:(cg + 1) * 512],
                                          start=(ft == 0), stop=(ft == 31))
                        for j in range(2):
                            ts = t0 + j * 128
                            x1, rx1 = x1s[j]
                            py, rpy = pys[j]
                            tt, rtt = tt_r.get()
                            I(DVE, "tensor_tensor", [rpy, r_bf2], [rtt], out=tt[:], in0=py[:], in1=bf2[:], op=ALU.add)
                            I(POOL, "tensor_tensor", [rtt, r_gates], [rtt], out=tt[:], in0=tt[:], in1=gates[:, 1, cnd, :], op=ALU.mult)
                            I(DVE, "scalar_tensor_tensor", [rx1, rtt], [rtt], out=tt[:], in0=x1[:], scalar=ALPHA, in1=tt[:], op0=ALU.mult, op1=ALU.add)
                            stt, rst = st_r.get(); mv, rmv = mv_r.get()
                            ln_stats(tt, rtt, mv, rmv, stt, rst, 1e-6)
                            I(DVE, "tensor_scalar", [rtt, rmv], [rtt], out=tt[:], in0=tt[:], scalar1=mv[:, 0:1], scalar2=mv[:, 2:3], op0=ALU.subtract, op1=ALU.mult)
                            I(POOL, "tensor_tensor", [rtt, r_l2w], [rtt], out=tt[:], in0=tt[:], in1=l2w[:], op=ALU.mult)
                            I(DVE, "tensor_tensor", [rtt, r_l2b], [rx1], out=x1[:], in0=tt[:], in1=l2b[:], op=ALU.add)
                            P.dma(SP, q["y"][ts:ts + 128, :], x1[:], reads=[rx1])
                P.flush()
    return nc


def _consts(odd):
    k = np.arange(64)
    ut = np.zeros((2, 64, 64), np.float32)
    ut[0] = (k[:, None] <= k[None, :])
    ut[1] = (k[:, None] >= k[None, :])
    ut8 = np.broadcast_to(ut[:, :, None, None, :], (2, 64, 4, 2, 64)).reshape(2, 64, 512)
    neg = np.zeros((2, 64, 2, 64), np.float32)
    j = k[:, None]; i = k[None, :]
    neg[0, :, 0] = np.where(i > j, 0.0, NEGBIG); neg[0, :, 1] = np.where(i >= j, 0.0, NEGBIG)
    neg[1, :, 0] = np.where(i < j, 0.0, NEGBIG); neg[1, :, 1] = np.where(i <= j, 0.0, NEGBIG)
    neg8 = np.broadcast_to(neg[:, :, None, :, :], (2, 64, 4, 2, 64)).reshape(2, 64, 512)
    lg = np.log1p(-np.exp2(-5.0 - np.arange(4, dtype=np.float64)))
    C = 128
    p = np.arange(C, dtype=np.float64)
    dmt = np.zeros((2, C, 4, C)); xi = np.zeros((2, 4, C)); zeta = np.zeros((C, 8)); gch = np.zeros((8,))
    for d in range(2):
        td = d ^ odd
        lgd = lg if td == 0 else lg[::-1]
        for h in range(4):
            g = lgd[h]
            jj = p[:, None]; ii = p[None, :]
            if d == 0:
                dmt[d, :, h, :] = np.where(ii >= jj, np.exp(g * np.maximum(ii - jj, 0)), 0.0)
                xi[d, h] = np.exp(g * (p + 1)); zeta[:, d * 4 + h] = np.exp(g * (C - 1 - p))
            else:
                dmt[d, :, h, :] = np.where(ii <= jj, np.exp(g * np.maximum(jj - ii, 0)), 0.0)
                xi[d, h] = np.exp(g * (C - p)); zeta[:, d * 4 + h] = np.exp(g * p)
            gch[d * 4 + h] = np.exp(g * C)
    dmt *= 0.125; xi *= 0.125
    xi64 = np.broadcast_to(xi.reshape(2, 1, 512), (2, 64, 512))
    gch64 = np.broadcast_to(gch[None, :], (64, 8))
    r = np.repeat(np.arange(64, dtype=np.float32), 64); col = np.tile(np.arange(64, dtype=np.float32), 64)
    inv = (np.float32(10000.0) ** (-np.arange(16, dtype=np.float32) / np.float32(16))).astype(np.float32)
    ang = np.concatenate([r[:, None] * inv, col[:, None] * inv], -1).astype(np.float32)
    cos, sin = np.cos(ang), np.sin(ang)
    if odd:
        cos, sin = cos[::-1], sin[::-1]
    f = lambda a: np.ascontiguousarray(a, dtype=np.float32)
    return dict(c_ut8=f(ut8), c_neg=f(neg8), c_dmt=f(dmt.reshape(2, C, 512)), c_xi=f(xi64), c_zeta=f(zeta), c_gch=f(gch64),
                rope_cos=f(cos), rope_sin=f(sin))


_NC_CACHE = {}


def kernel(x_prompt, x_sample, c, state_delta, state_ret, c_ctx, w_mod, b_mod, w_in, conv_w, a_log, dt_bias,
           norm_a_w, gn_w, gn_b, w_o, ln1_w, ln1_b, w_ff1, b_ff1, w_ff2, b_ff2, ln2_w, ln2_b):
    f = lambda a: np.ascontiguousarray(np.asarray(a), dtype=np.float32)
    x_prompt, x_sample, c, state_delta, state_ret, c_ctx = map(f, (x_prompt, x_sample, c, state_delta, state_ret, c_ctx))
    w_in0 = f(w_in)[0]
    perm = np.arange(DIN)
    perm[2048:2052], perm[2052:2056] = np.arange(2052, 2056), np.arange(2048, 2052)
    perm[2056:2060], perm[2060:2064] = np.arange(2060, 2064), np.arange(2056, 2060)
    common = dict(w_mod=f(w_mod)[0], b_mod=f(b_mod)[0], norm_a_w=f(norm_a_w)[0], gn_w=f(gn_w)[0], gn_b=f(gn_b)[0], w_o=f(w_o)[0],
                  ln1_w=f(ln1_w)[0], ln1_b=f(ln1_b)[0], w_ff1=f(w_ff1)[0], b_ff1=f(b_ff1)[0], w_ff2=f(w_ff2)[0], b_ff2=f(b_ff2)[0],
                  ln2_w=f(ln2_w)[0], ln2_b=f(ln2_b)[0])
    per_par = []
    for odd in range(2):
        dd = dict(common)
        dd.update(_consts(odd))
        dd["w_in"] = f(w_in0[:, perm]) if odd else w_in0
        dd["conv_w"] = f(f(conv_w)[0][::-1]) if odd else f(conv_w)[0]
        dd["a_log"] = f(f(a_log)[0][::-1] if odd else f(a_log)[0]).reshape(8)
        dd["dt_bias"] = f(f(dt_bias)[0][::-1] if odd else f(dt_bias)[0]).reshape(8)
        per_par.append(dd)
    in_maps = []
    for core in range(8):
        s, odd = core // 2, core % 2
        m = dict(per_par[odd])
        xs_ = x_sample[s]
        xp_ = x_prompt[2 * core:2 * core + 2]
        sd = state_delta[s, 0]
        sr = state_ret[s, 0]
        if odd:
            xs_ = xs_[::-1]; xp_ = xp_[:, ::-1]; sd = sd[::-1]; sr = sr[::-1]
        m["xs"] = f(xs_)
        m["xp"] = f(xp_).reshape(2 * TP, D)
        m["cond"] = f(np.stack([c[s], c_ctx]))
        m["sd0"] = f(sd).reshape(2, 512, 128)
        m["sr0"] = f(sr).reshape(2, 256, 128)
        in_maps.append(m)
    if "nc" not in _NC_CACHE:
        _NC_CACHE["nc"] = build_program()
    res = run_bass_kernel_spmd(_NC_CACHE["nc"], in_maps, core_ids=list(range(8)))
    y_prompt = np.zeros((16, TP, D), np.float32)
    y_sample = np.zeros((4, TS, D), np.float32)
    new_sd = np.zeros((16, 1, 2, 4, 128, 128), np.float32)
    new_sr = np.zeros((16, 1, 2, 4, 64, 128), np.float32)
    for core in range(8):
        s, odd = core // 2, core % 2
        r = res.results[core]
        ys_ = np.asarray(r["ys"], dtype=np.float32)
        yp_ = np.asarray(r["yp"], dtype=np.float32).reshape(2, TP, D)
        sd_ = np.asarray(r["nsd"], dtype=np.float32).reshape(2, 2, 4, 128, 128)
        sr_ = np.asarray(r["nsr"], dtype=np.float32).reshape(2, 2, 4, 64, 128)
        if odd:
            y_sample[s, OWN:] = ys_[::-1]
            y_prompt[2 * core:2 * core + 2] = yp_[:, ::-1]
            new_sd[2 * core:2 * core + 2, 0] = sd_[:, ::-1]
            new_sr[2 * core:2 * core + 2, 0] = sr_[:, ::-1]
        else:
            y_sample[s, :OWN] = ys_
            y_prompt[2 * core:2 * core + 2] = yp_
            new_sd[2 * core:2 * core + 2, 0] = sd_
            new_sr[2 * core:2 * core + 2, 0] = sr_
    return (y_prompt, y_sample, new_sd, new_sr)
```

```python
import numpy as np
from contextlib import ExitStack
import concourse.bass as bass
import concourse.mybir as mybir
from concourse.bass_utils import run_bass_kernel_spmd

F32 = mybir.dt.float32
BF16 = mybir.dt.bfloat16
AF = mybir.ActivationFunctionType
ALU = mybir.AluOpType

PE, ACT, DVE, POOL, SP = "tensor", "scalar", "vector", "gpsimd", "sync"

D = 1024
TS = 4096
OWN = 2048
TP = 256
DIN = 3600
DFF = 4096
ALPHA = 2.0 ** 0.25
NEGBIG = -30000.0
NOREORDER = set()


class Res:
    __slots__ = ("w", "rs")

    def __init__(self):
        self.w = None
        self.rs = []


class Op:
    __slots__ = ("eng", "fn", "deps", "dma_sem", "token", "signal", "epoch", "cost", "lat", "is_dma", "tag")


class Prog:
    NDMA = 12

    def __init__(self, nc, stack):
        self.nc = nc
        self.ops = []
        self.epoch = 0
        self.esem = {}
        self.ecnt = {}
        for e in (PE, ACT, DVE, POOL):
            self.esem[e] = stack.enter_context(nc.semaphore("s_" + e))
            self.ecnt[e] = 0
        self.dsem = {}
        self.dcnt = {}
        self.dlast = {}
        self.drr = {}
        for q in (SP, ACT, POOL):
            self.dsem[q] = [stack.enter_context(nc.semaphore("d_%s%d" % (q, i))) for i in range(self.NDMA)]
            self.dcnt[q] = [0] * self.NDMA
            self.dlast[q] = [None] * self.NDMA
            self.drr[q] = 0
        self.waited = {e: {} for e in (PE, ACT, DVE, POOL, SP)}
        self.n_inst = 0
        self.reorder = True
        self.sim_total = 0.0

    def op(self, eng, fn, reads=(), writes=(), cost=300.0, lat=None):
        op = Op()
        op.eng = eng
        op.fn = fn
        op.deps = []
        op.dma_sem = None
        op.token = None
        op.signal = False
        op.epoch = self.epoch
        op.cost = cost
        op.lat = cost if lat is None else lat
        op.is_dma = False
        op.tag = ""
        deps = op.deps
        for r in reads:
            if r.w is not None:
                deps.append(r.w)
        for r in writes:
            if r.w is not None:
                deps.append(r.w)
            deps.extend(r.rs)
        for r in reads:
            r.rs.append(op)
        for r in writes:
            r.w = op
            r.rs = []
        self.ops.append(op)
        return op

    def dma(self, q, out, in_, reads=(), writes=(), **kw):
        def fn(e):
            return e.dma_start(out=out, in_=in_, **kw)
        nbytes = 1
        for d_ in out.shape:
            nbytes *= d_
        nbytes *= 2 if out.dtype == BF16 else 4
        op = self.op(q, fn, reads, writes, cost=(400.0 if q == POOL else 80.0), lat=2000.0 + nbytes / 150.0)
        op.tag = "dma:%s<-%s" % (out.name, in_.name)
        op.is_dma = True
        op.signal = True
        return op

    def schedule(self, ops):
        import heapq
        n = len(ops)
        idx = {id(o): i for i, o in enumerate(ops)}
        succs = [[] for _ in range(n)]
        npred = [0] * n
        for i, o in enumerate(ops):
            ps = set()
            for d in o.deps:
                if d.epoch == self.epoch:
                    j = idx[id(d)]
                    if j != i:
                        ps.add(j)
            npred[i] = len(ps)
            for j in ps:
                succs[j].append(i)
        ready_t = [0.0] * n
        engs = (PE, ACT, DVE, POOL, SP)
        pend = {e: [] for e in engs}
        avail = {e: [] for e in engs}
        free_t = {e: 0.0 for e in engs}
        for i in range(n):
            if npred[i] == 0:
                heapq.heappush(avail[ops[i].eng], i)
        order = []
        done = 0
        crit = [-1] * n
        rdy_from = [-1] * n
        st_t = [0.0] * n
        last_on = {e: -1 for e in engs}
        while done < n:
            best = None
            for e in engs:
                pe_, av = pend[e], avail[e]
                while pe_ and pe_[0][0] <= free_t[e]:
                    heapq.heappush(av, heapq.heappop(pe_)[1])
                if av:
                    cand = (free_t[e], 0, av[0], e)
                elif pe_:
                    cand = (pe_[0][0], 1, pe_[0][1], e)
                else:
                    continue
                if best is None or cand < best:
                    best = cand
            start, kind, i, e = best
            if kind == 0:
                heapq.heappop(avail[e])
            else:
                heapq.heappop(pend[e])
            o = ops[i]
            start = max(start, ready_t[i], free_t[e])
            if ready_t[i] >= free_t[e]:
                crit[i] = rdy_from[i]
            else:
                crit[i] = last_on[e]
            last_on[e] = i
            st_t[i] = start
            free_t[e] = start + o.cost
            fin = start + o.lat + 120.0
            order.append(i)
            done += 1
            for j in succs[i]:
                f_ = (start + o.cost) if (e == PE and ops[j].eng == PE) else fin
                if f_ > ready_t[j]:
                    ready_t[j] = f_
                    rdy_from[j] = i
                npred[j] -= 1
                if npred[j] == 0:
                    heapq.heappush(pend[ops[j].eng], (ready_t[j], j))
        self.sim_time = max(free_t.values())
        if getattr(self, "debug_crit", False) and order:
            import collections
            i = order[-1]
            agg = collections.Counter(); cnt_ = collections.Counter()
            prev_t = st_t[i] + ops[i].cost
            while i >= 0:
                key = ops[i].eng[:3] + ":" + ops[i].tag
                agg[key] += prev_t - st_t[i]
                cnt_[key] += 1
                prev_t = st_t[i]
                i = crit[i]
            for k_, v_ in agg.most_common(25):
                print("[crit] %-60s %8.1f us  n=%d" % (k_, v_ / 1e3, cnt_[k_]))
        tot = {e: 0.0 for e in engs}
        cnt = {e: 0 for e in engs}
        for o in ops:
            tot[o.eng] += o.cost
            cnt[o.eng] += 1
        print("[prog]   busy us: " + " ".join("%s=%.0f(%d)" % (e, tot[e] / 1e3, cnt[e]) for e in engs))
        return [ops[i] for i in order]

    def flush(self):
        nc = self.nc
        ops = self.schedule(self.ops) if (self.reorder and self.epoch not in NOREORDER) else self.ops
        ep = self.epoch
        for op in ops:
            if op.is_dma:
                q = op.eng
                i = self.drr[q]
                self.drr[q] = (i + 1) % self.NDMA
                prev = self.dlast[q][i]
                if prev is not None:
                    op.deps.append(prev)
                self.dcnt[q][i] += 16
                op.dma_sem = self.dsem[q][i]
                op.token = (op.dma_sem, self.dcnt[q][i])
                self.dlast[q][i] = op
        for op in ops:
            nd = []
            for d in op.deps:
                if d.epoch != ep:
                    continue
                if d.eng == PE and op.eng == PE and d.dma_sem is None and op.dma_sem is None:
                    continue
                nd.append(d)
                if d.dma_sem is None:
                    d.signal = True
            op.deps = nd
        for op in ops:
            if op.dma_sem is None and op.signal:
                self.ecnt[op.eng] += 1
                op.token = (self.esem[op.eng], self.ecnt[op.eng])
        by_eng = {e: [] for e in (PE, ACT, DVE, POOL, SP)}
        for op in ops:
            by_eng[op.eng].append(op)
        self.n_inst += len(ops)

        def emit(ename):
            lst = by_eng[ename]
            waited = self.waited[ename]

            def body(e):
                for op in lst:
                    for d in op.deps:
                        sem, val = d.token
                        k = id(sem)
                        if waited.get(k, 0) < val:
                            e.wait_ge(sem, val)
                            waited[k] = val
                    inst = op.fn(e)
                    if op.signal:
                        if op.dma_sem is not None:
                            inst.then_inc(op.dma_sem, 16)
                        else:
                            inst.then_inc(self.esem[ename], 1)
                if ename in self.dsem:
                    for i, s in enumerate(self.dsem[ename]):
                        v = self.dcnt[ename][i]
                        if v > 0 and waited.get(id(s), 0) < v:
                            e.wait_ge(s, v)
                            waited[id(s)] = v
            return body

        with nc.Block() as blk:
            for ename in (SP, POOL, ACT, DVE, PE):
                if by_eng[ename] or ename in self.dsem:
                    getattr(blk, ename)(emit(ename))
        self.sim_total += getattr(self, "sim_time", 0.0)
        print("[prog] block %d: %d ops, sim %.0f us" % (self.epoch, len(ops), getattr(self, "sim_time", 0.0) / 1e3))
        self.ops = []
        self.epoch += 1


class Ring:
    def __init__(self, items):
        self.items = items
        self.i = 0

    def get(self):
        t = self.items[self.i % len(self.items)]
        self.i += 1
        return t


def build_program(debug=False):
    nc = bass.Bass("TRN2", target_bir_lowering=False)

    def din(name, shape, dt=F32):
        return nc.dram_tensor(name, list(shape), dt, kind="ExternalInput").ap()

    def dout(name, shape, dt=F32):
        return nc.dram_tensor(name, list(shape), dt, kind="ExternalOutput").ap()

    def dscr(name, shape, dt):
        return nc.dram_tensor(name, list(shape), dt, kind="Internal").ap()

    xs = din("xs", [TS, D])
    xp = din("xp", [2 * TP, D])
    cond = din("cond", [2, D])
    sd0 = din("sd0", [2, 4 * 128, 128])
    sr0 = din("sr0", [2, 4 * 64, 128])
    w_mod = din("w_mod", [D, 6 * D])
    b_mod = din("b_mod", [6 * D])
    w_in = din("w_in", [D, DIN])
    conv_w = din("conv_w", [5, 1536])
    a_log = din("a_log", [8])
    dt_bias = din("dt_bias", [8])
    norm_a_w = din("norm_a_w", [128])
    gn_w = din("gn_w", [512])
    gn_b = din("gn_b", [512])
    w_o = din("w_o", [D, D])
    ln1_w = din("ln1_w", [D])
    ln1_b = din("ln1_b", [D])
    w_ff1 = din("w_ff1", [D, DFF])
    b_ff1 = din("b_ff1", [DFF])
    w_ff2 = din("w_ff2", [DFF, D])
    b_ff2 = din("b_ff2", [D])
    ln2_w = din("ln2_w", [D])
    ln2_b = din("ln2_b", [D])
    rope_cos = din("rope_cos", [TS, 32])
    rope_sin = din("rope_sin", [TS, 32])
    c_ut8 = din("c_ut8", [2, 64, 512])
    c_neg = din("c_neg", [2, 64, 512])
    c_dmt = din("c_dmt", [2, 128, 512])
    c_xi = din("c_xi", [2, 64, 512])
    c_zeta = din("c_zeta", [128, 8])
    c_gch = din("c_gch", [64, 8])

    ys = dout("ys", [OWN, D])
    yp = dout("yp", [2 * TP, D])
    nsd = dout("nsd", [2, 2, 4 * 128, 128])
    nsr = dout("nsr", [2, 2, 4 * 64, 128])

    seqs = []
    for si, (nm, T, own) in enumerate((("s", TS, OWN), ("p0", TP, TP), ("p1", TP, TP))):
        q = dict(i=si, name=nm, T=T, own=own, rope=(si == 0), cond=(0 if si == 0 else 1))
        q["x"] = xs if si == 0 else xp[(si - 1) * TP:si * TP, :]
        q["y"] = ys if si == 0 else yp[(si - 1) * TP:si * TP, :]
        q["PQ"] = dscr("PQ" + nm, [1536, T], BF16)
        q["QK"] = dscr("QK" + nm, [1024, T], BF16)
        q["KV"] = dscr("KV" + nm, [T, 1024], BF16)
        q["TM"] = dscr("TM" + nm, [T, 1536], BF16)
        q["RQK"] = dscr("RQK" + nm, [512, T], BF16)
        q["RK"] = dscr("RK" + nm, [T, 256], BF16)
        q["OA"] = dscr("OA" + nm, [2, own, 512], F32)
        q["OR"] = dscr("OR" + nm, [2, own, 512], F32)
        q["X1"] = dscr("X1" + nm, [own, D], F32)
        q["S6Q"] = dscr("S6Q" + nm, [2, T // 64, 64, 512], BF16)
        q["nch"] = T // 64
        seqs.append(q)

    with ExitStack() as st0:
        P = Prog(nc, st0)

        def I(eng, meth, reads, writes, *a, **k):
            o_ = k.get("out", a[0] if a else None)
            fr = 1
            for d_ in o_.shape[1:]:
                fr *= d_
            if eng == PE:
                l_ = k.get("lhsT", a[1] if len(a) > 1 else None)
                c_ = (max(64, fr) + 8) / 2.4 * (4.0 if (l_ is not None and l_.dtype == F32) else 1.0)
                lat = c_ + 150.0
            elif eng == ACT:
                c_ = (224 + fr) / 1.2
                lat = c_
            elif eng == DVE:
                c_ = (150 + fr) / 0.96
                lat = c_
            else:
                c_ = (150 + 2 * fr) / 1.2
                lat = c_
            op_ = P.op(eng, lambda e: getattr(e, meth)(*a, **k), reads, writes, cost=c_, lat=lat)
            op_.tag = "%s:%s" % (meth, o_.name)

        def sbt(stack, name, shape, dt=F32):
            return stack.enter_context(nc.sbuf_tensor(name, list(shape), dt))

        def pst(stack, name, shape, dt=F32):
            return stack.enter_context(nc.psum_tensor(name, list(shape), dt))

        def ring(stack, name, shape, dt, n, psum=False):
            items = []
            for i in range(n):
                t = (pst if psum else sbt)(stack, "%s%d" % (name, i), shape, dt)
                items.append((t, Res()))
            return Ring(items)

        identf = sbt(st0, "identf", [128, 128]); r_identf = Res()
        identb = sbt(st0, "identb", [128, 128], BF16); r_identb = Res()
        onesb = sbt(st0, "onesb", [128, 128], BF16); r_onesb = Res()
        onesq = sbt(st0, "onesq", [128, 128], BF16); r_onesq = Res()
        onesf = sbt(st0, "onesf", [64, 128]); r_onesf = Res()
        modc = sbt(st0, "modc", [128, 6, 8, 2]); r_modc = Res()
        gates = sbt(st0, "gates", [128, 2, 2, D]); r_gates = Res()
        dtb = sbt(st0, "dtb", [8, 1]); negA = sbt(st0, "negA", [8, 1]); r_gc = Res()
        stG = ExitStack()
        G = []
        for q in seqs:
            G.append((sbt(stG, "G" + q["name"], [64, q["nch"], 24]), Res()))
        G128 = []
        for q in seqs:
            G128.append((sbt(stG, "GG" + q["name"], [128, q["nch"] // 2, 24]), Res()))

        I(POOL, "memset", [], [r_identf], identf[:], 0.0)
        I(POOL, "affine_select", [r_identf], [r_identf], out=identf[:], in_=identf[:], pattern=[[-1, 128]],
                                             compare_op=ALU.not_equal, fill=1.0, base=0, channel_multiplier=1)
        I(DVE, "tensor_copy", [r_identf], [r_identb], out=identb[:], in_=identf[:])
        I(POOL, "memset", [], [r_onesb], onesb[:], 1.0)
        I(POOL, "memset", [], [r_onesq], onesq[:], 128.0)
        I(POOL, "memset", [], [r_onesf], onesf[:], 1.0)
        P.dma(SP, dtb[:], dt_bias.rearrange("(p o) -> p o", o=1), writes=[r_gc])
        P.dma(SP, negA[:], a_log.rearrange("(p o) -> p o", o=1), writes=[r_gc])
        I(ACT, "activation", [r_gc], [r_gc], out=negA[:], in_=negA[:], func=AF.Exp)
        I(DVE, "tensor_scalar", [r_gc], [r_gc], out=negA[:], in0=negA[:], scalar1=-1.0, scalar2=None, op0=ALU.mult)

        stW = ExitStack()
        w_in_sb = sbt(stW, "w_in_sb", [128, 8, DIN], BF16); r_win = [Res() for _ in range(8)]
        for kc in range(8):
            P.dma(POOL, w_in_sb[:, kc, :], w_in[kc * 128:(kc + 1) * 128, :], writes=[r_win[kc]])
        with ExitStack() as st:
            psA = ring(st, "ps0_", [128, 512], F32, 4, psum=True)
            crow = sbt(st, "crow", [16, 128]); r_crow = Res()
            scT = sbt(st, "scT", [128, 16]); r_scT = Res()
            brow = sbt(st, "brow", [48, 128]); r_brow = Res()
            bcol = sbt(st, "bcol", [128, 48]); r_bcol = Res()
            wm = ring(st, "wm", [128, 8, D], F32, 2)
            wm_res = {}
            gbt = ring(st, "gbt", [128, 128], F32, 2)
            P.dma(SP, crow[:], cond.rearrange("c (k p) -> (c k) p", p=128), writes=[r_crow])
            P.dma(SP, brow[:], b_mod.rearrange("(a p) -> a p", p=128), writes=[r_brow])
            ps, rps = psA.get()
            I(PE, "matmul", [r_crow, r_identf], [rps], ps[:, 0:16], lhsT=crow[:], rhs=identf[0:16, 0:16], start=True, stop=True)
            I(ACT, "activation", [rps], [r_scT], out=scT[:], in_=ps[:, 0:16], func=AF.Exp, scale=-1.0)
            I(DVE, "tensor_scalar", [r_scT], [r_scT], out=scT[:], in0=scT[:], scalar1=1.0, scalar2=None, op0=ALU.add)
            I(DVE, "reciprocal", [r_scT], [r_scT], out=scT[:], in_=scT[:])
            I(DVE, "tensor_tensor", [r_scT, rps], [r_scT], out=scT[:], in0=scT[:], in1=ps[:, 0:16], op=ALU.mult)
            ps, rps = psA.get()
            I(PE, "matmul", [r_brow, r_identf], [rps], ps[:, 0:48], lhsT=brow[:], rhs=identf[0:48, 0:48], start=True, stop=True)
            I(DVE, "tensor_copy", [rps], [r_bcol], out=bcol[:], in_=ps[:, 0:48])
            scv = scT[:].rearrange("p (c k) -> p c k", c=2)
            for blk in range(6):
                wt, rw0 = wm.get()
                if id(rw0) not in wm_res:
                    wm_res[id(rw0)] = [Res() for _ in range(8)]
                rw = wm_res[id(rw0)]
                for kc in range(8):
                    P.dma(SP if kc % 2 == 0 else ACT, wt[:, kc, :], w_mod[kc * 128:(kc + 1) * 128, blk * D:(blk + 1) * D], writes=[rw[kc]])
                ps, rps = psA.get()
                for ft in range(8):
                    for kc in range(8):
                        I(PE, "matmul", [rw[kc], r_scT], [rps], ps[:, ft * 2:ft * 2 + 2], lhsT=wt[:, kc, ft * 128:(ft + 1) * 128], rhs=scv[:, :, kc],
                            start=(kc == 0), stop=(kc == 7))
                I(DVE, "tensor_tensor", [rps, r_bcol], [r_modc], out=modc[:, blk, :, :], in0=ps[:, 0:16].rearrange("p (f c) -> p f c", c=2),
                    in1=bcol[:, blk * 8:(blk + 1) * 8].unsqueeze(2).broadcast_to([128, 8, 2]), op=ALU.add)
                if blk in (1, 4):
                    I(DVE, "tensor_scalar", [r_modc], [r_modc], out=modc[:, blk, :, :], in0=modc[:, blk, :, :], scalar1=1.0, scalar2=None, op0=ALU.add)
            for wi, blk in enumerate((2, 5)):
                for c in range(2):
                    for half in range(2):
                        ps, rps = psA.get()
                        for f4 in range(4):
                            ft = half * 4 + f4
                            gb, rgb = gbt.get()
                            I(DVE, "tensor_copy", [r_modc], [rgb], out=gb[:], in_=modc[:, blk, ft, c:c + 1].broadcast_to([128, 128]))
                            I(PE, "matmul", [rgb, r_identf], [rps], ps[:, f4 * 128:(f4 + 1) * 128], lhsT=gb[:], rhs=identf[:], start=True, stop=True)
                        I(ACT, "copy", [rps], [r_gates], out=gates[:, wi, c, half * 512:(half + 1) * 512], in_=ps[:])
            P.flush()

        with ExitStack() as st:
            cosT = sbt(st, "cosT", [128, TS // 128, 32]); sinT = sbt(st, "sinT", [128, TS // 128, 32]); r_rope = Res()
            P.dma(SP, cosT[:], rope_cos.rearrange("(n p) f -> p n f", p=128), writes=[r_rope])
            P.dma(SP, sinT[:], rope_sin.rearrange("(n p) f -> p n f", p=128), writes=[r_rope])
            psA = ring(st, "psA_", [128, 512], F32, 6, psum=True)
            psT = ring(st, "psT_", [128, 1024], BF16, 2, psum=True)
            xt_r = ring(st, "xt", [128, D], F32, 3)
            xn_r = ring(st, "xn", [128, D], BF16, 2)
            st_r = ring(st, "bst", [128, 2, 6], F32, 2)
            mv_r = ring(st, "bmv", [128, 4], F32, 2)
            hT_r = ring(st, "hT", [128, 8, 512], BF16, 2)
            pq_r = ring(st, "pq", [128, 512], BF16, 4)
            tm_r = ring(st, "tm", [128, 1536], BF16, 2)
            qkf_r = ring(st, "qkf", [128, 512], F32, 2)
            rt_r = ring(st, "rt", [128, 4, 256], F32, 1)
            rqk_r = ring(st, "rqk", [128, 512], BF16, 3)
            rqT_r = ring(st, "rqT", [128, 4, 128], BF16, 3)
            gt_r = ring(st, "gt", [8, 5, 512], F32, 1)

            def layernorm_stats(xt, rx, mv, rmv, stt, rst, eps):
                for c2 in range(2):
                    I(DVE, "bn_stats", [rx], [rst], out=stt[:, c2, :], in_=xt[:, c2 * 512:(c2 + 1) * 512])
                I(DVE, "bn_aggr", [rst], [rmv], out=mv[:, 0:2], in_=stt[:])
                I(ACT, "activation", [rmv], [rmv], out=mv[:, 2:3], in_=mv[:, 1:2], func=AF.Ln, bias=eps)
                I(ACT, "activation", [rmv], [rmv], out=mv[:, 2:3], in_=mv[:, 2:3], func=AF.Exp, scale=-0.5)

            cwr = sbt(st, "cwr", [60, 128]); r_cwr = Res()
            cwc = sbt(st, "cwc", [128, 60]); r_cwc = Res()
            Dg = sbt(st, "Dg", [128, 60, 128], BF16); r_Dg = Res()
            P.dma(SP, cwr[:], conv_w.rearrange("k (c p) -> (k c) p", p=128), writes=[r_cwr])
            ps, rps = psA.get()
            I(PE, "matmul", [r_cwr, r_identf], [rps], ps[:, 0:60], lhsT=cwr[:], rhs=identf[0:60, 0:60], start=True, stop=True)
            I(DVE, "tensor_copy", [rps], [r_cwc], out=cwc[:], in_=ps[:, 0:60])
            for i in range(60):
                I(DVE, "tensor_scalar", [r_identf, r_cwc], [r_Dg], out=Dg[:, i, :], in0=identf[:], scalar1=cwc[:, i:i + 1], scalar2=None, op0=ALU.mult)
            pw_r = ring(st, "pw", [128, 516], BF16, 4)
            cs_r = ring(st, "cs", [128, 512], F32, 3)
            sq_r = ring(st, "sq", [128, 512], BF16, 3)
            rs_r = ring(st, "rs", [128, 512], F32, 3)
            xh_r = ring(st, "xh", [128, 512], BF16, 4)
            kt_r = ring(st, "kt", [128, 4, 128], BF16, 4)
            def phaseA_tile(q, t0):
                T = q["T"]
                TT = min(512, T)
                nsub = TT // 128
                Gt, rG = G[q["i"]]
                cnd = q["cond"]
                foreign = t0 >= q["own"]
                hT, rhT = hT_r.get()
                for j in range(nsub):
                    ts = t0 + j * 128
                    xt, rx = xt_r.get()
                    P.dma(SP, xt[:], q["x"][ts:ts + 128, :], writes=[rx])
                    stt, rst = st_r.get(); mv, rmv = mv_r.get()
                    layernorm_stats(xt, rx, mv, rmv, stt, rst, 1e-6)
                    xn, rxn = xn_r.get()
                    I(DVE, "tensor_scalar", [rx, rmv], [rxn], out=xn[:], in0=xt[:], scalar1=mv[:, 0:1], scalar2=mv[:, 2:3],
                                                                           op0=ALU.subtract, op1=ALU.mult)
                    pt, rpt = psT.get()
                    for kc in range(8):
                        I(PE, "transpose", [rxn, r_identb], [rpt], pt[:, kc * 128:(kc + 1) * 128], xn[:, kc * 128:(kc + 1) * 128], identb[:])
                    for kc in range(8):
                        I(ACT, "activation", [rpt, r_modc], [rhT], out=hT[:, kc, j * 128:(j + 1) * 128], in_=pt[:, kc * 128:(kc + 1) * 128], func=AF.Identity,
                            scale=modc[:, 1, kc, cnd:cnd + 1], bias=modc[:, 0, kc, cnd:cnd + 1])
                for ct in range(12):
                    if foreign and ct < 4 and t0 != q["own"]:
                        continue
                    ps, rps = psA.get()
                    for kc in range(8):
                        I(PE, "matmul", [r_win[kc], rhT], [rps], ps[:, 0:TT], lhsT=w_in_sb[:, kc, ct * 128:(ct + 1) * 128], rhs=hT[:, kc, 0:TT],
                                                                            start=(kc == 0), stop=(kc == 7))
                    pq, rpq = pq_r.get()
                    if ct % 2 == 0:
                        I(ACT, "copy", [rps], [rpq], out=pq[:, 0:TT], in_=ps[:, 0:TT])
                    else:
                        I(DVE, "tensor_copy", [rps], [rpq], out=pq[:, 0:TT], in_=ps[:, 0:TT])
                    rPQ[(q["i"], t0, ct)] = Res()
                    P.dma(SP, q["PQ"][ct * 128:(ct + 1) * 128, t0:t0 + TT], pq[:, 0:TT], reads=[rpq], writes=[rPQ[(q["i"], t0, ct)]])
                psa_, rpa = psA.get()
                psb_, rpb = psA.get()
                for which, (pp, rp) in enumerate(((psa_, rpa), (psb_, rpb))):
                    c0 = 2048 + which * 8
                    for kc in range(8):
                        I(PE, "matmul", [r_win[kc], rhT], [rp], pp[0:8, 0:TT], lhsT=w_in_sb[:, kc, c0:c0 + 8], rhs=hT[:, kc, 0:TT], start=(kc == 0), stop=(kc == 7))
                pa = psa_[0:8, 0:TT]; pb = psb_[0:8, 0:TT]
                gt, rgt = gt_r.get()
                I(ACT, "activation", [rpa, r_gc], [rgt], out=gt[:, 0, 0:TT], in_=pa, func=AF.Exp, bias=dtb[:, 0:1])
                I(ACT, "activation", [rgt], [rgt], out=gt[:, 0, 0:TT], in_=gt[:, 0, 0:TT], func=AF.Ln, bias=1.0)
                I(DVE, "tensor_scalar", [rgt, r_gc], [rgt], out=gt[:, 0, 0:TT], in0=gt[:, 0, 0:TT], scalar1=negA[:, 0:1], scalar2=None, op0=ALU.mult)
                I(ACT, "activation", [rpb], [rgt], out=gt[:, 3, 0:TT], in_=pb, func=AF.Exp, scale=-1.0)
                I(ACT, "activation", [rgt], [rgt], out=gt[:, 2, 0:TT], in_=gt[:, 3, 0:TT], func=AF.Ln, bias=1.0)
                I(DVE, "tensor_scalar", [rgt], [rgt], out=gt[:, 3, 0:TT], in0=gt[:, 3, 0:TT], scalar1=1.0, scalar2=None, op0=ALU.add)
                I(DVE, "reciprocal", [rgt], [rgt], out=gt[:, 1, 0:TT], in_=gt[:, 3, 0:TT])
                ps, rps = psA.get()
                nc64 = TT // 64
                for c64 in range(nc64):
                    for a in range(3):
                        I(PE, "matmul", [rgt, r_identf], [rps], ps[0:64, c64 * 24 + a * 8:c64 * 24 + a * 8 + 8],
                                                                             lhsT=gt[:, a, c64 * 64:(c64 + 1) * 64], rhs=identf[0:8, 0:8], start=True, stop=True)
                I(DVE, "tensor_copy", [rps], [rG], out=Gt[:, t0 // 64:t0 // 64 + nc64, :], in_=ps[0:64, 0:nc64 * 24].rearrange("p (c a) -> p c a", a=24))
                ps, rps = psA.get()
                Gt2, rG2 = G128[q["i"]]
                for c128 in range(nsub):
                    for a in range(3):
                        I(PE, "matmul", [rgt, r_identf], [rps], ps[:, c128 * 24 + a * 8:c128 * 24 + a * 8 + 8],
                          lhsT=gt[:, a, c128 * 128:(c128 + 1) * 128], rhs=identf[0:8, 0:8], start=True, stop=True)
                I(DVE, "tensor_copy", [rps], [rG2], out=Gt2[:, t0 // 128:t0 // 128 + nsub, :], in_=ps[:, 0:nsub * 24].rearrange("p (c a) -> p c a", a=24))
                for j in range(nsub):
                    ts = t0 + j * 128
                    tm, rtm = tm_r.get()
                    for gi, c0 in enumerate((1536, 2064, 2576, 3088)):
                        if foreign and gi in (0, 3):
                            continue
                        ps, rps = psA.get()
                        for kc in range(8):
                            I(PE, "matmul", [r_win[kc], rhT], [rps], ps[:], lhsT=hT[:, kc, j * 128:(j + 1) * 128], rhs=w_in_sb[:, kc, c0:c0 + 512],
                                                                                  start=(kc == 0), stop=(kc == 7))
                        if gi == 0:
                            I(ACT, "activation", [rps], [rtm], out=tm[:, 0:512], in_=ps[:], func=AF.Silu)
                        elif gi == 2:
                            I(DVE, "tensor_copy", [rps], [rtm], out=tm[:, 512:1024], in_=ps[:])
                        elif gi == 3:
                            I(ACT, "activation", [rps], [rtm], out=tm[:, 1024:1536], in_=ps[:], func=AF.Silu)
                        else:
                            rqk, rrqk = rqk_r.get()
                            if q["rope"]:
                                qkf, rqkf = qkf_r.get()
                                I(ACT, "copy", [rps], [rqkf], out=qkf[:], in_=ps[:])
                                rt, rrt = rt_r.get()
                                xv = qkf[:].rearrange("p (a h f) -> p a h f", a=8, h=2)
                                ov = rqk[:].rearrange("p (a h f) -> p a h f", a=8, h=2)
                                nt = ts // 128
                                cb = cosT[:, nt, :].unsqueeze(1).broadcast_to([128, 8, 32])
                                sb_ = sinT[:, nt, :].unsqueeze(1).broadcast_to([128, 8, 32])
                                rv = [rt[:, k, :].rearrange("p (a f) -> p a f", a=8) for k in range(4)]
                                I(DVE, "tensor_tensor", [rqkf, r_rope], [rrt], out=rv[0], in0=xv[:, :, 0, :], in1=cb, op=ALU.mult)
                                I(DVE, "tensor_tensor", [rqkf, r_rope], [rrt], out=rv[1], in0=xv[:, :, 1, :], in1=sb_, op=ALU.mult)
                                I(DVE, "tensor_tensor", [rqkf, r_rope], [rrt], out=rv[2], in0=xv[:, :, 0, :], in1=sb_, op=ALU.mult)
                                I(DVE, "tensor_tensor", [rqkf, r_rope], [rrt], out=rv[3], in0=xv[:, :, 1, :], in1=cb, op=ALU.mult)
                                I(DVE, "tensor_tensor", [rrt], [rrqk], out=ov[:, :, 0, :], in0=rv[0], in1=rv[1], op=ALU.subtract)
                                I(DVE, "tensor_tensor", [rrt], [rrqk], out=ov[:, :, 1, :], in0=rv[2], in1=rv[3], op=ALU.add)
                            else:
                                I(ACT, "copy", [rps], [rrqk], out=rqk[:], in_=ps[:])
                            P.dma(SP, q["RK"][ts:ts + 128, :], rqk[:, 256:512], reads=[rrqk])
                            pt, rpt = psT.get()
                            for b4 in range(4):
                                I(PE, "transpose", [rrqk, r_identb], [rpt], pt[:, b4 * 128:(b4 + 1) * 128], rqk[:, b4 * 128:(b4 + 1) * 128], identb[:])
                            rqT, rrqT = rqT_r.get()
                            I(DVE, "tensor_copy", [rpt], [rrqT], out=rqT[:].rearrange("p a t -> p (a t)"), in_=pt[:, 0:512])
                            P.dma(SP, q["RQK"][:, ts:ts + 128].rearrange("(a p) t -> p a t", p=128), rqT[:], reads=[rrqT])
                    if foreign:
                        P.dma(SP, q["TM"][ts:ts + 128, 512:1024], tm[:, 512:1024], reads=[rtm])
                    else:
                        P.dma(SP, q["TM"][ts:ts + 128, :], tm[:], reads=[rtm])

            def phaseA2_tile(q, t0):
                T = q["T"]
                TT = min(512, T)
                nsub = TT // 128
                for ct in range(12):
                    if t0 >= q["own"] and ct < 4:
                        continue
                    kind = ct // 4
                    h = ct % 4
                    pw, rpw = pw_r.get()
                    lo = max(t0 - 2, 0); hi = min(t0 + TT + 2, T)
                    if lo > t0 - 2:
                        I(DVE, "memset", [], [rpw], pw[:, 0:2], 0.0)
                    if hi < t0 + TT + 2:
                        I(DVE, "memset", [], [rpw], pw[:, TT + 2:TT + 4], 0.0)
                    rdeps = [rPQ[(q["i"], t_, ct)] for t_ in (t0 - TT, t0, t0 + TT) if (q["i"], t_, ct) in rPQ]
                    P.dma(SP, pw[:, lo - t0 + 2:hi - t0 + 2], q["PQ"][ct * 128:(ct + 1) * 128, lo:hi], reads=rdeps, writes=[rpw])
                    ps, rps = psA.get()
                    for k in range(5):
                        I(PE, "matmul", [r_Dg, rpw], [rps], ps[:, 0:TT], lhsT=Dg[:, k * 12 + ct, :], rhs=pw[:, k:k + TT], start=(k == 0), stop=(k == 4))
                    xh, rxh = xh_r.get()
                    if kind == 2:
                        I(ACT, "activation", [rps], [rxh], out=xh[:, 0:TT], in_=ps[:, 0:TT], func=AF.Silu)
                    else:
                        cs, rcs = cs_r.get()
                        I(ACT, "activation", [rps], [rcs], out=cs[:, 0:TT], in_=ps[:, 0:TT], func=AF.Silu)
                        sq, rsq = sq_r.get()
                        I(DVE, "tensor_tensor", [rcs], [rsq], out=sq[:, 0:TT], in0=cs[:, 0:TT], in1=cs[:, 0:TT], op=ALU.mult)
                        ps2, rps2 = psA.get()
                        ones_ = onesq if kind == 0 else onesb
                        r_ones = r_onesq if kind == 0 else r_onesb
                        I(PE, "matmul", [rsq, r_ones], [rps2], ps2[:, 0:TT], lhsT=ones_[:], rhs=sq[:, 0:TT], start=True, stop=True)
                        rs, rrs = rs_r.get()
                        eps = 1e-6 * (128.0 if kind == 0 else 1.0)
                        I(ACT, "activation", [rps2], [rrs], out=rs[:, 0:TT], in_=ps2[:, 0:TT], func=AF.Ln, bias=eps)
                        I(ACT, "activation", [rrs], [rrs], out=rs[:, 0:TT], in_=rs[:, 0:TT], func=AF.Exp, scale=-0.5)
                        I(DVE, "tensor_tensor", [rcs, rrs], [rxh], out=xh[:, 0:TT], in0=cs[:, 0:TT], in1=rs[:, 0:TT], op=ALU.mult)
                        rb = h * 2 + (1 if kind == 0 else 0)
                        P.dma(SP, q["QK"][rb * 128:(rb + 1) * 128, t0:t0 + TT], xh[:, 0:TT], reads=[rxh])
                    if kind >= 1:
                        pt, rpt = psT.get()
                        for j in range(nsub):
                            I(PE, "transpose", [rxh, r_identb], [rpt], pt[:, j * 128:(j + 1) * 128], xh[:, j * 128:(j + 1) * 128], identb[:])
                        kt, rkt = kt_r.get()
                        if kind == 1:
                            I(DVE, "tensor_copy", [rpt], [rkt], out=kt[:, 0:nsub, :].rearrange("p a t -> p (a t)"), in_=pt[:, 0:nsub * 128])
                        else:
                            I(ACT, "copy", [rpt], [rkt], out=kt[:, 0:nsub, :].rearrange("p a t -> p (a t)"), in_=pt[:, 0:nsub * 128])
                        c0 = (kind - 1) * 512 + h * 128
                        P.dma(SP, q["KV"][t0:t0 + TT, c0:c0 + 128].rearrange("(j p) c -> p j c", p=128), kt[:, 0:nsub, :], reads=[rkt])

            rPQ = {}
            tilesA = []
            for q in seqs:
                TT_ = min(512, q["T"])
                for t0 in range(0, q["T"], TT_):
                    tilesA.append((q, t0, TT_))
            pend = []
            for (q, t0, TT_) in tilesA:
                phaseA_tile(q, t0)
                pend.append((q, t0, TT_))
                for (q2, t2, TT2) in list(pend):
                    nxt = t2 + TT2
                    if nxt >= q2["T"] or (q2["i"], nxt, 11) in rPQ:
                        phaseA2_tile(q2, t2)
                        pend.remove((q2, t2, TT2))
            assert not pend
            P.flush()
        stW.close()

        with ExitStack() as st2:


            with ExitStack() as st:
                psA = ring(st, "psC_", [128, 512], F32, 7, psum=True)
                psR = psA
                psT = ring(st, "psCT_", [128, 1024], BF16, 1, psum=True)
                ut8 = sbt(st, "ut8", [64, 2, 512]); negm = sbt(st, "negm", [64, 2, 512]); r_cst = Res()
                ut8b = sbt(st, "ut8b", [64, 2, 512], BF16); negb = sbt(st, "negb", [64, 2, 512], BF16)
                idb4 = sbt(st, "idb4", [64, 4, 64], BF16)
                for d in range(2):
                    P.dma(SP, ut8[:, d, :], c_ut8[d], writes=[r_cst])
                    P.dma(SP, negm[:, d, :], c_neg[d], writes=[r_cst])
                I(DVE, "tensor_copy", [r_cst], [r_cst], out=ut8b[:], in_=ut8[:])
                I(DVE, "tensor_copy", [r_cst], [r_cst], out=negb[:], in_=negm[:])
                I(DVE, "tensor_copy", [r_identb], [r_cst], out=idb4[:], in_=identb[0:64, 0:64].unsqueeze(1).broadcast_to([64, 4, 64]))
                qk_r = ring(st, "qkc", [128, 8, 64], BF16, 5)
                chains = []
                for q in seqs:
                    Gt, rG = G[q["i"]]
                    nch = q["nch"]
                    for d in range(2):
                        nm = "%s%d" % (q["name"], d)
                        S = [sbt(st, "S%s_%d" % (nm, k), [128, 4, 128]) for k in range(2)]
                        rS = [Res(), Res()]
                        Sb = sbt(st, "Sb" + nm, [128, 4, 128], BF16); rSb = Res()
                        pre = sbt(st, "pre" + nm, [64, nch, 28]); rpre = Res()
                        egl = sbt(st, "egl" + nm, [128, nch, 4]); regl = Res()
                        if q["i"] == 0:
                            P.dma(SP, S[0][:], sd0[d].rearrange("(h k) v -> k h v", k=128), writes=[rS[0]])
                        else:
                            I(POOL, "memset", [], [rS[0]], S[0][:], 0.0)
                        I(ACT, "copy", [rS[0]], [rSb], out=Sb[:], in_=S[0][:])
                        I(DVE, "tensor_copy", [rG], [rpre], out=pre[:, :, 0:4], in_=Gt[:, :, d * 4:d * 4 + 4])
                        nn = nch * 4
                        gsd = sbt(st, "gsd" + nm, [64, nn]); rgsd = Res()
                        I(DVE, "tensor_copy", [rG], [rgsd], out=gsd[:].rearrange("p (c h) -> p c h", h=4), in_=Gt[:, :, d * 4:d * 4 + 4])
                        ps, rps = psA.get()
                        ps2, rps2 = psA.get()
                        I(PE, "matmul", [r_cst, rgsd], [rps], ps[0:64, 0:nn], lhsT=ut8[:, d, 0:64], rhs=gsd[:], start=True, stop=True)
                        I(PE, "matmul", [r_onesf, rgsd], [rps2], ps2[:, 0:nn], lhsT=onesf[:], rhs=gsd[:], start=True, stop=True)
                        gcv = ps[0:64, 0:nn].rearrange("p (c h) -> p c h", h=4)
                        I(DVE, "tensor_copy", [rps], [rpre], out=pre[:, :, 4:8], in_=gcv)
                        b2 = pre[:, :, 8:16].rearrange("p c (h a) -> p c h a", a=2)
                        I(DVE, "tensor_tensor", [rps, rG], [rpre], out=b2[:, :, :, 0], in0=gcv, in1=Gt[:, :, 16 + d * 4:20 + d * 4], op=ALU.add)
                        I(DVE, "tensor_copy", [rps], [rpre], out=b2[:, :, :, 1], in_=gcv)
                        I(ACT, "activation", [rps], [rpre], out=pre[:, :, 20:24], in_=gcv, func=AF.Exp)
                        I(DVE, "tensor_scalar", [rpre], [rpre], out=pre[:, :, 16:20], in0=pre[:, :, 20:24], scalar1=-1.0, scalar2=None, op0=ALU.mult)
                        glv = ps2[0:64, 0:nn].rearrange("p (c h) -> p c h", h=4)
                        I(DVE, "tensor_tensor", [rps2, rpre], [rpre], out=pre[:, :, 24:28], in0=glv, in1=pre[:, :, 4:8], op=ALU.subtract)
                        I(ACT, "activation", [rpre], [rpre], out=pre[:, :, 24:28], in_=pre[:, :, 24:28], func=AF.Exp)
                        I(DVE, "tensor_tensor", [rpre, rG], [rpre], out=pre[:, :, 24:28], in0=pre[:, :, 24:28], in1=Gt[:, :, 8 + d * 4:12 + d * 4], op=ALU.mult)
                        I(ACT, "activation", [rps2], [regl], out=egl[:].rearrange("p c h -> p (c h)"), in_=ps2[:, 0:nn], func=AF.Exp)
                        nown = q["own"] // 64
                        if d == 0:
                            order = list(range(nown))
                        else:
                            order = list(range(nch - 1, -1, -1))
                        npair = nch // 2
                        if d == 0:
                            porder = list(range(nown // 2))
                        else:
                            porder = list(range(npair - 1, -1, -1))
                        chains.append(dict(q=q, d=d, S=S, rS=rS, Sb=Sb, rSb=rSb, pre=pre, rpre=rpre, egl=egl, regl=regl, order=order, pos=0, cur=0, nown=nown,
                                           porder=porder, nm=nm, nch=nch))

                stB0 = ExitStack()
                ut8w = sbt(stB0, "ut8w", [128, 2, 512]); negw = sbt(stB0, "negw", [128, 2, 512]); r_cw = Res()
                ut8wb = sbt(stB0, "ut8wb", [128, 2, 512], BF16); negwb = sbt(stB0, "negwb", [128, 2, 512], BF16)
                idw4 = sbt(stB0, "idw4", [128, 4, 64], BF16)
                for d in range(2):
                    for e_ in range(2):
                        P.dma(SP, ut8w[e_ * 64:(e_ + 1) * 64, d, :], c_ut8[d], writes=[r_cw])
                        P.dma(SP, negw[e_ * 64:(e_ + 1) * 64, d, :], c_neg[d], writes=[r_cw])
                I(DVE, "tensor_copy", [r_cw], [r_cw], out=ut8wb[:], in_=ut8w[:])
                I(DVE, "tensor_copy", [r_cw], [r_cw], out=negwb[:], in_=negw[:])
                I(DVE, "tensor_copy", [r_identb], [r_cw], out=idw4[0:64], in_=identb[0:64, 0:64].unsqueeze(1).broadcast_to([64, 4, 64]))
                I(DVE, "tensor_copy", [r_identb], [r_cw], out=idw4[64:128], in_=identb[64:128, 64:128].unsqueeze(1).broadcast_to([64, 4, 64]))
                for ch in chains:
                    q = ch["q"]; d = ch["d"]; nm = ch["nm"]; nch = ch["nch"]
                    Gt2, rG2 = G128[q["i"]]
                    npair = nch // 2
                    n2 = npair * 4
                    pw_ = sbt(stB0, "prw" + nm, [128, npair, 12]); rpw_ = Res()
                    g2c = sbt(stB0, "g2c" + nm, [128, n2]); rg2c = Res()
                    I(DVE, "tensor_copy", [rG2], [rpw_], out=pw_[:, :, 0:4], in_=Gt2[:, :, d * 4:d * 4 + 4])
                    I(DVE, "tensor_copy", [rG2], [rg2c], out=g2c[:].rearrange("p (c h) -> p c h", h=4), in_=Gt2[:, :, d * 4:d * 4 + 4])
                    ps3, rps3 = psA.get()
                    I(PE, "matmul", [r_cw, rg2c], [rps3], ps3[0:64, 0:n2], lhsT=ut8w[0:64, d, 0:64], rhs=g2c[0:64, :], start=True, stop=True)
                    I(PE, "matmul", [r_cw, rg2c], [rps3], ps3[64:128, 0:n2], lhsT=ut8w[64:128, d, 0:64], rhs=g2c[64:128, :], start=True, stop=True, tile_position=(64, 64))
                    gcw = ps3[:, 0:n2].rearrange("p (c h) -> p c h", h=4)
                    b2w = pw_[:, :, 4:12].rearrange("p c (h a) -> p c h a", a=2)
                    I(DVE, "tensor_tensor", [rps3, rG2], [rpw_], out=b2w[:, :, :, 0], in0=gcw, in1=Gt2[:, :, 16 + d * 4:20 + d * 4], op=ALU.add)
                    I(DVE, "tensor_tensor", [rps3, rG2], [rpw_], out=b2w[:, :, :, 1], in0=gcw, in1=Gt2[:, :, 16 + d * 4:20 + d * 4], op=ALU.add)

                    ch["pw"] = pw_; ch["rpw"] = rpw_
                qkp_r = ring(stB0, "qkp", [128, 8, 128], BF16, 4)
                gu_r = ring(stB0, "gu", [128, 512], BF16, 4)
                dd_r = ring(stB0, "dd", [128, 512], F32, 4)
                ee_r = ring(stB0, "ee", [128, 512], F32, 4)
                pc_r = ring(stB0, "pc", [128, 4, 2, 64], BF16, 8)
                sc_r = ring(stB0, "sc", [128, 4, 64], BF16, 8)
                stg_r = ring(stB0, "stg", [128, 512], BF16, 4)

                def prep_step(ch, m):
                    q = ch["q"]; d = ch["d"]
                    own = (2 * m) < ch["nown"]
                    pw_, rpw_ = ch["pw"], ch["rpw"]
                    t0 = m * 128
                    HV = ((0, None), (64, (64, 64)))
                    qk, rqk = qkp_r.get()
                    if own:
                        P.dma(SP, qk[:], q["QK"][:, t0:t0 + 128].rearrange("(a p) t -> p a t", p=128), writes=[rqk])
                    else:
                        P.dma(SP, qk[:].rearrange("p (h two) t -> p h two t", two=2)[:, :, 0, :],
                              q["QK"][:, t0:t0 + 128].rearrange("(h two p) t -> p h two t", two=2, p=128)[:, :, 0, :], writes=[rqk])
                    stg, rstg = stg_r.get()
                    gu, rgu = gu_r.get()
                    I(DVE, "tensor_tensor", [r_cw, rpw_], [rgu], out=gu[:].rearrange("p (h x) -> p h x", h=4), in0=ut8wb[:, d, :].rearrange("p (h x) -> p h x", h=4),
                      in1=pw_[:, m, 0:4].unsqueeze(2).broadcast_to([128, 4, 128]), op=ALU.mult)
                    dps, rdps = psA.get()
                    for b_, tp in HV:
                        kw = {} if tp is None else {"tile_position": tp}
                        I(PE, "matmul", [r_onesb, rgu], [rdps], dps[b_:b_ + 64, :], lhsT=onesb[b_:b_ + 64, 0:64], rhs=gu[b_:b_ + 64, :], start=True, stop=False, **kw)
                        I(PE, "matmul", [r_identb, r_cw], [rdps], dps[b_:b_ + 64, :], lhsT=identb[b_:b_ + 64, b_:b_ + 64], rhs=negwb[b_:b_ + 64, d, :], start=False, stop=True, **kw)
                    dd, rdd = dd_r.get()
                    I(DVE, "tensor_tensor", [rdps, rpw_], [rdd], out=dd[:].rearrange("p (a x) -> p a x", a=8), in0=dps[:, :].rearrange("p (a x) -> p a x", a=8),
                      in1=pw_[:, m, 4:12].unsqueeze(2).broadcast_to([128, 8, 64]), op=ALU.subtract)
                    ee, ree = ee_r.get()
                    I(ACT, "activation", [rdd], [ree], out=ee[:], in_=dd[:], func=AF.Exp)
                    kps, rkps = psA.get()
                    for e_, (b_, tp) in enumerate(HV):
                        kw = {} if tp is None else {"tile_position": (0, 64)}
                        for h in range(4):
                            if own:
                                I(PE, "matmul", [rqk], [rkps], kps[b_:b_ + 64, h * 128:(h + 1) * 128], lhsT=qk[:, 2 * h, b_:b_ + 64], rhs=qk[:, 2 * h:2 * h + 2, b_:b_ + 64],
                                  start=True, stop=True, **kw)
                            else:
                                I(PE, "matmul", [rqk], [rkps], kps[b_:b_ + 64, h * 128:h * 128 + 64], lhsT=qk[:, 2 * h, b_:b_ + 64], rhs=qk[:, 2 * h, b_:b_ + 64], start=True, stop=True, **kw)
                    pc, rpc = pc_r.get()
                    kv4 = kps[:, :].rearrange("p (h a x) -> p h a x", h=4, a=2)
                    ev4 = ee[:].rearrange("p (h a x) -> p h a x", h=4, a=2)
                    I(DVE, "tensor_tensor", [rkps, ree], [rpc], out=pc[:, :, 0, :], in0=kv4[:, :, 0, :], in1=ev4[:, :, 0, :], op=ALU.mult)
                    if own:
                        I(DVE, "tensor_tensor", [rkps, ree], [rstg], out=stg[:, 256:512].rearrange("p (h x) -> p h x", h=4), in0=kv4[:, :, 1, :], in1=ev4[:, :, 1, :], op=ALU.mult)
                    pt, rpt = psT.get()
                    for b_, tp in HV:
                        kw = {} if tp is None else {"tile_position": tp}
                        for h in range(4):
                            I(PE, "transpose", [rpc, r_identb], [rpt], pt[b_:b_ + 64, h * 64:(h + 1) * 64], pc[b_:b_ + 64, h, 0, :], identb[b_:b_ + 64, b_:b_ + 64], **kw)
                    I(ACT, "copy", [rpt], [rpc], out=pc[:, :, 1, :], in_=pt[:, 0:256].rearrange("p (h x) -> p h x", h=4))
                    sc, rsc = sc_r.get()
                    I(DVE, "tensor_tensor", [rpc, r_cw], [rsc], out=sc[:], in0=idw4[:], in1=pc[:, :, 0, :], op=ALU.subtract)
                    yield
                    for lvl in range(5):
                        xps, rxps = psA.get()
                        for b_, tp in HV:
                            kw = {} if tp is None else {"tile_position": tp}
                            for h in range(4):
                                if lvl < 4:
                                    I(PE, "matmul", [rpc], [rxps], xps[b_:b_ + 64, h * 128:h * 128 + 64], lhsT=pc[b_:b_ + 64, h, 1, :], rhs=pc[b_:b_ + 64, h, 0, :], start=True, stop=True, **kw)
                                I(PE, "matmul", [rpc], [rxps], xps[b_:b_ + 64, h * 128 + 64:h * 128 + 128], lhsT=pc[b_:b_ + 64, h, 0, :], rhs=pc[b_:b_ + 64, h, 1, :], start=True, stop=True, **kw)
                        pcn, rpcn = pc_r.get()
                        if lvl < 4:
                            I(ACT, "copy", [rxps], [rpcn], out=pcn[:].rearrange("p h a x -> p (h a x)"), in_=xps[:, :])
                        else:
                            I(ACT, "copy", [rxps], [rpcn], out=pcn[:, :, 1, :], in_=xps[:, :].rearrange("p (h a x) -> p h a x", h=4, a=2)[:, :, 1, :])
                        pc, rpc = pcn, rpcn
                        yps, ryps = psA.get()
                        for b_, tp in HV:
                            kw = {} if tp is None else {"tile_position": tp}
                            for h in range(4):
                                I(PE, "matmul", [rpc, rsc], [ryps], yps[b_:b_ + 64, h * 64:(h + 1) * 64], lhsT=pc[b_:b_ + 64, h, 1, :], rhs=sc[b_:b_ + 64, h, :], start=True, stop=True, **kw)
                        if lvl == 4:
                            I(DVE, "tensor_tensor", [ryps, rsc], [rstg], out=stg[:, 0:256].rearrange("p (h x) -> p h x", h=4), in0=yps[:, 0:256].rearrange("p (h x) -> p h x", h=4), in1=sc[:], op=ALU.add)
                        else:
                            scn, rscn = sc_r.get()
                            I(DVE, "tensor_tensor", [ryps, rsc], [rscn], out=scn[:], in0=yps[:, 0:256].rearrange("p (h x) -> p h x", h=4), in1=sc[:], op=ALU.add)
                            sc, rsc = scn, rscn
                        yield
                    dst = q["S6Q"][d, 2 * m:2 * m + 2].rearrange("e p c -> (e p) c")
                    if own:
                        P.dma(SP, dst, stg[:], reads=[rstg])
                    else:
                        P.dma(SP, dst[:, 0:256], stg[:, 0:256], reads=[rstg])

                def rec_step(ch):
                    q = ch["q"]; d = ch["d"]; n = ch["order"][ch["pos"]]
                    own = n < ch["nown"]
                    Gt, rG = G[q["i"]]
                    pre, rpre = ch["pre"], ch["rpre"]
                    t0 = n * 64
                    qk, rqk = qk_r.get()
                    kv, rkv = kv_r.get()
                    s6q, rs6q = s6q_r.get()
                    if own:
                        P.dma(SP, qk[:], q["QK"][:, t0:t0 + 64].rearrange("(a p) t -> p a t", p=128), writes=[rqk])
                        P.dma(SP, s6q[:], q["S6Q"][d, n], writes=[rs6q])
                    else:
                        P.dma(SP, qk[:].rearrange("p (h two) t -> p h two t", two=2)[:, :, 0, :],
                              q["QK"][:, t0:t0 + 64].rearrange("(h two p) t -> p h two t", two=2, p=128)[:, :, 0, :], writes=[rqk])
                        P.dma(SP, s6q[:, 0:256], q["S6Q"][d, n, :, 0:256], writes=[rs6q])
                    P.dma(SP, kv[:], q["KV"][t0:t0 + 64, :], writes=[rkv])
                    sc = s6q[:, 0:256].rearrange("p (h x) -> p h x", h=4); rsc = rs6q
                    qm = s6q[:, 256:512].rearrange("p (h x) -> p h x", h=4); rqm = rs6q
                    S_old, rS_old = ch["S"][ch["cur"]], ch["rS"][ch["cur"]]
                    S_new, rS_new = ch["S"][1 - ch["cur"]], ch["rS"][1 - ch["cur"]]
                    Sb, rSb = ch["Sb"], ch["rSb"]
                    ksp, rksp = psR.get()
                    for h in range(4):
                        I(PE, "matmul", [rqk, rSb], [rksp], ksp[0:64, h * 128:(h + 1) * 128], lhsT=qk[:, 2 * h, :], rhs=Sb[:, h, :], start=True, stop=True)
                    if own:
                        qsp, rqsp = psR.get()
                        for h in range(4):
                            I(PE, "matmul", [rqk, rSb], [rqsp], qsp[0:64, h * 128:(h + 1) * 128], lhsT=qk[:, 2 * h + 1, :], rhs=Sb[:, h, :], start=True, stop=True)
                    yield
                    rr, rrr = rr_r.get()
                    for h in range(4):
                        I(DVE, "scalar_tensor_tensor", [rksp, rpre, rkv], [rrr], out=rr[:, h, :], in0=ksp[0:64, h * 128:(h + 1) * 128], scalar=pre[:, n, 16 + h:17 + h],
                                                                       in1=kv[:, 512 + h * 128:512 + (h + 1) * 128], op0=ALU.mult, op1=ALU.add)
                    yield
                    trp, rtrp = psR.get()
                    for h in range(4):
                        I(PE, "matmul", [rsc, rrr], [rtrp], trp[0:64, h * 128:(h + 1) * 128], lhsT=sc[:, h, :], rhs=rr[:, h, :], start=True, stop=True)
                    yield
                    vn, rvn = vn_r.get()
                    I(ACT, "copy", [rtrp], [rvn], out=vn[:].rearrange("p h x -> p (h x)"), in_=trp[0:64, :])
                    kd, rkd = kd_r.get()
                    I(DVE, "tensor_tensor", [rkv, rpre], [rkd], out=kd[:], in0=kv[:, 0:512].rearrange("p (h x) -> p h x", h=4),
                                                         in1=pre[:, n, 24:28].unsqueeze(2).broadcast_to([64, 4, 128]), op=ALU.mult)
                    yield
                    if own:
                        oa, roa = oa_r.get()
                        for h in range(4):
                            I(ACT, "activation", [rqsp, rpre], [roa], out=oa[:, h, :], in_=qsp[0:64, h * 128:(h + 1) * 128], func=AF.Copy, scale=pre[:, n, 20 + h:21 + h])
                        obp, robp = psR.get()
                        for h in range(4):
                            I(PE, "matmul", [rqm, rvn], [robp], obp[0:64, h * 128:(h + 1) * 128], lhsT=qm[:, h, :], rhs=vn[:, h, :], start=True, stop=True)
                        oo, roo = oo_r.get()
                        I(DVE, "tensor_tensor", [robp, roa], [roo], out=oo[:], in0=obp[0:64, :], in1=oa[:].rearrange("p h x -> p (h x)"), op=ALU.add)
                        P.dma(SP, q["OA"][d, t0:t0 + 64, :], oo[:], reads=[roo])
                    sup, rsup = psR.get()
                    for h in range(4):
                        I(PE, "matmul", [rkd, rvn], [rsup], sup[:, h * 128:(h + 1) * 128], lhsT=kd[:, h, :], rhs=vn[:, h, :], start=True, stop=True)
                    yield
                    egl = ch["egl"]
                    for h in range(4):
                        I(DVE, "scalar_tensor_tensor", [rS_old, ch["regl"], rsup], [rS_new], out=S_new[:, h, :], in0=S_old[:, h, :], scalar=egl[:, n, h:h + 1], in1=sup[:, h * 128:(h + 1) * 128],
                                                                       op0=ALU.mult, op1=ALU.add)
                    I(ACT, "copy", [rS_new], [rSb], out=Sb[:], in_=S_new[:])
                    ch["cur"] = 1 - ch["cur"]
                    ch["pos"] += 1
                    if ch["pos"] == len(ch["order"]) and q["i"] > 0:
                        P.dma(SP, nsd[q["i"] - 1, d].rearrange("(h k) v -> k h v", k=128), S_new[:], reads=[rS_new])

                def lockstep(gens):
                    gens = list(gens)
                    while gens:
                        for g_ in list(gens):
                            try:
                                next(g_)
                            except StopIteration:
                                gens.remove(g_)

                KLOCK = 3
                tasks = []
                ppos = {id(ch): 0 for ch in chains}
                active = list(chains)
                while active:
                    for ch in list(active):
                        tasks.append((ch, ch["porder"][ppos[id(ch)]]))
                        ppos[id(ch)] += 1
                        if ppos[id(ch)] == len(ch["porder"]):
                            active.remove(ch)
                for i in range(0, len(tasks), KLOCK):
                    lockstep([prep_step(ch, n) for ch, n in tasks[i:i + KLOCK]])
                P.flush()
                stB0.close()
                kv_r = ring(st, "kvc", [64, 1024], BF16, 5)
                rr_r = ring(st, "rr", [64, 4, 128], BF16, 4)
                vn_r = ring(st, "vn", [64, 4, 128], BF16, 4)
                kd_r = ring(st, "kd", [64, 4, 128], BF16, 4)
                oa_r = ring(st, "oa", [64, 4, 128], F32, 4)
                oo_r = ring(st, "oo", [64, 512], F32, 4)
                s6q_r = ring(st, "s6q", [64, 512], BF16, 5)
                dmt = sbt(st, "dmt", [128, 2, 512]); xi = sbt(st, "xi", [64, 2, 512]); zeta = sbt(st, "zeta", [128, 8]); gch = sbt(st, "gch", [64, 8]); r_cst2 = Res()
                for d in range(2):
                    P.dma(SP, dmt[:, d, :], c_dmt[d], writes=[r_cst2])
                    P.dma(SP, xi[:, d, :], c_xi[d], writes=[r_cst2])
                P.dma(SP, zeta[:], c_zeta, writes=[r_cst2])
                P.dma(SP, gch[:], c_gch, writes=[r_cst2])
                rq_r = ring(st, "rqc", [64, 8, 128], BF16, 3)
                rk_r = ring(st, "rkc", [128, 256], BF16, 3)
                vb_r = ring(st, "vbc", [128, 512], BF16, 3)
                sm_r = ring(st, "smc", [128, 4, 128], BF16, 2)
                qx_r = ring(st, "qxc", [64, 4, 128], BF16, 2)
                kz_r = ring(st, "kzc", [128, 4, 64], BF16, 2)
                or_r = ring(st, "orc", [128, 512], F32, 2)
                rchains2 = []
                for q in seqs:
                    nch = q["T"] // 128
                    nown = q["own"] // 128
                    for d in range(2):
                        nm = "%s%d" % (q["name"], d)
                        S = [sbt(st, "R%s_%d" % (nm, k), [64, 4, 128]) for k in range(2)]
                        rS = [Res(), Res()]
                        Sb = sbt(st, "Rb" + nm, [64, 4, 128], BF16); rSb = Res()
                        if q["i"] == 0:
                            P.dma(SP, S[0][:], sr0[d].rearrange("(h k) v -> k h v", k=64), writes=[rS[0]])
                        else:
                            I(POOL, "memset", [], [rS[0]], S[0][:], 0.0)
                        I(ACT, "copy", [rS[0]], [rSb], out=Sb[:], in_=S[0][:])
                        order = list(range(nown)) if d == 0 else list(range(nch - 1, -1, -1))
                        rchains2.append(dict(q=q, d=d, S=S, rS=rS, Sb=Sb, rSb=rSb, order=order, pos=0, cur=0, nown=nown))

                def ret_step(ch):
                    q = ch["q"]; d = ch["d"]; n = ch["order"][ch["pos"]]
                    own = n < ch["nown"]
                    t0 = n * 128
                    rq, rrq = rq_r.get(); rk, rrk = rk_r.get(); vb, rvb = vb_r.get()
                    if own:
                        P.dma(SP, rq[:], q["RQK"][:, t0:t0 + 128].rearrange("(a p) t -> p a t", p=64), writes=[rrq])
                    P.dma(SP, rk[:], q["RK"][t0:t0 + 128, :], writes=[rrk])
                    P.dma(SP, vb[:], q["TM"][t0:t0 + 128, 512:1024], writes=[rvb])
                    S_old, rS_old = ch["S"][ch["cur"]], ch["rS"][ch["cur"]]
                    S_new, rS_new = ch["S"][1 - ch["cur"]], ch["rS"][1 - ch["cur"]]
                    Sb, rSb = ch["Sb"], ch["rSb"]
                    if own:
                        scp, rscp = psA.get()
                        for h in range(4):
                            I(PE, "matmul", [rrq], [rscp], scp[:, h * 128:(h + 1) * 128], lhsT=rq[:, 4 + h, :], rhs=rq[:, h, :], start=True, stop=True)
                        sm, rsm = sm_r.get()
                        I(DVE, "tensor_tensor", [rscp, r_cst2], [rsm], out=sm[:].rearrange("p h x -> p (h x)"), in0=scp[:], in1=dmt[:, d, :], op=ALU.mult)
                        qx, rqx = qx_r.get()
                        I(DVE, "tensor_tensor", [rrq, r_cst2], [rqx], out=qx[:].rearrange("p h x -> p (h x)"), in0=rq[:, 0:4, :].rearrange("p h x -> p (h x)"), in1=xi[:, d, :], op=ALU.mult)
                        orp, rorp = psA.get()
                        for h in range(4):
                            I(PE, "matmul", [rsm, rvb], [rorp], orp[:, h * 128:(h + 1) * 128], lhsT=sm[:, h, :], rhs=vb[:, h * 128:(h + 1) * 128], start=True, stop=False)
                            I(PE, "matmul", [rqx, rSb], [rorp], orp[:, h * 128:(h + 1) * 128], lhsT=qx[:, h, :], rhs=Sb[:, h, :], start=False, stop=True)
                        oc, roc = or_r.get()
                        I(ACT, "copy", [rorp], [roc], out=oc[:], in_=orp[:])
                        P.dma(SP, q["OR"][d, t0:t0 + 128, :], oc[:], reads=[roc])
                    kz, rkz = kz_r.get()
                    I(DVE, "tensor_tensor", [rrk, r_cst2], [rkz], out=kz[:], in0=rk[:].rearrange("p (h x) -> p h x", h=4), in1=zeta[:, d * 4:d * 4 + 4].unsqueeze(2).broadcast_to([128, 4, 64]), op=ALU.mult)
                    dsp, rdsp = psA.get()
                    for h in range(4):
                        I(PE, "matmul", [rkz, rvb], [rdsp], dsp[0:64, h * 128:(h + 1) * 128], lhsT=kz[:, h, :], rhs=vb[:, h * 128:(h + 1) * 128], start=True, stop=True)
                    for h in range(4):
                        I(DVE, "scalar_tensor_tensor", [rS_old, r_cst2, rdsp], [rS_new], out=S_new[:, h, :], in0=S_old[:, h, :], scalar=gch[:, d * 4 + h:d * 4 + h + 1], in1=dsp[0:64, h * 128:(h + 1) * 128],
                                                                       op0=ALU.mult, op1=ALU.add)
                    I(ACT, "copy", [rS_new], [rSb], out=Sb[:], in_=S_new[:])
                    ch["cur"] = 1 - ch["cur"]
                    ch["pos"] += 1
                    if ch["pos"] == len(ch["order"]) and q["i"] > 0:
                        P.dma(SP, nsr[q["i"] - 1, d].rearrange("(h k) v -> k h v", k=64), S_new[:], reads=[rS_new])


                rchains = sorted(chains, key=lambda c: -len(c["order"]))
                rchains2 = sorted(rchains2, key=lambda c: -len(c["order"]))
                rnd = 0
                while True:
                    act = [ch for ch in rchains if ch["pos"] < len(ch["order"])][:KLOCK]
                    act2 = [ch for ch in rchains2 if ch["pos"] < len(ch["order"])][:2]
                    if not act and not act2:
                        break
                    if act:
                        lockstep([rec_step(ch) for ch in act])
                    if act2 and (rnd % 2 == 1 or not act):
                        for ch in act2:
                            ret_step(ch)
                    rnd += 1
                P.flush()
            stG.close()

            w1_sb = sbt(st2, "w1_sb", [128, 8, DFF], BF16); r_w1 = [Res() for _ in range(8)]
            for kc in range(8):
                P.dma(POOL, w1_sb[:, kc, :], w_ff1[kc * 128:(kc + 1) * 128, :], writes=[r_w1[kc]])

            def ln_stats(xt, rx, mv, rmv, stt, rst, eps):
                for c2 in range(2):
                    I(DVE, "bn_stats", [rx], [rst], out=stt[:, c2, :], in_=xt[:, c2 * 512:(c2 + 1) * 512])
                I(DVE, "bn_aggr", [rst], [rmv], out=mv[:, 0:2], in_=stt[:])
                I(ACT, "activation", [rmv], [rmv], out=mv[:, 2:3], in_=mv[:, 1:2], func=AF.Ln, bias=eps)
                I(ACT, "activation", [rmv], [rmv], out=mv[:, 2:3], in_=mv[:, 2:3], func=AF.Exp, scale=-0.5)

            def bcast_row(stack, name, src, n):
                t = sbt(stack, name, [128, n]); r = Res()
                P.dma(SP, t[:], src.partition_broadcast(128), writes=[r])
                return t, r

            with ExitStack() as st:
                psY = ring(st, "psE_", [128, 1024], F32, 2, psum=True)
                psT = ring(st, "psET_", [128, 1024], BF16, 2, psum=True)
                wo_sb = sbt(st, "wo_sb", [128, 8, D], BF16); r_wo = [Res() for _ in range(8)]
                for kc in range(8):
                    P.dma(POOL, wo_sb[:, kc, :], w_o[kc * 128:(kc + 1) * 128, :], writes=[r_wo[kc]])
                l1w, r_l1w = bcast_row(st, "l1w", ln1_w, D)
                l1b, r_l1b = bcast_row(st, "l1b", ln1_b, D)
                naw, r_naw = bcast_row(st, "naw", norm_a_w, 128)
                gnw, r_gnw = bcast_row(st, "gnw", gn_w, 512)
                gnb, r_gnb = bcast_row(st, "gnb", gn_b, 512)
                o0_r = ring(st, "o0", [128, 512], F32, 5)
                o1_r = ring(st, "o1", [128, 512], F32, 5)
                jk_r = ring(st, "jk", [128, 512], F32, 3)
                tm_r = ring(st, "tmc", [128, 1536], BF16, 3)
                ss_r = ring(st, "ss", [128, 8], F32, 2)
                bs_r = ring(st, "bs", [128, 4, 6], F32, 2)
                bm_r = ring(st, "bm", [128, 4, 3], F32, 2)
                mix_r = ring(st, "mix", [128, D], BF16, 3)
                mT_r = ring(st, "mT", [128, 8, 128], BF16, 3)
                xt_r = ring(st, "xtc", [128, D], F32, 3)
                tt_r = ring(st, "ttc", [128, D], F32, 3)
                st_r = ring(st, "bstc", [128, 2, 6], F32, 2)
                mv_r = ring(st, "bmvc", [128, 4], F32, 2)
                for q in seqs:
                    cnd = q["cond"]
                    for ts in range(0, q["own"], 128):
                        tm, rtm = tm_r.get()
                        P.dma(ACT, tm[:], q["TM"][ts:ts + 128, :], writes=[rtm])
                        mix, rmix = mix_r.get()
                        o0, ro0 = o0_r.get(); o1, ro1 = o1_r.get()
                        P.dma(SP, o0[:], q["OA"][0, ts:ts + 128, :], writes=[ro0])
                        P.dma(SP, o1[:], q["OA"][1, ts:ts + 128, :], writes=[ro1])
                        I(POOL, "tensor_tensor", [ro0, ro1], [ro0], out=o0[:], in0=o0[:], in1=o1[:], op=ALU.add)
                        ss, rss = ss_r.get(); jk, rjk = jk_r.get()
                        I(POOL, "memset", [], [rss], ss[:], 0.0)
                        for h in range(4):
                            I(ACT, "activation", [ro0], [rjk, rss], out=jk[:, h * 128:(h + 1) * 128], in_=o0[:, h * 128:(h + 1) * 128], func=AF.Square, accum_out=ss[:, h:h + 1])
                        I(ACT, "activation", [rss], [rss], out=ss[:, 4:8], in_=ss[:, 0:4], func=AF.Ln, scale=1.0 / 128.0, bias=1e-6)
                        I(ACT, "activation", [rss], [rss], out=ss[:, 4:8], in_=ss[:, 4:8], func=AF.Exp, scale=-0.5)
                        I(DVE, "tensor_tensor", [ro0, rss], [ro0], out=o0[:].rearrange("p (h x) -> p h x", h=4), in0=o0[:].rearrange("p (h x) -> p h x", h=4),
                                                                       in1=ss[:, 4:8].unsqueeze(2).broadcast_to([128, 4, 128]), op=ALU.mult)
                        I(POOL, "tensor_tensor", [ro0, r_naw], [ro0], out=o0[:].rearrange("p (h x) -> p h x", h=4), in0=o0[:].rearrange("p (h x) -> p h x", h=4),
                                                                  in1=naw[:].unsqueeze(1).broadcast_to([128, 4, 128]), op=ALU.mult)
                        I(DVE, "tensor_tensor", [ro0, rtm], [rmix], out=mix[:, 0:512], in0=o0[:], in1=tm[:, 0:512], op=ALU.mult)
                        p0, rp0 = o0_r.get(); p1, rp1 = o1_r.get()
                        P.dma(SP, p0[:], q["OR"][0, ts:ts + 128, :], writes=[rp0])
                        P.dma(SP, p1[:], q["OR"][1, ts:ts + 128, :], writes=[rp1])
                        I(POOL, "tensor_tensor", [rp0, rp1], [rp0], out=p0[:], in0=p0[:], in1=p1[:], op=ALU.add)
                        bs, rbs = bs_r.get(); bm, rbm = bm_r.get()
                        for h in range(4):
                            I(DVE, "bn_stats", [rp0], [rbs], out=bs[:, h, :], in_=p0[:, h * 128:(h + 1) * 128])
                        for h in range(4):
                            I(DVE, "bn_aggr", [rbs], [rbm], out=bm[:, h, 0:2], in_=bs[:, h, :])
                        I(ACT, "activation", [rbm], [rbm], out=bm[:, :, 2], in_=bm[:, :, 1], func=AF.Ln, bias=1e-5)
                        I(ACT, "activation", [rbm], [rbm], out=bm[:, :, 2], in_=bm[:, :, 2], func=AF.Exp, scale=-0.5)
                        for h in range(4):
                            I(DVE, "tensor_scalar", [rp0, rbm], [rp0], out=p0[:, h * 128:(h + 1) * 128], in0=p0[:, h * 128:(h + 1) * 128], scalar1=bm[:, h, 0:1], scalar2=bm[:, h, 2:3],
                                                                                  op0=ALU.subtract, op1=ALU.mult)
                        I(POOL, "tensor_tensor", [rp0, r_gnw], [rp0], out=p0[:], in0=p0[:], in1=gnw[:], op=ALU.mult)
                        I(POOL, "tensor_tensor", [rp0, r_gnb], [rp0], out=p0[:], in0=p0[:], in1=gnb[:], op=ALU.add)
                        I(DVE, "tensor_tensor", [rp0, rtm], [rmix], out=mix[:, 512:1024], in0=p0[:], in1=tm[:, 1024:1536], op=ALU.mult)
                        pt, rpt = psT.get()
                        for kc in range(8):
                            I(PE, "transpose", [rmix, r_identb], [rpt], pt[:, kc * 128:(kc + 1) * 128], mix[:, kc * 128:(kc + 1) * 128], identb[:])
                        mT, rmT = mT_r.get()
                        I(ACT, "copy", [rpt], [rmT], out=mT[:].rearrange("p a t -> p (a t)"), in_=pt[:])
                        py, rpy = psY.get()
                        for cg in range(2):
                            for kc in range(8):
                                I(PE, "matmul", [rmT, r_wo[kc]], [rpy], py[:, cg * 512:(cg + 1) * 512], lhsT=mT[:, kc, :], rhs=wo_sb[:, kc, cg * 512:(cg + 1) * 512],
                                                                                  start=(kc == 0), stop=(kc == 7))
                        xt, rx = xt_r.get()
                        P.dma(ACT, xt[:], q["x"][ts:ts + 128, :], writes=[rx])
                        tt, rtt = tt_r.get()
                        I(DVE, "tensor_tensor", [rpy, r_gates], [rtt], out=tt[:], in0=py[:], in1=gates[:, 0, cnd, :], op=ALU.mult)
                        I(DVE, "scalar_tensor_tensor", [rx, rtt], [rtt], out=tt[:], in0=xt[:], scalar=ALPHA, in1=tt[:], op0=ALU.mult, op1=ALU.add)
                        stt, rst = st_r.get(); mv, rmv = mv_r.get()
                        ln_stats(tt, rtt, mv, rmv, stt, rst, 1e-6)
                        I(DVE, "tensor_scalar", [rtt, rmv], [rtt], out=tt[:], in0=tt[:], scalar1=mv[:, 0:1], scalar2=mv[:, 2:3], op0=ALU.subtract, op1=ALU.mult)
                        I(POOL, "tensor_tensor", [rtt, r_l1w], [rtt], out=tt[:], in0=tt[:], in1=l1w[:], op=ALU.mult)
                        I(DVE, "tensor_tensor", [rtt, r_l1b], [rtt], out=tt[:], in0=tt[:], in1=l1b[:], op=ALU.add)
                        P.dma(SP, q["X1"][ts:ts + 128, :], tt[:], reads=[rtt])
                P.flush()

            with ExitStack() as st:
                w2_sb = sbt(st, "w2_sb", [128, 32, D], BF16); r_w2 = [Res() for _ in range(32)]
                for kc in range(32):
                    P.dma(POOL, w2_sb[:, kc, :], w_ff2[kc * 128:(kc + 1) * 128, :], writes=[r_w2[kc]])
                psA = ring(st, "psF_", [128, 512], F32, 3, psum=True)
                psY = ring(st, "psFY_", [128, 1024], F32, 2, psum=True)
                psT = ring(st, "psFT_", [128, 1024], BF16, 1, psum=True)
                l2w, r_l2w = bcast_row(st, "l2w", ln2_w, D)
                l2b, r_l2b = bcast_row(st, "l2b", ln2_b, D)
                bf2, r_bf2 = bcast_row(st, "bf2", b_ff2, D)
                b1r = sbt(st, "b1r", [32, 128]); b1c = sbt(st, "b1c", [128, 32]); r_b1 = Res()
                P.dma(SP, b1r[:], b_ff1.rearrange("(a p) -> a p", p=128), writes=[r_b1])
                ps, rps = psA.get()
                I(PE, "matmul", [r_b1, r_identf], [rps], ps[:, 0:32], lhsT=b1r[:], rhs=identf[0:32, 0:32], start=True, stop=True)
                I(DVE, "tensor_copy", [rps], [r_b1], out=b1c[:], in_=ps[:, 0:32])
                x1_r = ring(st, "x1", [128, D], F32, 4)
                xn_r = ring(st, "xnf", [128, D], BF16, 2)
                h2_r = ring(st, "h2T", [128, 8, 256], BF16, 2)
                aT_r = ring(st, "aT", [128, 8, 256], BF16, 2)
                rl_r = ring(st, "rl", [128, 256], F32, 3)
                tt_r = ring(st, "ttf", [128, D], F32, 2)
                st_r = ring(st, "bstf", [128, 2, 6], F32, 2)
                mv_r = ring(st, "bmvf", [128, 4], F32, 2)
                for q in seqs:
                    cnd = q["cond"]
                    TT = 256
                    for t0 in range(0, q["own"], TT):
                        h2, rh2 = h2_r.get()
                        x1s = []
                        for j in range(2):
                            ts = t0 + j * 128
                            x1, rx1 = x1_r.get()
                            x1s.append((x1, rx1))
                            P.dma(SP, x1[:], q["X1"][ts:ts + 128, :], writes=[rx1])
                            stt, rst = st_r.get(); mv, rmv = mv_r.get()
                            ln_stats(x1, rx1, mv, rmv, stt, rst, 1e-6)
                            xn, rxn = xn_r.get()
                            I(DVE, "tensor_scalar", [rx1, rmv], [rxn], out=xn[:], in0=x1[:], scalar1=mv[:, 0:1], scalar2=mv[:, 2:3], op0=ALU.subtract, op1=ALU.mult)
                            pt, rpt = psT.get()
                            for kc in range(8):
                                I(PE, "transpose", [rxn, r_identb], [rpt], pt[:, kc * 128:(kc + 1) * 128], xn[:, kc * 128:(kc + 1) * 128], identb[:])
                            for kc in range(8):
                                I(ACT, "activation", [rpt, r_modc], [rh2], out=h2[:, kc, j * 128:(j + 1) * 128], in_=pt[:, kc * 128:(kc + 1) * 128], func=AF.Identity,
                                                                                                scale=modc[:, 4, kc, cnd:cnd + 1], bias=modc[:, 3, kc, cnd:cnd + 1])
                        pys = [psY.get() for _ in range(2)]
                        for g in range(4):
                            aT, raT = aT_r.get()
                            for f in range(8):
                                ft = g * 8 + f
                                ps, rps = psA.get()
                                for kc in range(8):
                                    I(PE, "matmul", [r_w1[kc], rh2], [rps], ps[:, 0:TT], lhsT=w1_sb[:, kc, ft * 128:(ft + 1) * 128], rhs=h2[:, kc, :], start=(kc == 0), stop=(kc == 7))
                                rl, rrl = rl_r.get()
                                I(ACT, "activation", [rps, r_b1], [rrl], out=rl[:], in_=ps[:, 0:TT], func=AF.Relu, bias=b1c[:, ft:ft + 1])
                                I(POOL if ft % 2 else DVE, "tensor_tensor", [rrl], [raT], out=aT[:, f, :], in0=rl[:], in1=rl[:], op=ALU.mult)
                            for j in range(2):
                                py, rpy = pys[j]
                                for cg in range(2):
                                    for f in range(8):
                                        ft = g * 8 + f
                                        I(PE, "matmul", [raT, r_w2[ft]], [rpy], py[:, cg * 512:(cg + 1) * 512], lhsT=aT[:, f, j * 128:(j + 1) * 128], rhs=w2_sb[:, ft, cg * 512:(cg + 1) * 512],
                                          start=(ft == 0), stop=(ft == 31))
                        for j in range(2):
                            ts = t0 + j * 128
                            x1, rx1 = x1s[j]
                            py, rpy = pys[j]
                            tt, rtt = tt_r.get()
                            I(DVE, "tensor_tensor", [rpy, r_bf2], [rtt], out=tt[:], in0=py[:], in1=bf2[:], op=ALU.add)
                            I(POOL, "tensor_tensor", [rtt, r_gates], [rtt], out=tt[:], in0=tt[:], in1=gates[:, 1, cnd, :], op=ALU.mult)
                            I(DVE, "scalar_tensor_tensor", [rx1, rtt], [rtt], out=tt[:], in0=x1[:], scalar=ALPHA, in1=tt[:], op0=ALU.mult, op1=ALU.add)
                            stt, rst = st_r.get(); mv, rmv = mv_r.get()
                            ln_stats(tt, rtt, mv, rmv, stt, rst, 1e-6)
                            I(DVE, "tensor_scalar", [rtt, rmv], [rtt], out=tt[:], in0=tt[:], scalar1=mv[:, 0:1], scalar2=mv[:, 2:3], op0=ALU.subtract, op1=ALU.mult)
                            I(POOL, "tensor_tensor", [rtt, r_l2w], [rtt], out=tt[:], in0=tt[:], in1=l2w[:], op=ALU.mult)
                            I(DVE, "tensor_tensor", [rtt, r_l2b], [rx1], out=x1[:], in0=tt[:], in1=l2b[:], op=ALU.add)
                            P.dma(SP, q["y"][ts:ts + 128, :], x1[:], reads=[rx1])
                P.flush()
    return nc


def _consts(odd):
    k = np.arange(64)
    ut = np.zeros((2, 64, 64), np.float32)
    ut[0] = (k[:, None] <= k[None, :])
    ut[1] = (k[:, None] >= k[None, :])
    ut8 = np.broadcast_to(ut[:, :, None, None, :], (2, 64, 4, 2, 64)).reshape(2, 64, 512)
    neg = np.zeros((2, 64, 2, 64), np.float32)
    j = k[:, None]; i = k[None, :]
    neg[0, :, 0] = np.where(i > j, 0.0, NEGBIG); neg[0, :, 1] = np.where(i >= j, 0.0, NEGBIG)
    neg[1, :, 0] = np.where(i < j, 0.0, NEGBIG); neg[1, :, 1] = np.where(i <= j, 0.0, NEGBIG)
    neg8 = np.broadcast_to(neg[:, :, None, :, :], (2, 64, 4, 2, 64)).reshape(2, 64, 512)
    lg = np.log1p(-np.exp2(-5.0 - np.arange(4, dtype=np.float64)))
    C = 128
    p = np.arange(C, dtype=np.float64)
    dmt = np.zeros((2, C, 4, C)); xi = np.zeros((2, 4, C)); zeta = np.zeros((C, 8)); gch = np.zeros((8,))
    for d in range(2):
        td = d ^ odd
        lgd = lg if td == 0 else lg[::-1]
        for h in range(4):
            g = lgd[h]
            jj = p[:, None]; ii = p[None, :]
            if d == 0:
                dmt[d, :, h, :] = np.where(ii >= jj, np.exp(g * np.maximum(ii - jj, 0)), 0.0)
                xi[d, h] = np.exp(g * (p + 1)); zeta[:, d * 4 + h] = np.exp(g * (C - 1 - p))
            else:
                dmt[d, :, h, :] = np.where(ii <= jj, np.exp(g * np.maximum(jj - ii, 0)), 0.0)
                xi[d, h] = np.exp(g * (C - p)); zeta[:, d * 4 + h] = np.exp(g * p)
            gch[d * 4 + h] = np.exp(g * C)
    dmt *= 0.125; xi *= 0.125
    xi64 = np.broadcast_to(xi.reshape(2, 1, 512), (2, 64, 512))
    gch64 = np.broadcast_to(gch[None, :], (64, 8))
    r = np.repeat(np.arange(64, dtype=np.float32), 64); col = np.tile(np.arange(64, dtype=np.float32), 64)
    inv = (np.float32(10000.0) ** (-np.arange(16, dtype=np.float32) / np.float32(16))).astype(np.float32)
    ang = np.concatenate([r[:, None] * inv, col[:, None] * inv], -1).astype(np.float32)
    cos, sin = np.cos(ang), np.sin(ang)
    if odd:
        cos, sin = cos[::-1], sin[::-1]
    f = lambda a: np.ascontiguousarray(a, dtype=np.float32)
    return dict(c_ut8=f(ut8), c_neg=f(neg8), c_dmt=f(dmt.reshape(2, C, 512)), c_xi=f(xi64), c_zeta=f(zeta), c_gch=f(gch64),
                rope_cos=f(cos), rope_sin=f(sin))


_NC_CACHE = {}


def kernel(x_prompt, x_sample, c, state_delta, state_ret, c_ctx, w_mod, b_mod, w_in, conv_w, a_log, dt_bias,
           norm_a_w, gn_w, gn_b, w_o, ln1_w, ln1_b, w_ff1, b_ff1, w_ff2, b_ff2, ln2_w, ln2_b):
    f = lambda a: np.ascontiguousarray(np.asarray(a), dtype=np.float32)
    x_prompt, x_sample, c, state_delta, state_ret, c_ctx = map(f, (x_prompt, x_sample, c, state_delta, state_ret, c_ctx))
    w_in0 = f(w_in)[0]
    perm = np.arange(DIN)
    perm[2048:2052], perm[2052:2056] = np.arange(2052, 2056), np.arange(2048, 2052)
    perm[2056:2060], perm[2060:2064] = np.arange(2060, 2064), np.arange(2056, 2060)
    common = dict(w_mod=f(w_mod)[0], b_mod=f(b_mod)[0], norm_a_w=f(norm_a_w)[0], gn_w=f(gn_w)[0], gn_b=f(gn_b)[0], w_o=f(w_o)[0],
                  ln1_w=f(ln1_w)[0], ln1_b=f(ln1_b)[0], w_ff1=f(w_ff1)[0], b_ff1=f(b_ff1)[0], w_ff2=f(w_ff2)[0], b_ff2=f(b_ff2)[0],
                  ln2_w=f(ln2_w)[0], ln2_b=f(ln2_b)[0])
    per_par = []
    for odd in range(2):
        dd = dict(common)
        dd.update(_consts(odd))
        dd["w_in"] = f(w_in0[:, perm]) if odd else w_in0
        dd["conv_w"] = f(f(conv_w)[0][::-1]) if odd else f(conv_w)[0]
        dd["a_log"] = f(f(a_log)[0][::-1] if odd else f(a_log)[0]).reshape(8)
        dd["dt_bias"] = f(f(dt_bias)[0][::-1] if odd else f(dt_bias)[0]).reshape(8)
        per_par.append(dd)
    in_maps = []
    for core in range(8):
        s, odd = core // 2, core % 2
        m = dict(per_par[odd])
        xs_ = x_sample[s]
        xp_ = x_prompt[2 * core:2 * core + 2]
        sd = state_delta[s, 0]
        sr = state_ret[s, 0]
        if odd:
            xs_ = xs_[::-1]; xp_ = xp_[:, ::-1]; sd = sd[::-1]; sr = sr[::-1]
        m["xs"] = f(xs_)
        m["xp"] = f(xp_).reshape(2 * TP, D)
        m["cond"] = f(np.stack([c[s], c_ctx]))
        m["sd0"] = f(sd).reshape(2, 512, 128)
        m["sr0"] = f(sr).reshape(2, 256, 128)
        in_maps.append(m)
    if "nc" not in _NC_CACHE:
        _NC_CACHE["nc"] = build_program()
    res = run_bass_kernel_spmd(_NC_CACHE["nc"], in_maps, core_ids=list(range(8)))
    y_prompt = np.zeros((16, TP, D), np.float32)
    y_sample = np.zeros((4, TS, D), np.float32)
    new_sd = np.zeros((16, 1, 2, 4, 128, 128), np.float32)
    new_sr = np.zeros((16, 1, 2, 4, 64, 128), np.float32)
    for core in range(8):
        s, odd = core // 2, core % 2
        r = res.results[core]
        ys_ = np.asarray(r["ys"], dtype=np.float32)
        yp_ = np.asarray(r["yp"], dtype=np.float32).reshape(2, TP, D)
        sd_ = np.asarray(r["nsd"], dtype=np.float32).reshape(2, 2, 4, 128, 128)
        sr_ = np.asarray(r["nsr"], dtype=np.float32).reshape(2, 2, 4, 64, 128)
        if odd:
            y_sample[s, OWN:] = ys_[::-1]
            y_prompt[2 * core:2 * core + 2] = yp_[:, ::-1]
            new_sd[2 * core:2 * core + 2, 0] = sd_[:, ::-1]
            new_sr[2 * core:2 * core + 2, 0] = sr_[:, ::-1]
        else:
            y_sample[s, :OWN] = ys_
            y_prompt[2 * core:2 * core + 2] = yp_
            new_sd[2 * core:2 * core + 2, 0] = sd_
            new_sr[2 * core:2 * core + 2, 0] = sr_
    return (y_prompt, y_sample, new_sd, new_sr)
```

```python
import numpy as np
from contextlib import ExitStack
import concourse.bass as bass
import concourse.mybir as mybir
from concourse.bass_utils import run_bass_kernel_spmd

F32 = mybir.dt.float32
BF16 = mybir.dt.bfloat16
AF = mybir.ActivationFunctionType
ALU = mybir.AluOpType

PE, ACT, DVE, POOL, SP = "tensor", "scalar", "vector", "gpsimd", "sync"

D = 1024
TS = 4096
OWN = 2048
TP = 256
DIN = 3600
DFF = 4096
ALPHA = 2.0 ** 0.25
NEGBIG = -30000.0
NOREORDER = set()


class Res:
    __slots__ = ("w", "rs")

    def __init__(self):
        self.w = None
        self.rs = []


class Op:
    __slots__ = ("eng", "fn", "deps", "dma_sem", "token", "signal", "epoch", "cost", "lat", "is_dma", "tag")


class Prog:
    NDMA = 12

    def __init__(self, nc, stack):
        self.nc = nc
        self.ops = []
        self.epoch = 0
        self.esem = {}
        self.ecnt = {}
        for e in (PE, ACT, DVE, POOL):
            self.esem[e] = stack.enter_context(nc.semaphore("s_" + e))
            self.ecnt[e] = 0
        self.dsem = {}
        self.dcnt = {}
        self.dlast = {}
        self.drr = {}
        for q in (SP, ACT, POOL):
            self.dsem[q] = [stack.enter_context(nc.semaphore("d_%s%d" % (q, i))) for i in range(self.NDMA)]
            self.dcnt[q] = [0] * self.NDMA
            self.dlast[q] = [None] * self.NDMA
            self.drr[q] = 0
        self.waited = {e: {} for e in (PE, ACT, DVE, POOL, SP)}
        self.n_inst = 0
        self.reorder = True
        self.sim_total = 0.0

    def op(self, eng, fn, reads=(), writes=(), cost=300.0, lat=None):
        op = Op()
        op.eng = eng
        op.fn = fn
        op.deps = []
        op.dma_sem = None
        op.token = None
        op.signal = False
        op.epoch = self.epoch
        op.cost = cost
        op.lat = cost if lat is None else lat
        op.is_dma = False
        op.tag = ""
        deps = op.deps
        for r in reads:
            if r.w is not None:
                deps.append(r.w)
        for r in writes:
            if r.w is not None:
                deps.append(r.w)
            deps.extend(r.rs)
        for r in reads:
            r.rs.append(op)
        for r in writes:
            r.w = op
            r.rs = []
        self.ops.append(op)
        return op

    def dma(self, q, out, in_, reads=(), writes=(), **kw):
        def fn(e):
            return e.dma_start(out=out, in_=in_, **kw)
        nbytes = 1
        for d_ in out.shape:
            nbytes *= d_
        nbytes *= 2 if out.dtype == BF16 else 4
        op = self.op(q, fn, reads, writes, cost=(400.0 if q == POOL else 80.0), lat=2000.0 + nbytes / 150.0)
        op.tag = "dma:%s<-%s" % (out.name, in_.name)
        op.is_dma = True
        op.signal = True
        return op

    def schedule(self, ops):
        import heapq
        n = len(ops)
        idx = {id(o): i for i, o in enumerate(ops)}
        succs = [[] for _ in range(n)]
        npred = [0] * n
        for i, o in enumerate(ops):
            ps = set()
            for d in o.deps:
                if d.epoch == self.epoch:
                    j = idx[id(d)]
                    if j != i:
                        ps.add(j)
            npred[i] = len(ps)
            for j in ps:
                succs[j].append(i)
        ready_t = [0.0] * n
        engs = (PE, ACT, DVE, POOL, SP)
        pend = {e: [] for e in engs}
        avail = {e: [] for e in engs}
        free_t = {e: 0.0 for e in engs}
        for i in range(n):
            if npred[i] == 0:
                heapq.heappush(avail[ops[i].eng], i)
        order = []
        done = 0
        crit = [-1] * n
        rdy_from = [-1] * n
        st_t = [0.0] * n
        last_on = {e: -1 for e in engs}
        while done < n:
            best = None
            for e in engs:
                pe_, av = pend[e], avail[e]
                while pe_ and pe_[0][0] <= free_t[e]:
                    heapq.heappush(av, heapq.heappop(pe_)[1])
                if av:
                    cand = (free_t[e], 0, av[0], e)
                elif pe_:
                    cand = (pe_[0][0], 1, pe_[0][1], e)
                else:
                    continue
                if best is None or cand < best:
                    best = cand
            start, kind, i, e = best
            if kind == 0:
                heapq.heappop(avail[e])
            else:
                heapq.heappop(pend[e])
            o = ops[i]
            start = max(start, ready_t[i], free_t[e])
            if ready_t[i] >= free_t[e]:
                crit[i] = rdy_from[i]
            else:
                crit[i] = last_on[e]
            last_on[e] = i
            st_t[i] = start
            free_t[e] = start + o.cost
            fin = start + o.lat + 120.0
            order.append(i)
            done += 1
            for j in succs[i]:
                f_ = (start + o.cost) if (e == PE and ops[j].eng == PE) else fin
                if f_ > ready_t[j]:
                    ready_t[j] = f_
                    rdy_from[j] = i
                npred[j] -= 1
                if npred[j] == 0:
                    heapq.heappush(pend[ops[j].eng], (ready_t[j], j))
        self.sim_time = max(free_t.values())
        if getattr(self, "debug_crit", False) and order:
            import collections
            i = order[-1]
            agg = collections.Counter(); cnt_ = collections.Counter()
            prev_t = st_t[i] + ops[i].cost
            while i >= 0:
                key = ops[i].eng[:3] + ":" + ops[i].tag
                agg[key] += prev_t - st_t[i]
                cnt_[key] += 1
                prev_t = st_t[i]
                i = crit[i]
            for k_, v_ in agg.most_common(25):
                print("[crit] %-60s %8.1f us  n=%d" % (k_, v_ / 1e3, cnt_[k_]))
        tot = {e: 0.0 for e in engs}
        cnt = {e: 0 for e in engs}
        for o in ops:
            tot[o.eng] += o.cost
            cnt[o.eng] += 1
        print("[prog]   busy us: " + " ".join("%s=%.0f(%d)" % (e, tot[e] / 1e3, cnt[e]) for e in engs))
        return [ops[i] for i in order]

    def flush(self):
        nc = self.nc
        ops = self.schedule(self.ops) if (self.reorder and self.epoch not in NOREORDER) else self.ops
        ep = self.epoch
        for op in ops:
            if op.is_dma:
                q = op.eng
                i = self.drr[q]
                self.drr[q] = (i + 1) % self.NDMA
                prev = self.dlast[q][i]
                if prev is not None:
                    op.deps.append(prev)
                self.dcnt[q][i] += 16
                op.dma_sem = self.dsem[q][i]
                op.token = (op.dma_sem, self.dcnt[q][i])
                self.dlast[q][i] = op
        for op in ops:
            nd = []
            for d in op.deps:
                if d.epoch != ep:
                    continue
                if d.eng == PE and op.eng == PE and d.dma_sem is None and op.dma_sem is None:
                    continue
                nd.append(d)
                if d.dma_sem is None:
                    d.signal = True
            op.deps = nd
        for op in ops:
            if op.dma_sem is None and op.signal:
                self.ecnt[op.eng] += 1
                op.token = (self.esem[op.eng], self.ecnt[op.eng])
        by_eng = {e: [] for e in (PE, ACT, DVE, POOL, SP)}
        for op in ops:
            by_eng[op.eng].append(op)
        self.n_inst += len(ops)

        def emit(ename):
            lst = by_eng[ename]
            waited = self.waited[ename]

            def body(e):
                for op in lst:
                    for d in op.deps:
                        sem, val = d.token
                        k = id(sem)
                        if waited.get(k, 0) < val:
                            e.wait_ge(sem, val)
                            waited[k] = val
                    inst = op.fn(e)
                    if op.signal:
                        if op.dma_sem is not None:
                            inst.then_inc(op.dma_sem, 16)
                        else:
                            inst.then_inc(self.esem[ename], 1)
                if ename in self.dsem:
                    for i, s in enumerate(self.dsem[ename]):
                        v = self.dcnt[ename][i]
                        if v > 0 and waited.get(id(s), 0) < v:
                            e.wait_ge(s, v)
                            waited[id(s)] = v
            return body

        with nc.Block() as blk:
            for ename in (SP, POOL, ACT, DVE, PE):
                if by_eng[ename] or ename in self.dsem:
                    getattr(blk, ename)(emit(ename))
        self.sim_total += getattr(self, "sim_time", 0.0)
        print("[prog] block %d: %d ops, sim %.0f us" % (self.epoch, len(ops), getattr(self, "sim_time", 0.0) / 1e3))
        self.ops = []
        self.epoch += 1


class Ring:
    def __init__(self, items):
        self.items = items
        self.i = 0

    def get(self):
        t = self.items[self.i % len(self.items)]
        self.i += 1
        return t


def build_program(debug=False):
    nc = bass.Bass("TRN2", target_bir_lowering=False)

    def din(name, shape, dt=F32):
        return nc.dram_tensor(name, list(shape), dt, kind="ExternalInput").ap()

    def dout(name, shape, dt=F32):
        return nc.dram_tensor(name, list(shape), dt, kind="ExternalOutput").ap()

    def dscr(name, shape, dt):
        return nc.dram_tensor(name, list(shape), dt, kind="Internal").ap()

    xs = din("xs", [TS, D])
    xp = din("xp", [2 * TP, D])
    cond = din("cond", [2, D])
    sd0 = din("sd0", [2, 4 * 128, 128])
    sr0 = din("sr0", [2, 4 * 64, 128])
    w_mod = din("w_mod", [D, 6 * D])
    b_mod = din("b_mod", [6 * D])
    w_in = din("w_in", [D, DIN])
    conv_w = din("conv_w", [5, 1536])
    a_log = din("a_log", [8])
    dt_bias = din("dt_bias", [8])
    norm_a_w = din("norm_a_w", [128])
    gn_w = din("gn_w", [512])
    gn_b = din("gn_b", [512])
    w_o = din("w_o", [D, D])
    ln1_w = din("ln1_w", [D])
    ln1_b = din("ln1_b", [D])
    w_ff1 = din("w_ff1", [D, DFF])
    b_ff1 = din("b_ff1", [DFF])
    w_ff2 = din("w_ff2", [DFF, D])
    b_ff2 = din("b_ff2", [D])
    ln2_w = din("ln2_w", [D])
    ln2_b = din("ln2_b", [D])
    rope_cos = din("rope_cos", [TS, 32])
    rope_sin = din("rope_sin", [TS, 32])
    c_ut8 = din("c_ut8", [2, 64, 512])
    c_neg = din("c_neg", [2, 64, 512])
    c_dmt = din("c_dmt", [2, 128, 512])
    c_xi = din("c_xi", [2, 64, 512])
    c_zeta = din("c_zeta", [128, 8])
    c_gch = din("c_gch", [64, 8])

    ys = dout("ys", [OWN, D])
    yp = dout("yp", [2 * TP, D])
    nsd = dout("nsd", [2, 2, 4 * 128, 128])
    nsr = dout("nsr", [2, 2, 4 * 64, 128])

    seqs = []
    for si, (nm, T, own) in enumerate((("s", TS, OWN), ("p0", TP, TP), ("p1", TP, TP))):
        q = dict(i=si, name=nm, T=T, own=own, rope=(si == 0), cond=(0 if si == 0 else 1))
        q["x"] = xs if si == 0 else xp[(si - 1) * TP:si * TP, :]
        q["y"] = ys if si == 0 else yp[(si - 1) * TP:si * TP, :]
        q["PQ"] = dscr("PQ" + nm, [1536, T], BF16)
        q["QK"] = dscr("QK" + nm, [1024, T], BF16)
        q["KV"] = dscr("KV" + nm, [T, 1024], BF16)
        q["TM"] = dscr("TM" + nm, [T, 1536], BF16)
        q["RQK"] = dscr("RQK" + nm, [512, T], BF16)
        q["RK"] = dscr("RK" + nm, [T, 256], BF16)
        q["OA"] = dscr("OA" + nm, [2, own, 512], F32)
        q["OR"] = dscr("OR" + nm, [2, own, 512], F32)
        q["X1"] = dscr("X1" + nm, [own, D], F32)
        q["S6Q"] = dscr("S6Q" + nm, [2, T // 64, 64, 512], BF16)
        q["nch"] = T // 64
        seqs.append(q)

    with ExitStack() as st0:
        P = Prog(nc, st0)

        def I(eng, meth, reads, writes, *a, **k):
            o_ = k.get("out", a[0] if a else None)
            fr = 1
            for d_ in o_.shape[1:]:
                fr *= d_
            if eng == PE:
                l_ = k.get("lhsT", a[1] if len(a) > 1 else None)
                c_ = (max(64, fr) + 8) / 2.4 * (4.0 if (l_ is not None and l_.dtype == F32) else 1.0)
                lat = c_ + 150.0
            elif eng == ACT:
                c_ = (224 + fr) / 1.2
                lat = c_
            elif eng == DVE:
                c_ = (150 + fr) / 0.96
                lat = c_
            else:
                c_ = (150 + 2 * fr) / 1.2
                lat = c_
            op_ = P.op(eng, lambda e: getattr(e, meth)(*a, **k), reads, writes, cost=c_, lat=lat)
            op_.tag = "%s:%s" % (meth, o_.name)

        def sbt(stack, name, shape, dt=F32):
            return stack.enter_context(nc.sbuf_tensor(name, list(shape), dt))

        def pst(stack, name, shape, dt=F32):
            return stack.enter_context(nc.psum_tensor(name, list(shape), dt))

        def ring(stack, name, shape, dt, n, psum=False):
            items = []
            for i in range(n):
                t = (pst if psum else sbt)(stack, "%s%d" % (name, i), shape, dt)
                items.append((t, Res()))
            return Ring(items)

        identf = sbt(st0, "identf", [128, 128]); r_identf = Res()
        identb = sbt(st0, "identb", [128, 128], BF16); r_identb = Res()
        onesb = sbt(st0, "onesb", [128, 128], BF16); r_onesb = Res()
        onesq = sbt(st0, "onesq", [128, 128], BF16); r_onesq = Res()
        onesf = sbt(st0, "onesf", [64, 128]); r_onesf = Res()
        modc = sbt(st0, "modc", [128, 6, 8, 2]); r_modc = Res()
        gates = sbt(st0, "gates", [128, 2, 2, D]); r_gates = Res()
        dtb = sbt(st0, "dtb", [8, 1]); negA = sbt(st0, "negA", [8, 1]); r_gc = Res()
        stG = ExitStack()
        G = []
        for q in seqs:
            G.append((sbt(stG, "G" + q["name"], [64, q["nch"], 24]), Res()))
        G128 = []
        for q in seqs:
            G128.append((sbt(stG, "GG" + q["name"], [128, q["nch"] // 2, 24]), Res()))

        I(POOL, "memset", [], [r_identf], identf[:], 0.0)
        I(POOL, "affine_select", [r_identf], [r_identf], out=identf[:], in_=identf[:], pattern=[[-1, 128]],
                                             compare_op=ALU.not_equal, fill=1.0, base=0, channel_multiplier=1)
        I(DVE, "tensor_copy", [r_identf], [r_identb], out=identb[:], in_=identf[:])
        I(POOL, "memset", [], [r_onesb], onesb[:], 1.0)
        I(POOL, "memset", [], [r_onesq], onesq[:], 128.0)
        I(POOL, "memset", [], [r_onesf], onesf[:], 1.0)
        P.dma(SP, dtb[:], dt_bias.rearrange("(p o) -> p o", o=1), writes=[r_gc])
        P.dma(SP, negA[:], a_log.rearrange("(p o) -> p o", o=1), writes=[r_gc])
        I(ACT, "activation", [r_gc], [r_gc], out=negA[:], in_=negA[:], func=AF.Exp)
        I(DVE, "tensor_scalar", [r_gc], [r_gc], out=negA[:], in0=negA[:], scalar1=-1.0, scalar2=None, op0=ALU.mult)

        stW = ExitStack()
        w_in_sb = sbt(stW, "w_in_sb", [128, 8, DIN], BF16); r_win = [Res() for _ in range(8)]
        for kc in range(8):
            P.dma(POOL, w_in_sb[:, kc, :], w_in[kc * 128:(kc + 1) * 128, :], writes=[r_win[kc]])
        with ExitStack() as st:
            psA = ring(st, "ps0_", [128, 512], F32, 4, psum=True)
            crow = sbt(st, "crow", [16, 128]); r_crow = Res()
            scT = sbt(st, "scT", [128, 16]); r_scT = Res()
            brow = sbt(st, "brow", [48, 128]); r_brow = Res()
            bcol = sbt(st, "bcol", [128, 48]); r_bcol = Res()
            wm = ring(st, "wm", [128, 8, D], F32, 2)
            wm_res = {}
            gbt = ring(st, "gbt", [128, 128], F32, 2)
            P.dma(SP, crow[:], cond.rearrange("c (k p) -> (c k) p", p=128), writes=[r_crow])
            P.dma(SP, brow[:], b_mod.rearrange("(a p) -> a p", p=128), writes=[r_brow])
            ps, rps = psA.get()
            I(PE, "matmul", [r_crow, r_identf], [rps], ps[:, 0:16], lhsT=crow[:], rhs=identf[0:16, 0:16], start=True, stop=True)
            I(ACT, "activation", [rps], [r_scT], out=scT[:], in_=ps[:, 0:16], func=AF.Exp, scale=-1.0)
            I(DVE, "tensor_scalar", [r_scT], [r_scT], out=scT[:], in0=scT[:], scalar1=1.0, scalar2=None, op0=ALU.add)
            I(DVE, "reciprocal", [r_scT], [r_scT], out=scT[:], in_=scT[:])
            I(DVE, "tensor_tensor", [r_scT, rps], [r_scT], out=scT[:], in0=scT[:], in1=ps[:, 0:16], op=ALU.mult)
            ps, rps = psA.get()
            I(PE, "matmul", [r_brow, r_identf], [rps], ps[:, 0:48], lhsT=brow[:], rhs=identf[0:48, 0:48], start=True, stop=True)
            I(DVE, "tensor_copy", [rps], [r_bcol], out=bcol[:], in_=ps[:, 0:48])
            scv = scT[:].rearrange("p (c k) -> p c k", c=2)
            for blk in range(6):
                wt, rw0 = wm.get()
                if id(rw0) not in wm_res:
                    wm_res[id(rw0)] = [Res() for _ in range(8)]
                rw = wm_res[id(rw0)]
                for kc in range(8):
                    P.dma(SP if kc % 2 == 0 else ACT, wt[:, kc, :], w_mod[kc * 128:(kc + 1) * 128, blk * D:(blk + 1) * D], writes=[rw[kc]])
                ps, rps = psA.get()
                for ft in range(8):
                    for kc in range(8):
                        I(PE, "matmul", [rw[kc], r_scT], [rps], ps[:, ft * 2:ft * 2 + 2], lhsT=wt[:, kc, ft * 128:(ft + 1) * 128], rhs=scv[:, :, kc],
                            start=(kc == 0), stop=(kc == 7))
                I(DVE, "tensor_tensor", [rps, r_bcol], [r_modc], out=modc[:, blk, :, :], in0=ps[:, 0:16].rearrange("p (f c) -> p f c", c=2),
                    in1=bcol[:, blk * 8:(blk + 1) * 8].unsqueeze(2).broadcast_to([128, 8, 2]), op=ALU.add)
                if blk in (1, 4):
                    I(DVE, "tensor_scalar", [r_modc], [r_modc], out=modc[:, blk, :, :], in0=modc[:, blk, :, :], scalar1=1.0, scalar2=None, op0=ALU.add)
            for wi, blk in enumerate((2, 5)):
                for c in range(2):
                    for half in range(2):
                        ps, rps = psA.get()
                        for f4 in range(4):
                            ft = half * 4 + f4
                            gb, rgb = gbt.get()
                            I(DVE, "tensor_copy", [r_modc], [rgb], out=gb[:], in_=modc[:, blk, ft, c:c + 1].broadcast_to([128, 128]))
                            I(PE, "matmul", [rgb, r_identf], [rps], ps[:, f4 * 128:(f4 + 1) * 128], lhsT=gb[:], rhs=identf[:], start=True, stop=True)
                        I(ACT, "copy", [rps], [r_gates], out=gates[:, wi, c, half * 512:(half + 1) * 512], in_=ps[:])
            P.flush()

        with ExitStack() as st:
            cosT = sbt(st, "cosT", [128, TS // 128, 32]); sinT = sbt(st, "sinT", [128, TS // 128, 32]); r_rope = Res()
            P.dma(SP, cosT[:], rope_cos.rearrange("(n p) f -> p n f", p=128), writes=[r_rope])
            P.dma(SP, sinT[:], rope_sin.rearrange("(n p) f -> p n f", p=128), writes=[r_rope])
            psA = ring(st, "psA_", [128, 512], F32, 6, psum=True)
            psT = ring(st, "psT_", [128, 1024], BF16, 2, psum=True)
            xt_r = ring(st, "xt", [128, D], F32, 3)
            xn_r = ring(st, "xn", [128, D], BF16, 2)
            st_r = ring(st, "bst", [128, 2, 6], F32, 2)
            mv_r = ring(st, "bmv", [128, 4], F32, 2)
            hT_r = ring(st, "hT", [128, 8, 512], BF16, 2)
            pq_r = ring(st, "pq", [128, 512], BF16, 4)
            tm_r = ring(st, "tm", [128, 1536], BF16, 2)
            qkf_r = ring(st, "qkf", [128, 512], F32, 2)
            rt_r = ring(st, "rt", [128, 4, 256], F32, 1)
            rqk_r = ring(st, "rqk", [128, 512], BF16, 3)
            rqT_r = ring(st, "rqT", [128, 4, 128], BF16, 3)
            gt_r = ring(st, "gt", [8, 5, 512], F32, 1)

            def layernorm_stats(xt, rx, mv, rmv, stt, rst, eps):
                for c2 in range(2):
                    I(DVE, "bn_stats", [rx], [rst], out=stt[:, c2, :], in_=xt[:, c2 * 512:(c2 + 1) * 512])
                I(DVE, "bn_aggr", [rst], [rmv], out=mv[:, 0:2], in_=stt[:])
                I(ACT, "activation", [rmv], [rmv], out=mv[:, 2:3], in_=mv[:, 1:2], func=AF.Ln, bias=eps)
                I(ACT, "activation", [rmv], [rmv], out=mv[:, 2:3], in_=mv[:, 2:3], func=AF.Exp, scale=-0.5)

            cwr = sbt(st, "cwr", [60, 128]); r_cwr = Res()
            cwc = sbt(st, "cwc", [128, 60]); r_cwc = Res()
            Dg = sbt(st, "Dg", [128, 60, 128], BF16); r_Dg = Res()
            P.dma(SP, cwr[:], conv_w.rearrange("k (c p) -> (k c) p", p=128), writes=[r_cwr])
            ps, rps = psA.get()
            I(PE, "matmul", [r_cwr, r_identf], [rps], ps[:, 0:60], lhsT=cwr[:], rhs=identf[0:60, 0:60], start=True, stop=True)
            I(DVE, "tensor_copy", [rps], [r_cwc], out=cwc[:], in_=ps[:, 0:60])
            for i in range(60):
                I(DVE, "tensor_scalar", [r_identf, r_cwc], [r_Dg], out=Dg[:, i, :], in0=identf[:], scalar1=cwc[:, i:i + 1], scalar2=None, op0=ALU.mult)
            pw_r = ring(st, "pw", [128, 516], BF16, 4)
            cs_r = ring(st, "cs", [128, 512], F32, 3)
            sq_r = ring(st, "sq", [128, 512], BF16, 3)
            rs_r = ring(st, "rs", [128, 512], F32, 3)
            xh_r = ring(st, "xh", [128, 512], BF16, 4)
            kt_r = ring(st, "kt", [128, 4, 128], BF16, 4)
            def phaseA_tile(q, t0):
                T = q["T"]
                TT = min(512, T)
                nsub = TT // 128
                Gt, rG = G[q["i"]]
                cnd = q["cond"]
                foreign = t0 >= q["own"]
                hT, rhT = hT_r.get()
                for j in range(nsub):
                    ts = t0 + j * 128
                    xt, rx = xt_r.get()
                    P.dma(SP, xt[:], q["x"][ts:ts + 128, :], writes=[rx])
                    stt, rst = st_r.get(); mv, rmv = mv_r.get()
                    layernorm_stats(xt, rx, mv, rmv, stt, rst, 1e-6)
                    xn, rxn = xn_r.get()
                    I(DVE, "tensor_scalar", [rx, rmv], [rxn], out=xn[:], in0=xt[:], scalar1=mv[:, 0:1], scalar2=mv[:, 2:3],
                                                                           op0=ALU.subtract, op1=ALU.mult)
                    pt, rpt = psT.get()
                    for kc in range(8):
                        I(PE, "transpose", [rxn, r_identb], [rpt], pt[:, kc * 128:(kc + 1) * 128], xn[:, kc * 128:(kc + 1) * 128], identb[:])
                    for kc in range(8):
                        I(ACT, "activation", [rpt, r_modc], [rhT], out=hT[:, kc, j * 128:(j + 1) * 128], in_=pt[:, kc * 128:(kc + 1) * 128], func=AF.Identity,
                            scale=modc[:, 1, kc, cnd:cnd + 1], bias=modc[:, 0, kc, cnd:cnd + 1])
                for ct in range(12):
                    if foreign and ct < 4 and t0 != q["own"]:
                        continue
                    ps, rps = psA.get()
                    for kc in range(8):
                        I(PE, "matmul", [r_win[kc], rhT], [rps], ps[:, 0:TT], lhsT=w_in_sb[:, kc, ct * 128:(ct + 1) * 128], rhs=hT[:, kc, 0:TT],
                                                                            start=(kc == 0), stop=(kc == 7))
                    pq, rpq = pq_r.get()
                    if ct % 2 == 0:
                        I(ACT, "copy", [rps], [rpq], out=pq[:, 0:TT], in_=ps[:, 0:TT])
                    else:
                        I(DVE, "tensor_copy", [rps], [rpq], out=pq[:, 0:TT], in_=ps[:, 0:TT])
                    rPQ[(q["i"], t0, ct)] = Res()
                    P.dma(SP, q["PQ"][ct * 128:(ct + 1) * 128, t0:t0 + TT], pq[:, 0:TT], reads=[rpq], writes=[rPQ[(q["i"], t0, ct)]])
                psa_, rpa = psA.get()
                psb_, rpb = psA.get()
                for which, (pp, rp) in enumerate(((psa_, rpa), (psb_, rpb))):
                    c0 = 2048 + which * 8
                    for kc in range(8):
                        I(PE, "matmul", [r_win[kc], rhT], [rp], pp[0:8, 0:TT], lhsT=w_in_sb[:, kc, c0:c0 + 8], rhs=hT[:, kc, 0:TT], start=(kc == 0), stop=(kc == 7))
                pa = psa_[0:8, 0:TT]; pb = psb_[0:8, 0:TT]
                gt, rgt = gt_r.get()
                I(ACT, "activation", [rpa, r_gc], [rgt], out=gt[:, 0, 0:TT], in_=pa, func=AF.Exp, bias=dtb[:, 0:1])
                I(ACT, "activation", [rgt], [rgt], out=gt[:, 0, 0:TT], in_=gt[:, 0, 0:TT], func=AF.Ln, bias=1.0)
                I(DVE, "tensor_scalar", [rgt, r_gc], [rgt], out=gt[:, 0, 0:TT], in0=gt[:, 0, 0:TT], scalar1=negA[:, 0:1], scalar2=None, op0=ALU.mult)
                I(ACT, "activation", [rpb], [rgt], out=gt[:, 3, 0:TT], in_=pb, func=AF.Exp, scale=-1.0)
                I(ACT, "activation", [rgt], [rgt], out=gt[:, 2, 0:TT], in_=gt[:, 3, 0:TT], func=AF.Ln, bias=1.0)
                I(DVE, "tensor_scalar", [rgt], [rgt], out=gt[:, 3, 0:TT], in0=gt[:, 3, 0:TT], scalar1=1.0, scalar2=None, op0=ALU.add)
                I(DVE, "reciprocal", [rgt], [rgt], out=gt[:, 1, 0:TT], in_=gt[:, 3, 0:TT])
                ps, rps = psA.get()
                nc64 = TT // 64
                for c64 in range(nc64):
                    for a in range(3):
                        I(PE, "matmul", [rgt, r_identf], [rps], ps[0:64, c64 * 24 + a * 8:c64 * 24 + a * 8 + 8],
                                                                             lhsT=gt[:, a, c64 * 64:(c64 + 1) * 64], rhs=identf[0:8, 0:8], start=True, stop=True)
                I(DVE, "tensor_copy", [rps], [rG], out=Gt[:, t0 // 64:t0 // 64 + nc64, :], in_=ps[0:64, 0:nc64 * 24].rearrange("p (c a) -> p c a", a=24))
                ps, rps = psA.get()
                Gt2, rG2 = G128[q["i"]]
                for c128 in range(nsub):
                    for a in range(3):
                        I(PE, "matmul", [rgt, r_identf], [rps], ps[:, c128 * 24 + a * 8:c128 * 24 + a * 8 + 8],
                          lhsT=gt[:, a, c128 * 128:(c128 + 1) * 128], rhs=identf[0:8, 0:8], start=True, stop=True)
                I(DVE, "tensor_copy", [rps], [rG2], out=Gt2[:, t0 // 128:t0 // 128 + nsub, :], in_=ps[:, 0:nsub * 24].rearrange("p (c a) -> p c a", a=24))
                for j in range(nsub):
                    ts = t0 + j * 128
                    tm, rtm = tm_r.get()
                    for gi, c0 in enumerate((1536, 2064, 2576, 3088)):
                        if foreign and gi in (0, 3):
                            continue
                        ps, rps = psA.get()
                        for kc in range(8):
                            I(PE, "matmul", [r_win[kc], rhT], [rps], ps[:], lhsT=hT[:, kc, j * 128:(j + 1) * 128], rhs=w_in_sb[:, kc, c0:c0 + 512],
                                                                                  start=(kc == 0), stop=(kc == 7))
                        if gi == 0:
                            I(ACT, "activation", [rps], [rtm], out=tm[:, 0:512], in_=ps[:], func=AF.Silu)
                        elif gi == 2:
                            I(DVE, "tensor_copy", [rps], [rtm], out=tm[:, 512:1024], in_=ps[:])
                        elif gi == 3:
                            I(ACT, "activation", [rps], [rtm], out=tm[:, 1024:1536], in_=ps[:], func=AF.Silu)
                        else:
                            rqk, rrqk = rqk_r.get()
                            if q["rope"]:
                                qkf, rqkf = qkf_r.get()
                                I(ACT, "copy", [rps], [rqkf], out=qkf[:], in_=ps[:])
                                rt, rrt = rt_r.get()
                                xv = qkf[:].rearrange("p (a h f) -> p a h f", a=8, h=2)
                                ov = rqk[:].rearrange("p (a h f) -> p a h f", a=8, h=2)
                                nt = ts // 128
                                cb = cosT[:, nt, :].unsqueeze(1).broadcast_to([128, 8, 32])
                                sb_ = sinT[:, nt, :].unsqueeze(1).broadcast_to([128, 8, 32])
                                rv = [rt[:, k, :].rearrange("p (a f) -> p a f", a=8) for k in range(4)]
                                I(DVE, "tensor_tensor", [rqkf, r_rope], [rrt], out=rv[0], in0=xv[:, :, 0, :], in1=cb, op=ALU.mult)
                                I(DVE, "tensor_tensor", [rqkf, r_rope], [rrt], out=rv[1], in0=xv[:, :, 1, :], in1=sb_, op=ALU.mult)
                                I(DVE, "tensor_tensor", [rqkf, r_rope], [rrt], out=rv[2], in0=xv[:, :, 0, :], in1=sb_, op=ALU.mult)
                                I(DVE, "tensor_tensor", [rqkf, r_rope], [rrt], out=rv[3], in0=xv[:, :, 1, :], in1=cb, op=ALU.mult)
                                I(DVE, "tensor_tensor", [rrt], [rrqk], out=ov[:, :, 0, :], in0=rv[0], in1=rv[1], op=ALU.subtract)
                                I(DVE, "tensor_tensor", [rrt], [rrqk], out=ov[:, :, 1, :], in0=rv[2], in1=rv[3], op=ALU.add)
                            else:
                                I(ACT, "copy", [rps], [rrqk], out=rqk[:], in_=ps[:])
                            P.dma(SP, q["RK"][ts:ts + 128, :], rqk[:, 256:512], reads=[rrqk])
                            pt, rpt = psT.get()
                            for b4 in range(4):
                                I(PE, "transpose", [rrqk, r_identb], [rpt], pt[:, b4 * 128:(b4 + 1) * 128], rqk[:, b4 * 128:(b4 + 1) * 128], identb[:])
                            rqT, rrqT = rqT_r.get()
                            I(DVE, "tensor_copy", [rpt], [rrqT], out=rqT[:].rearrange("p a t -> p (a t)"), in_=pt[:, 0:512])
                            P.dma(SP, q["RQK"][:, ts:ts + 128].rearrange("(a p) t -> p a t", p=128), rqT[:], reads=[rrqT])
                    if foreign:
                        P.dma(SP, q["TM"][ts:ts + 128, 512:1024], tm[:, 512:1024], reads=[rtm])
                    else:
                        P.dma(SP, q["TM"][ts:ts + 128, :], tm[:], reads=[rtm])

            def phaseA2_tile(q, t0):
                T = q["T"]
                TT = min(512, T)
                nsub = TT // 128
                for ct in range(12):
                    if t0 >= q["own"] and ct < 4:
                        continue
                    kind = ct // 4
                    h = ct % 4
                    pw, rpw = pw_r.get()
                    lo = max(t0 - 2, 0); hi = min(t0 + TT + 2, T)
                    if lo > t0 - 2:
                        I(DVE, "memset", [], [rpw], pw[:, 0:2], 0.0)
                    if hi < t0 + TT + 2:
                        I(DVE, "memset", [], [rpw], pw[:, TT + 2:TT + 4], 0.0)
                    rdeps = [rPQ[(q["i"], t_, ct)] for t_ in (t0 - TT, t0, t0 + TT) if (q["i"], t_, ct) in rPQ]
                    P.dma(SP, pw[:, lo - t0 + 2:hi - t0 + 2], q["PQ"][ct * 128:(ct + 1) * 128, lo:hi], reads=rdeps, writes=[rpw])
                    ps, rps = psA.get()
                    for k in range(5):
                        I(PE, "matmul", [r_Dg, rpw], [rps], ps[:, 0:TT], lhsT=Dg[:, k * 12 + ct, :], rhs=pw[:, k:k + TT], start=(k == 0), stop=(k == 4))
                    xh, rxh = xh_r.get()
                    if kind == 2:
                        I(ACT, "activation", [rps], [rxh], out=xh[:, 0:TT], in_=ps[:, 0:TT], func=AF.Silu)
                    else:
                        cs, rcs = cs_r.get()
                        I(ACT, "activation", [rps], [rcs], out=cs[:, 0:TT], in_=ps[:, 0:TT], func=AF.Silu)
                        sq, rsq = sq_r.get()
                        I(DVE, "tensor_tensor", [rcs], [rsq], out=sq[:, 0:TT], in0=cs[:, 0:TT], in1=cs[:, 0:TT], op=ALU.mult)
                        ps2, rps2 = psA.get()
                        ones_ = onesq if kind == 0 else onesb
                        r_ones = r_onesq if kind == 0 else r_onesb
                        I(PE, "matmul", [rsq, r_ones], [rps2], ps2[:, 0:TT], lhsT=ones_[:], rhs=sq[:, 0:TT], start=True, stop=True)
                        rs, rrs = rs_r.get()
                        eps = 1e-6 * (128.0 if kind == 0 else 1.0)
                        I(ACT, "activation", [rps2], [rrs], out=rs[:, 0:TT], in_=ps2[:, 0:TT], func=AF.Ln, bias=eps)
                        I(ACT, "activation", [rrs], [rrs], out=rs[:, 0:TT], in_=rs[:, 0:TT], func=AF.Exp, scale=-0.5)
                        I(DVE, "tensor_tensor", [rcs, rrs], [rxh], out=xh[:, 0:TT], in0=cs[:, 0:TT], in1=rs[:, 0:TT], op=ALU.mult)
                        rb = h * 2 + (1 if kind == 0 else 0)
                        P.dma(SP, q["QK"][rb * 128:(rb + 1) * 128, t0:t0 + TT], xh[:, 0:TT], reads=[rxh])
                    if kind >= 1:
                        pt, rpt = psT.get()
                        for j in range(nsub):
                            I(PE, "transpose", [rxh, r_identb], [rpt], pt[:, j * 128:(j + 1) * 128], xh[:, j * 128:(j + 1) * 128], identb[:])
                        kt, rkt = kt_r.get()
                        if kind == 1:
                            I(DVE, "tensor_copy", [rpt], [rkt], out=kt[:, 0:nsub, :].rearrange("p a t -> p (a t)"), in_=pt[:, 0:nsub * 128])
                        else:
                            I(ACT, "copy", [rpt], [rkt], out=kt[:, 0:nsub, :].rearrange("p a t -> p (a t)"), in_=pt[:, 0:nsub * 128])
                        c0 = (kind - 1) * 512 + h * 128
                        P.dma(SP, q["KV"][t0:t0 + TT, c0:c0 + 128].rearrange("(j p) c -> p j c", p=128), kt[:, 0:nsub, :], reads=[rkt])

            rPQ = {}
            tilesA = []
            for q in seqs:
                TT_ = min(512, q["T"])
                for t0 in range(0, q["T"], TT_):
                    tilesA.append((q, t0, TT_))
            pend = []
            for (q, t0, TT_) in tilesA:
                phaseA_tile(q, t0)
                pend.append((q, t0, TT_))
                for (q2, t2, TT2) in list(pend):
                    nxt = t2 + TT2
                    if nxt >= q2["T"] or (q2["i"], nxt, 11) in rPQ:
                        phaseA2_tile(q2, t2)
                        pend.remove((q2, t2, TT2))
            assert not pend
            P.flush()
        stW.close()

        with ExitStack() as st2:


            with ExitStack() as st:
                psA = ring(st, "psC_", [128, 512], F32, 7, psum=True)
                psR = psA
                psT = ring(st, "psCT_", [128, 1024], BF16, 1, psum=True)
                ut8 = sbt(st, "ut8", [64, 2, 512]); negm = sbt(st, "negm", [64, 2, 512]); r_cst = Res()
                ut8b = sbt(st, "ut8b", [64, 2, 512], BF16); negb = sbt(st, "negb", [64, 2, 512], BF16)
                idb4 = sbt(st, "idb4", [64, 4, 64], BF16)
                for d in range(2):
                    P.dma(SP, ut8[:, d, :], c_ut8[d], writes=[r_cst])
                    P.dma(SP, negm[:, d, :], c_neg[d], writes=[r_cst])
                I(DVE, "tensor_copy", [r_cst], [r_cst], out=ut8b[:], in_=ut8[:])
                I(DVE, "tensor_copy", [r_cst], [r_cst], out=negb[:], in_=negm[:])
                I(DVE, "tensor_copy", [r_identb], [r_cst], out=idb4[:], in_=identb[0:64, 0:64].unsqueeze(1).broadcast_to([64, 4, 64]))
                qk_r = ring(st, "qkc", [128, 8, 64], BF16, 7)
                chains = []
                for q in seqs:
                    Gt, rG = G[q["i"]]
                    nch = q["nch"]
                    for d in range(2):
                        nm = "%s%d" % (q["name"], d)
                        S = [sbt(st, "S%s_%d" % (nm, k), [128, 4, 128]) for k in range(2)]
                        rS = [Res(), Res()]
                        Sb = sbt(st, "Sb" + nm, [128, 4, 128], BF16); rSb = Res()
                        pre = sbt(st, "pre" + nm, [64, nch, 28]); rpre = Res()
                        egl = sbt(st, "egl" + nm, [128, nch, 4]); regl = Res()
                        if q["i"] == 0:
                            P.dma(SP, S[0][:], sd0[d].rearrange("(h k) v -> k h v", k=128), writes=[rS[0]])
                        else:
                            I(POOL, "memset", [], [rS[0]], S[0][:], 0.0)
                        I(ACT, "copy", [rS[0]], [rSb], out=Sb[:], in_=S[0][:])
                        I(DVE, "tensor_copy", [rG], [rpre], out=pre[:, :, 0:4], in_=Gt[:, :, d * 4:d * 4 + 4])
                        nn = nch * 4
                        gsd = sbt(st, "gsd" + nm, [64, nn]); rgsd = Res()
                        I(DVE, "tensor_copy", [rG], [rgsd], out=gsd[:].rearrange("p (c h) -> p c h", h=4), in_=Gt[:, :, d * 4:d * 4 + 4])
                        ps, rps = psA.get()
                        ps2, rps2 = psA.get()
                        I(PE, "matmul", [r_cst, rgsd], [rps], ps[0:64, 0:nn], lhsT=ut8[:, d, 0:64], rhs=gsd[:], start=True, stop=True)
                        I(PE, "matmul", [r_onesf, rgsd], [rps2], ps2[:, 0:nn], lhsT=onesf[:], rhs=gsd[:], start=True, stop=True)
                        gcv = ps[0:64, 0:nn].rearrange("p (c h) -> p c h", h=4)
                        I(DVE, "tensor_copy", [rps], [rpre], out=pre[:, :, 4:8], in_=gcv)
                        b2 = pre[:, :, 8:16].rearrange("p c (h a) -> p c h a", a=2)
                        I(DVE, "tensor_tensor", [rps, rG], [rpre], out=b2[:, :, :, 0], in0=gcv, in1=Gt[:, :, 16 + d * 4:20 + d * 4], op=ALU.add)
                        I(DVE, "tensor_copy", [rps], [rpre], out=b2[:, :, :, 1], in_=gcv)
                        I(ACT, "activation", [rps], [rpre], out=pre[:, :, 20:24], in_=gcv, func=AF.Exp)
                        I(DVE, "tensor_scalar", [rpre], [rpre], out=pre[:, :, 16:20], in0=pre[:, :, 20:24], scalar1=-1.0, scalar2=None, op0=ALU.mult)
                        glv = ps2[0:64, 0:nn].rearrange("p (c h) -> p c h", h=4)
                        I(DVE, "tensor_tensor", [rps2, rpre], [rpre], out=pre[:, :, 24:28], in0=glv, in1=pre[:, :, 4:8], op=ALU.subtract)
                        I(ACT, "activation", [rpre], [rpre], out=pre[:, :, 24:28], in_=pre[:, :, 24:28], func=AF.Exp)
                        I(DVE, "tensor_tensor", [rpre, rG], [rpre], out=pre[:, :, 24:28], in0=pre[:, :, 24:28], in1=Gt[:, :, 8 + d * 4:12 + d * 4], op=ALU.mult)
                        I(ACT, "activation", [rps2], [regl], out=egl[:].rearrange("p c h -> p (c h)"), in_=ps2[:, 0:nn], func=AF.Exp)
                        nown = q["own"] // 64
                        if d == 0:
                            order = list(range(nown))
                        else:
                            order = list(range(nch - 1, -1, -1))
                        npair = nch // 2
                        if d == 0:
                            porder = list(range(nown // 2))
                        else:
                            porder = list(range(npair - 1, -1, -1))
                        chains.append(dict(q=q, d=d, S=S, rS=rS, Sb=Sb, rSb=rSb, pre=pre, rpre=rpre, egl=egl, regl=regl, order=order, pos=0, cur=0, nown=nown,
                                           porder=porder, nm=nm, nch=nch))

                stB0 = ExitStack()
                ut8w = sbt(stB0, "ut8w", [128, 2, 512]); negw = sbt(stB0, "negw", [128, 2, 512]); r_cw = Res()
                ut8wb = sbt(stB0, "ut8wb", [128, 2, 512], BF16); negwb = sbt(stB0, "negwb", [128, 2, 512], BF16)
                idw4 = sbt(stB0, "idw4", [128, 4, 64], BF16)
                for d in range(2):
                    for e_ in range(2):
                        P.dma(SP, ut8w[e_ * 64:(e_ + 1) * 64, d, :], c_ut8[d], writes=[r_cw])
                        P.dma(SP, negw[e_ * 64:(e_ + 1) * 64, d, :], c_neg[d], writes=[r_cw])
                I(DVE, "tensor_copy", [r_cw], [r_cw], out=ut8wb[:], in_=ut8w[:])
                I(DVE, "tensor_copy", [r_cw], [r_cw], out=negwb[:], in_=negw[:])
                I(DVE, "tensor_copy", [r_identb], [r_cw], out=idw4[0:64], in_=identb[0:64, 0:64].unsqueeze(1).broadcast_to([64, 4, 64]))
                I(DVE, "tensor_copy", [r_identb], [r_cw], out=idw4[64:128], in_=identb[64:128, 64:128].unsqueeze(1).broadcast_to([64, 4, 64]))
                for ch in chains:
                    q = ch["q"]; d = ch["d"]; nm = ch["nm"]; nch = ch["nch"]
                    Gt2, rG2 = G128[q["i"]]
                    npair = nch // 2
                    n2 = npair * 4
                    pw_ = sbt(stB0, "prw" + nm, [128, npair, 12]); rpw_ = Res()
                    g2c = sbt(stB0, "g2c" + nm, [128, n2]); rg2c = Res()
                    I(DVE, "tensor_copy", [rG2], [rpw_], out=pw_[:, :, 0:4], in_=Gt2[:, :, d * 4:d * 4 + 4])
                    I(DVE, "tensor_copy", [rG2], [rg2c], out=g2c[:].rearrange("p (c h) -> p c h", h=4), in_=Gt2[:, :, d * 4:d * 4 + 4])
                    ps3, rps3 = psA.get()
                    I(PE, "matmul", [r_cw, rg2c], [rps3], ps3[0:64, 0:n2], lhsT=ut8w[0:64, d, 0:64], rhs=g2c[0:64, :], start=True, stop=True)
                    I(PE, "matmul", [r_cw, rg2c], [rps3], ps3[64:128, 0:n2], lhsT=ut8w[64:128, d, 0:64], rhs=g2c[64:128, :], start=True, stop=True, tile_position=(64, 64))
                    gcw = ps3[:, 0:n2].rearrange("p (c h) -> p c h", h=4)
                    b2w = pw_[:, :, 4:12].rearrange("p c (h a) -> p c h a", a=2)
                    I(DVE, "tensor_tensor", [rps3, rG2], [rpw_], out=b2w[:, :, :, 0], in0=gcw, in1=Gt2[:, :, 16 + d * 4:20 + d * 4], op=ALU.add)
                    I(DVE, "tensor_tensor", [rps3, rG2], [rpw_], out=b2w[:, :, :, 1], in0=gcw, in1=Gt2[:, :, 16 + d * 4:20 + d * 4], op=ALU.add)

                    ch["pw"] = pw_; ch["rpw"] = rpw_
                qkp_r = ring(stB0, "qkp", [128, 8, 128], BF16, 4)
                gu_r = ring(stB0, "gu", [128, 512], BF16, 4)
                dd_r = ring(stB0, "dd", [128, 512], F32, 4)
                ee_r = ring(stB0, "ee", [128, 512], F32, 4)
                pc_r = ring(stB0, "pc", [128, 4, 2, 64], BF16, 8)
                sc_r = ring(stB0, "sc", [128, 4, 64], BF16, 8)
                stg_r = ring(stB0, "stg", [128, 512], BF16, 4)

                def prep_step(ch, m):
                    q = ch["q"]; d = ch["d"]
                    own = (2 * m) < ch["nown"]
                    pw_, rpw_ = ch["pw"], ch["rpw"]
                    t0 = m * 128
                    HV = ((0, None), (64, (64, 64)))
                    qk, rqk = qkp_r.get()
                    if own:
                        P.dma(SP, qk[:], q["QK"][:, t0:t0 + 128].rearrange("(a p) t -> p a t", p=128), writes=[rqk])
                    else:
                        P.dma(SP, qk[:].rearrange("p (h two) t -> p h two t", two=2)[:, :, 0, :],
                              q["QK"][:, t0:t0 + 128].rearrange("(h two p) t -> p h two t", two=2, p=128)[:, :, 0, :], writes=[rqk])
                    stg, rstg = stg_r.get()
                    gu, rgu = gu_r.get()
                    I(DVE, "tensor_tensor", [r_cw, rpw_], [rgu], out=gu[:].rearrange("p (h x) -> p h x", h=4), in0=ut8wb[:, d, :].rearrange("p (h x) -> p h x", h=4),
                      in1=pw_[:, m, 0:4].unsqueeze(2).broadcast_to([128, 4, 128]), op=ALU.mult)
                    dps, rdps = psA.get()
                    for b_, tp in HV:
                        kw = {} if tp is None else {"tile_position": tp}
                        I(PE, "matmul", [r_onesb, rgu], [rdps], dps[b_:b_ + 64, :], lhsT=onesb[b_:b_ + 64, 0:64], rhs=gu[b_:b_ + 64, :], start=True, stop=False, **kw)
                        I(PE, "matmul", [r_identb, r_cw], [rdps], dps[b_:b_ + 64, :], lhsT=identb[b_:b_ + 64, b_:b_ + 64], rhs=negwb[b_:b_ + 64, d, :], start=False, stop=True, **kw)
                    dd, rdd = dd_r.get()
                    I(DVE, "tensor_tensor", [rdps, rpw_], [rdd], out=dd[:].rearrange("p (a x) -> p a x", a=8), in0=dps[:, :].rearrange("p (a x) -> p a x", a=8),
                      in1=pw_[:, m, 4:12].unsqueeze(2).broadcast_to([128, 8, 64]), op=ALU.subtract)
                    ee, ree = ee_r.get()
                    I(ACT, "activation", [rdd], [ree], out=ee[:], in_=dd[:], func=AF.Exp)
                    kps, rkps = psA.get()
                    for e_, (b_, tp) in enumerate(HV):
                        kw = {} if tp is None else {"tile_position": (0, 64)}
                        for h in range(4):
                            if own:
                                I(PE, "matmul", [rqk], [rkps], kps[b_:b_ + 64, h * 128:(h + 1) * 128], lhsT=qk[:, 2 * h, b_:b_ + 64], rhs=qk[:, 2 * h:2 * h + 2, b_:b_ + 64],
                                  start=True, stop=True, **kw)
                            else:
                                I(PE, "matmul", [rqk], [rkps], kps[b_:b_ + 64, h * 128:h * 128 + 64], lhsT=qk[:, 2 * h, b_:b_ + 64], rhs=qk[:, 2 * h, b_:b_ + 64], start=True, stop=True, **kw)
                    pc, rpc = pc_r.get()
                    kv4 = kps[:, :].rearrange("p (h a x) -> p h a x", h=4, a=2)
                    ev4 = ee[:].rearrange("p (h a x) -> p h a x", h=4, a=2)
                    I(DVE, "tensor_tensor", [rkps, ree], [rpc], out=pc[:, :, 0, :], in0=kv4[:, :, 0, :], in1=ev4[:, :, 0, :], op=ALU.mult)
                    if own:
                        I(DVE, "tensor_tensor", [rkps, ree], [rstg], out=stg[:, 256:512].rearrange("p (h x) -> p h x", h=4), in0=kv4[:, :, 1, :], in1=ev4[:, :, 1, :], op=ALU.mult)
                    pt, rpt = psT.get()
                    for b_, tp in HV:
                        kw = {} if tp is None else {"tile_position": tp}
                        for h in range(4):
                            I(PE, "transpose", [rpc, r_identb], [rpt], pt[b_:b_ + 64, h * 64:(h + 1) * 64], pc[b_:b_ + 64, h, 0, :], identb[b_:b_ + 64, b_:b_ + 64], **kw)
                    I(ACT, "copy", [rpt], [rpc], out=pc[:, :, 1, :], in_=pt[:, 0:256].rearrange("p (h x) -> p h x", h=4))
                    sc, rsc = sc_r.get()
                    I(DVE, "tensor_tensor", [rpc, r_cw], [rsc], out=sc[:], in0=idw4[:], in1=pc[:, :, 0, :], op=ALU.subtract)
                    yield
                    for lvl in range(5):
                        xps, rxps = psA.get()
                        for b_, tp in HV:
                            kw = {} if tp is None else {"tile_position": tp}
                            for h in range(4):
                                if lvl < 4:
                                    I(PE, "matmul", [rpc], [rxps], xps[b_:b_ + 64, h * 128:h * 128 + 64], lhsT=pc[b_:b_ + 64, h, 1, :], rhs=pc[b_:b_ + 64, h, 0, :], start=True, stop=True, **kw)
                                I(PE, "matmul", [rpc], [rxps], xps[b_:b_ + 64, h * 128 + 64:h * 128 + 128], lhsT=pc[b_:b_ + 64, h, 0, :], rhs=pc[b_:b_ + 64, h, 1, :], start=True, stop=True, **kw)
                        pcn, rpcn = pc_r.get()
                        if lvl < 4:
                            I(ACT, "copy", [rxps], [rpcn], out=pcn[:].rearrange("p h a x -> p (h a x)"), in_=xps[:, :])
                        else:
                            I(ACT, "copy", [rxps], [rpcn], out=pcn[:, :, 1, :], in_=xps[:, :].rearrange("p (h a x) -> p h a x", h=4, a=2)[:, :, 1, :])
                        pc, rpc = pcn, rpcn
                        yps, ryps = psA.get()
                        for b_, tp in HV:
                            kw = {} if tp is None else {"tile_position": tp}
                            for h in range(4):
                                I(PE, "matmul", [rpc, rsc], [ryps], yps[b_:b_ + 64, h * 64:(h + 1) * 64], lhsT=pc[b_:b_ + 64, h, 1, :], rhs=sc[b_:b_ + 64, h, :], start=True, stop=True, **kw)
                        if lvl == 4:
                            I(DVE, "tensor_tensor", [ryps, rsc], [rstg], out=stg[:, 0:256].rearrange("p (h x) -> p h x", h=4), in0=yps[:, 0:256].rearrange("p (h x) -> p h x", h=4), in1=sc[:], op=ALU.add)
                        else:
                            scn, rscn = sc_r.get()
                            I(DVE, "tensor_tensor", [ryps, rsc], [rscn], out=scn[:], in0=yps[:, 0:256].rearrange("p (h x) -> p h x", h=4), in1=sc[:], op=ALU.add)
                            sc, rsc = scn, rscn
                        yield
                    dst = q["S6Q"][d, 2 * m:2 * m + 2].rearrange("e p c -> (e p) c")
                    if own:
                        P.dma(SP, dst, stg[:], reads=[rstg])
                    else:
                        P.dma(SP, dst[:, 0:256], stg[:, 0:256], reads=[rstg])

                def rec_step(ch):
                    q = ch["q"]; d = ch["d"]; n = ch["order"][ch["pos"]]
                    own = n < ch["nown"]
                    Gt, rG = G[q["i"]]
                    pre, rpre = ch["pre"], ch["rpre"]
                    t0 = n * 64
                    qk, rqk = qk_r.get()
                    kv, rkv = kv_r.get()
                    s6q, rs6q = s6q_r.get()
                    if own:
                        P.dma(SP, qk[:], q["QK"][:, t0:t0 + 64].rearrange("(a p) t -> p a t", p=128), writes=[rqk])
                        P.dma(SP, s6q[:], q["S6Q"][d, n], writes=[rs6q])
                    else:
                        P.dma(SP, qk[:].rearrange("p (h two) t -> p h two t", two=2)[:, :, 0, :],
                              q["QK"][:, t0:t0 + 64].rearrange("(h two p) t -> p h two t", two=2, p=128)[:, :, 0, :], writes=[rqk])
                        P.dma(SP, s6q[:, 0:256], q["S6Q"][d, n, :, 0:256], writes=[rs6q])
                    P.dma(SP, kv[:], q["KV"][t0:t0 + 64, :], writes=[rkv])
                    sc = s6q[:, 0:256].rearrange("p (h x) -> p h x", h=4); rsc = rs6q
                    qm = s6q[:, 256:512].rearrange("p (h x) -> p h x", h=4); rqm = rs6q
                    S_old, rS_old = ch["S"][ch["cur"]], ch["rS"][ch["cur"]]
                    S_new, rS_new = ch["S"][1 - ch["cur"]], ch["rS"][1 - ch["cur"]]
                    Sb, rSb = ch["Sb"], ch["rSb"]
                    ksp, rksp = psR.get()
                    for h in range(4):
                        I(PE, "matmul", [rqk, rSb], [rksp], ksp[0:64, h * 128:(h + 1) * 128], lhsT=qk[:, 2 * h, :], rhs=Sb[:, h, :], start=True, stop=True)
                    if own:
                        qsp, rqsp = psR.get()
                        for h in range(4):
                            I(PE, "matmul", [rqk, rSb], [rqsp], qsp[0:64, h * 128:(h + 1) * 128], lhsT=qk[:, 2 * h + 1, :], rhs=Sb[:, h, :], start=True, stop=True)
                    yield
                    rr, rrr = rr_r.get()
                    for h in range(4):
                        I(DVE, "scalar_tensor_tensor", [rksp, rpre, rkv], [rrr], out=rr[:, h, :], in0=ksp[0:64, h * 128:(h + 1) * 128], scalar=pre[:, n, 16 + h:17 + h],
                                                                       in1=kv[:, 512 + h * 128:512 + (h + 1) * 128], op0=ALU.mult, op1=ALU.add)
                    yield
                    trp, rtrp = psR.get()
                    for h in range(4):
                        I(PE, "matmul", [rsc, rrr], [rtrp], trp[0:64, h * 128:(h + 1) * 128], lhsT=sc[:, h, :], rhs=rr[:, h, :], start=True, stop=True)
                    yield
                    vn, rvn = vn_r.get()
                    I(ACT, "copy", [rtrp], [rvn], out=vn[:].rearrange("p h x -> p (h x)"), in_=trp[0:64, :])
                    kd, rkd = kd_r.get()
                    I(DVE, "tensor_tensor", [rkv, rpre], [rkd], out=kd[:], in0=kv[:, 0:512].rearrange("p (h x) -> p h x", h=4),
                                                         in1=pre[:, n, 24:28].unsqueeze(2).broadcast_to([64, 4, 128]), op=ALU.mult)
                    yield
                    if own:
                        oa, roa = oa_r.get()
                        for h in range(4):
                            I(ACT, "activation", [rqsp, rpre], [roa], out=oa[:, h, :], in_=qsp[0:64, h * 128:(h + 1) * 128], func=AF.Copy, scale=pre[:, n, 20 + h:21 + h])
                        obp, robp = psR.get()
                        for h in range(4):
                            I(PE, "matmul", [rqm, rvn], [robp], obp[0:64, h * 128:(h + 1) * 128], lhsT=qm[:, h, :], rhs=vn[:, h, :], start=True, stop=True)
                        oo, roo = oo_r.get()
                        I(DVE, "tensor_tensor", [robp, roa], [roo], out=oo[:], in0=obp[0:64, :], in1=oa[:].rearrange("p h x -> p (h x)"), op=ALU.add)
                        P.dma(SP, q["OA"][d, t0:t0 + 64, :], oo[:], reads=[roo])
                    sup, rsup = psR.get()
                    for h in range(4):
                        I(PE, "matmul", [rkd, rvn], [rsup], sup[:, h * 128:(h + 1) * 128], lhsT=kd[:, h, :], rhs=vn[:, h, :], start=True, stop=True)
                    yield
                    egl = ch["egl"]
                    for h in range(4):
                        I(DVE, "scalar_tensor_tensor", [rS_old, ch["regl"], rsup], [rS_new], out=S_new[:, h, :], in0=S_old[:, h, :], scalar=egl[:, n, h:h + 1], in1=sup[:, h * 128:(h + 1) * 128],
                                                                       op0=ALU.mult, op1=ALU.add)
                    I(ACT, "copy", [rS_new], [rSb], out=Sb[:], in_=S_new[:])
                    ch["cur"] = 1 - ch["cur"]
                    ch["pos"] += 1
                    if ch["pos"] == len(ch["order"]) and q["i"] > 0:
                        P.dma(SP, nsd[q["i"] - 1, d].rearrange("(h k) v -> k h v", k=128), S_new[:], reads=[rS_new])

                def lockstep(gens):
                    gens = list(gens)
                    while gens:
                        for g_ in list(gens):
                            try:
                                next(g_)
                            except StopIteration:
                                gens.remove(g_)

                KLOCK = 3
                tasks = []
                ppos = {id(ch): 0 for ch in chains}
                active = list(chains)
                while active:
                    for ch in list(active):
                        tasks.append((ch, ch["porder"][ppos[id(ch)]]))
                        ppos[id(ch)] += 1
                        if ppos[id(ch)] == len(ch["porder"]):
                            active.remove(ch)
                for i in range(0, len(tasks), KLOCK):
                    lockstep([prep_step(ch, n) for ch, n in tasks[i:i + KLOCK]])
                P.flush()
                stB0.close()
                kv_r = ring(st, "kvc", [64, 1024], BF16, 7)
                rr_r = ring(st, "rr", [64, 4, 128], BF16, 4)
                vn_r = ring(st, "vn", [64, 4, 128], BF16, 4)
                kd_r = ring(st, "kd", [64, 4, 128], BF16, 4)
                oa_r = ring(st, "oa", [64, 4, 128], F32, 4)
                oo_r = ring(st, "oo", [64, 512], F32, 4)
                s6q_r = ring(st, "s6q", [64, 512], BF16, 7)
                dmt = sbt(st, "dmt", [128, 2, 512]); xi = sbt(st, "xi", [64, 2, 512]); zeta = sbt(st, "zeta", [128, 8]); gch = sbt(st, "gch", [64, 8]); r_cst2 = Res()
                for d in range(2):
                    P.dma(SP, dmt[:, d, :], c_dmt[d], writes=[r_cst2])
                    P.dma(SP, xi[:, d, :], c_xi[d], writes=[r_cst2])
                P.dma(SP, zeta[:], c_zeta, writes=[r_cst2])
                P.dma(SP, gch[:], c_gch, writes=[r_cst2])
                rq_r = ring(st, "rqc", [64, 8, 128], BF16, 3)
                rk_r = ring(st, "rkc", [128, 256], BF16, 3)
                vb_r = ring(st, "vbc", [128, 512], BF16, 3)
                sm_r = ring(st, "smc", [128, 4, 128], BF16, 2)
                qx_r = ring(st, "qxc", [64, 4, 128], BF16, 2)
                kz_r = ring(st, "kzc", [128, 4, 64], BF16, 2)
                or_r = ring(st, "orc", [128, 512], F32, 2)
                rchains2 = []
                for q in seqs:
                    nch = q["T"] // 128
                    nown = q["own"] // 128
                    for d in range(2):
                        nm = "%s%d" % (q["name"], d)
                        S = [sbt(st, "R%s_%d" % (nm, k), [64, 4, 128]) for k in range(2)]
                        rS = [Res(), Res()]
                        Sb = sbt(st, "Rb" + nm, [64, 4, 128], BF16); rSb = Res()
                        if q["i"] == 0:
                            P.dma(SP, S[0][:], sr0[d].rearrange("(h k) v -> k h v", k=64), writes=[rS[0]])
                        else:
                            I(POOL, "memset", [], [rS[0]], S[0][:], 0.0)
                        I(ACT, "copy", [rS[0]], [rSb], out=Sb[:], in_=S[0][:])
                        order = list(range(nown)) if d == 0 else list(range(nch - 1, -1, -1))
                        rchains2.append(dict(q=q, d=d, S=S, rS=rS, Sb=Sb, rSb=rSb, order=order, pos=0, cur=0, nown=nown))

                def ret_step(ch):
                    q = ch["q"]; d = ch["d"]; n = ch["order"][ch["pos"]]
                    own = n < ch["nown"]
                    t0 = n * 128
                    rq, rrq = rq_r.get(); rk, rrk = rk_r.get(); vb, rvb = vb_r.get()
                    if own:
                        P.dma(SP, rq[:], q["RQK"][:, t0:t0 + 128].rearrange("(a p) t -> p a t", p=64), writes=[rrq])
                    P.dma(SP, rk[:], q["RK"][t0:t0 + 128, :], writes=[rrk])
                    P.dma(SP, vb[:], q["TM"][t0:t0 + 128, 512:1024], writes=[rvb])
                    S_old, rS_old = ch["S"][ch["cur"]], ch["rS"][ch["cur"]]
                    S_new, rS_new = ch["S"][1 - ch["cur"]], ch["rS"][1 - ch["cur"]]
                    Sb, rSb = ch["Sb"], ch["rSb"]
                    if own:
                        scp, rscp = psA.get()
                        for h in range(4):
                            I(PE, "matmul", [rrq], [rscp], scp[:, h * 128:(h + 1) * 128], lhsT=rq[:, 4 + h, :], rhs=rq[:, h, :], start=True, stop=True)
                        sm, rsm = sm_r.get()
                        I(DVE, "tensor_tensor", [rscp, r_cst2], [rsm], out=sm[:].rearrange("p h x -> p (h x)"), in0=scp[:], in1=dmt[:, d, :], op=ALU.mult)
                        qx, rqx = qx_r.get()
                        I(DVE, "tensor_tensor", [rrq, r_cst2], [rqx], out=qx[:].rearrange("p h x -> p (h x)"), in0=rq[:, 0:4, :].rearrange("p h x -> p (h x)"), in1=xi[:, d, :], op=ALU.mult)
                        orp, rorp = psA.get()
                        for h in range(4):
                            I(PE, "matmul", [rsm, rvb], [rorp], orp[:, h * 128:(h + 1) * 128], lhsT=sm[:, h, :], rhs=vb[:, h * 128:(h + 1) * 128], start=True, stop=False)
                            I(PE, "matmul", [rqx, rSb], [rorp], orp[:, h * 128:(h + 1) * 128], lhsT=qx[:, h, :], rhs=Sb[:, h, :], start=False, stop=True)
                        oc, roc = or_r.get()
                        I(ACT, "copy", [rorp], [roc], out=oc[:], in_=orp[:])
                        P.dma(SP, q["OR"][d, t0:t0 + 128, :], oc[:], reads=[roc])
                    kz, rkz = kz_r.get()
                    I(DVE, "tensor_tensor", [rrk, r_cst2], [rkz], out=kz[:], in0=rk[:].rearrange("p (h x) -> p h x", h=4), in1=zeta[:, d * 4:d * 4 + 4].unsqueeze(2).broadcast_to([128, 4, 64]), op=ALU.mult)
                    dsp, rdsp = psA.get()
                    for h in range(4):
                        I(PE, "matmul", [rkz, rvb], [rdsp], dsp[0:64, h * 128:(h + 1) * 128], lhsT=kz[:, h, :], rhs=vb[:, h * 128:(h + 1) * 128], start=True, stop=True)
                    for h in range(4):
                        I(DVE, "scalar_tensor_tensor", [rS_old, r_cst2, rdsp], [rS_new], out=S_new[:, h, :], in0=S_old[:, h, :], scalar=gch[:, d * 4 + h:d * 4 + h + 1], in1=dsp[0:64, h * 128:(h + 1) * 128],
                                                                       op0=ALU.mult, op1=ALU.add)
                    I(ACT, "copy", [rS_new], [rSb], out=Sb[:], in_=S_new[:])
                    ch["cur"] = 1 - ch["cur"]
                    ch["pos"] += 1
                    if ch["pos"] == len(ch["order"]) and q["i"] > 0:
                        P.dma(SP, nsr[q["i"] - 1, d].rearrange("(h k) v -> k h v", k=64), S_new[:], reads=[rS_new])


                rchains = sorted(chains, key=lambda c: -len(c["order"]))
                rchains2 = sorted(rchains2, key=lambda c: -len(c["order"]))
                rnd = 0
                while True:
                    act = [ch for ch in rchains if ch["pos"] < len(ch["order"])][:KLOCK]
                    act2 = [ch for ch in rchains2 if ch["pos"] < len(ch["order"])][:2]
                    if not act and not act2:
                        break
                    if act:
                        lockstep([rec_step(ch) for ch in act])
                    if act2 and (rnd % 2 == 1 or not act):
                        for ch in act2:
                            ret_step(ch)
                    rnd += 1
                P.flush()
            stG.close()

            w1_sb = sbt(st2, "w1_sb", [128, 8, DFF], BF16); r_w1 = [Res() for _ in range(8)]
            for kc in range(8):
                P.dma(POOL, w1_sb[:, kc, :], w_ff1[kc * 128:(kc + 1) * 128, :], writes=[r_w1[kc]])

            def ln_stats(xt, rx, mv, rmv, stt, rst, eps):
                for c2 in range(2):
                    I(DVE, "bn_stats", [rx], [rst], out=stt[:, c2, :], in_=xt[:, c2 * 512:(c2 + 1) * 512])
                I(DVE, "bn_aggr", [rst], [rmv], out=mv[:, 0:2], in_=stt[:])
                I(ACT, "activation", [rmv], [rmv], out=mv[:, 2:3], in_=mv[:, 1:2], func=AF.Ln, bias=eps)
                I(ACT, "activation", [rmv], [rmv], out=mv[:, 2:3], in_=mv[:, 2:3], func=AF.Exp, scale=-0.5)

            def bcast_row(stack, name, src, n):
                t = sbt(stack, name, [128, n]); r = Res()
                P.dma(SP, t[:], src.partition_broadcast(128), writes=[r])
                return t, r

            with ExitStack() as st:
                psY = ring(st, "psE_", [128, 1024], F32, 2, psum=True)
                psT = ring(st, "psET_", [128, 1024], BF16, 2, psum=True)
                wo_sb = sbt(st, "wo_sb", [128, 8, D], BF16); r_wo = [Res() for _ in range(8)]
                for kc in range(8):
                    P.dma(POOL, wo_sb[:, kc, :], w_o[kc * 128:(kc + 1) * 128, :], writes=[r_wo[kc]])
                l1w, r_l1w = bcast_row(st, "l1w", ln1_w, D)
                l1b, r_l1b = bcast_row(st, "l1b", ln1_b, D)
                naw, r_naw = bcast_row(st, "naw", norm_a_w, 128)
                gnw, r_gnw = bcast_row(st, "gnw", gn_w, 512)
                gnb, r_gnb = bcast_row(st, "gnb", gn_b, 512)
                o0_r = ring(st, "o0", [128, 512], F32, 5)
                o1_r = ring(st, "o1", [128, 512], F32, 5)
                jk_r = ring(st, "jk", [128, 512], F32, 3)
                tm_r = ring(st, "tmc", [128, 1536], BF16, 3)
                ss_r = ring(st, "ss", [128, 8], F32, 2)
                bs_r = ring(st, "bs", [128, 4, 6], F32, 2)
                bm_r = ring(st, "bm", [128, 4, 3], F32, 2)
                mix_r = ring(st, "mix", [128, D], BF16, 3)
                mT_r = ring(st, "mT", [128, 8, 128], BF16, 3)
                xt_r = ring(st, "xtc", [128, D], F32, 3)
                tt_r = ring(st, "ttc", [128, D], F32, 3)
                st_r = ring(st, "bstc", [128, 2, 6], F32, 2)
                mv_r = ring(st, "bmvc", [128, 4], F32, 2)
                for q in seqs:
                    cnd = q["cond"]
                    for ts in range(0, q["own"], 128):
                        tm, rtm = tm_r.get()
                        P.dma(ACT, tm[:], q["TM"][ts:ts + 128, :], writes=[rtm])
                        mix, rmix = mix_r.get()
                        o0, ro0 = o0_r.get(); o1, ro1 = o1_r.get()
                        P.dma(SP, o0[:], q["OA"][0, ts:ts + 128, :], writes=[ro0])
                        P.dma(SP, o1[:], q["OA"][1, ts:ts + 128, :], writes=[ro1])
                        I(POOL, "tensor_tensor", [ro0, ro1], [ro0], out=o0[:], in0=o0[:], in1=o1[:], op=ALU.add)
                        ss, rss = ss_r.get(); jk, rjk = jk_r.get()
                        I(POOL, "memset", [], [rss], ss[:], 0.0)
                        for h in range(4):
                            I(ACT, "activation", [ro0], [rjk, rss], out=jk[:, h * 128:(h + 1) * 128], in_=o0[:, h * 128:(h + 1) * 128], func=AF.Square, accum_out=ss[:, h:h + 1])
                        I(ACT, "activation", [rss], [rss], out=ss[:, 4:8], in_=ss[:, 0:4], func=AF.Ln, scale=1.0 / 128.0, bias=1e-6)
                        I(ACT, "activation", [rss], [rss], out=ss[:, 4:8], in_=ss[:, 4:8], func=AF.Exp, scale=-0.5)
                        I(DVE, "tensor_tensor", [ro0, rss], [ro0], out=o0[:].rearrange("p (h x) -> p h x", h=4), in0=o0[:].rearrange("p (h x) -> p h x", h=4),
                                                                       in1=ss[:, 4:8].unsqueeze(2).broadcast_to([128, 4, 128]), op=ALU.mult)
                        I(POOL, "tensor_tensor", [ro0, r_naw], [ro0], out=o0[:].rearrange("p (h x) -> p h x", h=4), in0=o0[:].rearrange("p (h x) -> p h x", h=4),
                                                                  in1=naw[:].unsqueeze(1).broadcast_to([128, 4, 128]), op=ALU.mult)
                        I(DVE, "tensor_tensor", [ro0, rtm], [rmix], out=mix[:, 0:512], in0=o0[:], in1=tm[:, 0:512], op=ALU.mult)
                        p0, rp0 = o0_r.get(); p1, rp1 = o1_r.get()
                        P.dma(SP, p0[:], q["OR"][0, ts:ts + 128, :], writes=[rp0])
                        P.dma(SP, p1[:], q["OR"][1, ts:ts + 128, :], writes=[rp1])
                        I(POOL, "tensor_tensor", [rp0, rp1], [rp0], out=p0[:], in0=p0[:], in1=p1[:], op=ALU.add)
                        bs, rbs = bs_r.get(); bm, rbm = bm_r.get()
                        for h in range(4):
                            I(DVE, "bn_stats", [rp0], [rbs], out=bs[:, h, :], in_=p0[:, h * 128:(h + 1) * 128])
                        for h in range(4):
                            I(DVE, "bn_aggr", [rbs], [rbm], out=bm[:, h, 0:2], in_=bs[:, h, :])
                        I(ACT, "activation", [rbm], [rbm], out=bm[:, :, 2], in_=bm[:, :, 1], func=AF.Ln, bias=1e-5)
                        I(ACT, "activation", [rbm], [rbm], out=bm[:, :, 2], in_=bm[:, :, 2], func=AF.Exp, scale=-0.5)
                        for h in range(4):
                            I(DVE, "tensor_scalar", [rp0, rbm], [rp0], out=p0[:, h * 128:(h + 1) * 128], in0=p0[:, h * 128:(h + 1) * 128], scalar1=bm[:, h, 0:1], scalar2=bm[:, h, 2:3],
                                                                                  op0=ALU.subtract, op1=ALU.mult)
                        I(POOL, "tensor_tensor", [rp0, r_gnw], [rp0], out=p0[:], in0=p0[:], in1=gnw[:], op=ALU.mult)
                        I(POOL, "tensor_tensor", [rp0, r_gnb], [rp0], out=p0[:], in0=p0[:], in1=gnb[:], op=ALU.add)
                        I(DVE, "tensor_tensor", [rp0, rtm], [rmix], out=mix[:, 512:1024], in0=p0[:], in1=tm[:, 1024:1536], op=ALU.mult)
                        pt, rpt = psT.get()
                        for kc in range(8):
                            I(PE, "transpose", [rmix, r_identb], [rpt], pt[:, kc * 128:(kc + 1) * 128], mix[:, kc * 128:(kc + 1) * 128], identb[:])
                        mT, rmT = mT_r.get()
                        I(ACT, "copy", [rpt], [rmT], out=mT[:].rearrange("p a t -> p (a t)"), in_=pt[:])
                        py, rpy = psY.get()
                        for cg in range(2):
                            for kc in range(8):
                                I(PE, "matmul", [rmT, r_wo[kc]], [rpy], py[:, cg * 512:(cg + 1) * 512], lhsT=mT[:, kc, :], rhs=wo_sb[:, kc, cg * 512:(cg + 1) * 512],
                                                                                  start=(kc == 0), stop=(kc == 7))
                        xt, rx = xt_r.get()
                        P.dma(ACT, xt[:], q["x"][ts:ts + 128, :], writes=[rx])
                        tt, rtt = tt_r.get()
                        I(DVE, "tensor_tensor", [rpy, r_gates], [rtt], out=tt[:], in0=py[:], in1=gates[:, 0, cnd, :], op=ALU.mult)
                        I(DVE, "scalar_tensor_tensor", [rx, rtt], [rtt], out=tt[:], in0=xt[:], scalar=ALPHA, in1=tt[:], op0=ALU.mult, op1=ALU.add)
                        stt, rst = st_r.get(); mv, rmv = mv_r.get()
                        ln_stats(tt, rtt, mv, rmv, stt, rst, 1e-6)
                        I(DVE, "tensor_scalar", [rtt, rmv], [rtt], out=tt[:], in0=tt[:], scalar1=mv[:, 0:1], scalar2=mv[:, 2:3], op0=ALU.subtract, op1=ALU.mult)
                        I(POOL, "tensor_tensor", [rtt, r_l1w], [rtt], out=tt[:], in0=tt[:], in1=l1w[:], op=ALU.mult)
                        I(DVE, "tensor_tensor", [rtt, r_l1b], [rtt], out=tt[:], in0=tt[:], in1=l1b[:], op=ALU.add)
                        P.dma(SP, q["X1"][ts:ts + 128, :], tt[:], reads=[rtt])
                P.flush()

            with ExitStack() as st:
                w2_sb = sbt(st, "w2_sb", [128, 32, D], BF16); r_w2 = [Res() for _ in range(32)]
                for kc in range(32):
                    P.dma(POOL, w2_sb[:, kc, :], w_ff2[kc * 128:(kc + 1) * 128, :], writes=[r_w2[kc]])
                psA = ring(st, "psF_", [128, 512], F32, 3, psum=True)
                psY = ring(st, "psFY_", [128, 1024], F32, 2, psum=True)
                psT = ring(st, "psFT_", [128, 1024], BF16, 1, psum=True)
                l2w, r_l2w = bcast_row(st, "l2w", ln2_w, D)
                l2b, r_l2b = bcast_row(st, "l2b", ln2_b, D)
                bf2, r_bf2 = bcast_row(st, "bf2", b_ff2, D)
                b1r = sbt(st, "b1r", [32, 128]); b1c = sbt(st, "b1c", [128, 32]); r_b1 = Res()
                P.dma(SP, b1r[:], b_ff1.rearrange("(a p) -> a p", p=128), writes=[r_b1])
                ps, rps = psA.get()
                I(PE, "matmul", [r_b1, r_identf], [rps], ps[:, 0:32], lhsT=b1r[:], rhs=identf[0:32, 0:32], start=True, stop=True)
                I(DVE, "tensor_copy", [rps], [r_b1], out=b1c[:], in_=ps[:, 0:32])
                x1_r = ring(st, "x1", [128, D], F32, 4)
                xn_r = ring(st, "xnf", [128, D], BF16, 2)
                h2_r = ring(st, "h2T", [128, 8, 256], BF16, 2)
                aT_r = ring(st, "aT", [128, 8, 256], BF16, 2)
                rl_r = ring(st, "rl", [128, 256], F32, 3)
                tt_r = ring(st, "ttf", [128, D], F32, 2)
                st_r = ring(st, "bstf", [128, 2, 6], F32, 2)
                mv_r = ring(st, "bmvf", [128, 4], F32, 2)
                for q in seqs:
                    cnd = q["cond"]
                    TT = 256
                    for t0 in range(0, q["own"], TT):
                        h2, rh2 = h2_r.get()
                        x1s = []
                        for j in range(2):
                            ts = t0 + j * 128
                            x1, rx1 = x1_r.get()
                            x1s.append((x1, rx1))
                            P.dma(SP, x1[:], q["X1"][ts:ts + 128, :], writes=[rx1])
                            stt, rst = st_r.get(); mv, rmv = mv_r.get()
                            ln_stats(x1, rx1, mv, rmv, stt, rst, 1e-6)
                            xn, rxn = xn_r.get()
                            I(DVE, "tensor_scalar", [rx1, rmv], [rxn], out=xn[:], in0=x1[:], scalar1=mv[:, 0:1], scalar2=mv[:, 2:3], op0=ALU.subtract, op1=ALU.mult)
                            pt, rpt = psT.get()
                            for kc in range(8):
                                I(PE, "transpose", [rxn, r_identb], [rpt], pt[:, kc * 128:(kc + 1) * 128], xn[:, kc * 128:(kc + 1) * 128], identb[:])
                            for kc in range(8):
                                I(ACT, "activation", [rpt, r_modc], [rh2], out=h2[:, kc, j * 128:(j + 1) * 128], in_=pt[:, kc * 128:(kc + 1) * 128], func=AF.Identity,
                                                                                                scale=modc[:, 4, kc, cnd:cnd + 1], bias=modc[:, 3, kc, cnd:cnd + 1])
                        pys = [psY.get() for _ in range(2)]
                        for g in range(4):
                            aT, raT = aT_r.get()
                            for f in range(8):
                                ft = g * 8 + f
                                ps, rps = psA.get()
                                for kc in range(8):
                                    I(PE, "matmul", [r_w1[kc], rh2], [rps], ps[:, 0:TT], lhsT=w1_sb[:, kc, ft * 128:(ft + 1) * 128], rhs=h2[:, kc, :], start=(kc == 0), stop=(kc == 7))
                                rl, rrl = rl_r.get()
                                I(ACT, "activation", [rps, r_b1], [rrl], out=rl[:], in_=ps[:, 0:TT], func=AF.Relu, bias=b1c[:, ft:ft + 1])
                                I(POOL if ft % 2 else DVE, "tensor_tensor", [rrl], [raT], out=aT[:, f, :], in0=rl[:], in1=rl[:], op=ALU.mult)
                            for j in range(2):
                                py, rpy = pys[j]
                                for cg in range(2):
                                    for f in range(8):
                                        ft = g * 8 + f
                                        I(PE, "matmul", [raT, r_w2[ft]], [rpy], py[:, cg * 512:(cg + 1) * 512], lhsT=aT[:, f, j * 128:(j + 1) * 128], rhs=w2_sb[:, ft, cg * 512:(cg + 1) * 512],
                                          start=(ft == 0), stop=(ft == 31))
                        for j in range(2):
                            ts = t0 + j * 128
                            x1, rx1 = x1s[j]
                            py, rpy = pys[j]
                            tt, rtt = tt_r.get()
                            I(DVE, "tensor_tensor", [rpy, r_bf2], [rtt], out=tt[:], in0=py[:], in1=bf2[:], op=ALU.add)
                            I(POOL, "tensor_tensor", [rtt, r_gates], [rtt], out=tt[:], in0=tt[:], in1=gates[:, 1, cnd, :], op=ALU.mult)
                            I(DVE, "scalar_tensor_tensor", [rx1, rtt], [rtt], out=tt[:], in0=x1[:], scalar=ALPHA, in1=tt[:], op0=ALU.mult, op1=ALU.add)
                            stt, rst = st_r.get(); mv, rmv = mv_r.get()
                            ln_stats(tt, rtt, mv, rmv, stt, rst, 1e-6)
                            I(DVE, "tensor_scalar", [rtt, rmv], [rtt], out=tt[:], in0=tt[:], scalar1=mv[:, 0:1], scalar2=mv[:, 2:3], op0=ALU.subtract, op1=ALU.mult)
                            I(POOL, "tensor_tensor", [rtt, r_l2w], [rtt], out=tt[:], in0=tt[:], in1=l2w[:], op=ALU.mult)
                            I(DVE, "tensor_tensor", [rtt, r_l2b], [rx1], out=x1[:], in0=tt[:], in1=l2b[:], op=ALU.add)
                            P.dma(SP, q["y"][ts:ts + 128, :], x1[:], reads=[rx1])
                P.flush()
    return nc


def _consts(odd):
    k = np.arange(64)
    ut = np.zeros((2, 64, 64), np.float32)
    ut[0] = (k[:, None] <= k[None, :])
    ut[1] = (k[:, None] >= k[None, :])
    ut8 = np.broadcast_to(ut[:, :, None, None, :], (2, 64, 4, 2, 64)).reshape(2, 64, 512)
    neg = np.zeros((2, 64, 2, 64), np.float32)
    j = k[:, None]; i = k[None, :]
    neg[0, :, 0] = np.where(i > j, 0.0, NEGBIG); neg[0, :, 1] = np.where(i >= j, 0.0, NEGBIG)
    neg[1, :, 0] = np.where(i < j, 0.0, NEGBIG); neg[1, :, 1] = np.where(i <= j, 0.0, NEGBIG)
    neg8 = np.broadcast_to(neg[:, :, None, :, :], (2, 64, 4, 2, 64)).reshape(2, 64, 512)
    lg = np.log1p(-np.exp2(-5.0 - np.arange(4, dtype=np.float64)))
    C = 128
    p = np.arange(C, dtype=np.float64)
    dmt = np.zeros((2, C, 4, C)); xi = np.zeros((2, 4, C)); zeta = np.zeros((C, 8)); gch = np.zeros((8,))
    for d in range(2):
        td = d ^ odd
        lgd = lg if td == 0 else lg[::-1]
        for h in range(4):
            g = lgd[h]
            jj = p[:, None]; ii = p[None, :]
            if d == 0:
                dmt[d, :, h, :] = np.where(ii >= jj, np.exp(g * np.maximum(ii - jj, 0)), 0.0)
                xi[d, h] = np.exp(g * (p + 1)); zeta[:, d * 4 + h] = np.exp(g * (C - 1 - p))
            else:
                dmt[d, :, h, :] = np.where(ii <= jj, np.exp(g * np.maximum(jj - ii, 0)), 0.0)
                xi[d, h] = np.exp(g * (C - p)); zeta[:, d * 4 + h] = np.exp(g * p)
            gch[d * 4 + h] = np.exp(g * C)
    dmt *= 0.125; xi *= 0.125
    xi64 = np.broadcast_to(xi.reshape(2, 1, 512), (2, 64, 512))
    gch64 = np.broadcast_to(gch[None, :], (64, 8))
    r = np.repeat(np.arange(64, dtype=np.float32), 64); col = np.tile(np.arange(64, dtype=np.float32), 64)
    inv = (np.float32(10000.0) ** (-np.arange(16, dtype=np.float32) / np.float32(16))).astype(np.float32)
    ang = np.concatenate([r[:, None] * inv, col[:, None] * inv], -1).astype(np.float32)
    cos, sin = np.cos(ang), np.sin(ang)
    if odd:
        cos, sin = cos[::-1], sin[::-1]
    f = lambda a: np.ascontiguousarray(a, dtype=np.float32)
    return dict(c_ut8=f(ut8), c_neg=f(neg8), c_dmt=f(dmt.reshape(2, C, 512)), c_xi=f(xi64), c_zeta=f(zeta), c_gch=f(gch64),
                rope_cos=f(cos), rope_sin=f(sin))


_NC_CACHE = {}


def kernel(x_prompt, x_sample, c, state_delta, state_ret, c_ctx, w_mod, b_mod, w_in, conv_w, a_log, dt_bias,
           norm_a_w, gn_w, gn_b, w_o, ln1_w, ln1_b, w_ff1, b_ff1, w_ff2, b_ff2, ln2_w, ln2_b):
    f = lambda a: np.ascontiguousarray(np.asarray(a), dtype=np.float32)
    x_prompt, x_sample, c, state_delta, state_ret, c_ctx = map(f, (x_prompt, x_sample, c, state_delta, state_ret, c_ctx))
    w_in0 = f(w_in)[0]
    perm = np.arange(DIN)
    perm[2048:2052], perm[2052:2056] = np.arange(2052, 2056), np.arange(2048, 2052)
    perm[2056:2060], perm[2060:2064] = np.arange(2060, 2064), np.arange(2056, 2060)
    common = dict(w_mod=f(w_mod)[0], b_mod=f(b_mod)[0], norm_a_w=f(norm_a_w)[0], gn_w=f(gn_w)[0], gn_b=f(gn_b)[0], w_o=f(w_o)[0],
                  ln1_w=f(ln1_w)[0], ln1_b=f(ln1_b)[0], w_ff1=f(w_ff1)[0], b_ff1=f(b_ff1)[0], w_ff2=f(w_ff2)[0], b_ff2=f(b_ff2)[0],
                  ln2_w=f(ln2_w)[0], ln2_b=f(ln2_b)[0])
    per_par = []
    for odd in range(2):
        dd = dict(common)
        dd.update(_consts(odd))
        dd["w_in"] = f(w_in0[:, perm]) if odd else w_in0
        dd["conv_w"] = f(f(conv_w)[0][::-1]) if odd else f(conv_w)[0]
        dd["a_log"] = f(f(a_log)[0][::-1] if odd else f(a_log)[0]).reshape(8)
        dd["dt_bias"] = f(f(dt_bias)[0][::-1] if odd else f(dt_bias)[0]).reshape(8)
        per_par.append(dd)
    in_maps = []
    for core in range(8):
        s, odd = core // 2, core % 2
        m = dict(per_par[odd])
        xs_ = x_sample[s]
        xp_ = x_prompt[2 * core:2 * core + 2]
        sd = state_delta[s, 0]
        sr = state_ret[s, 0]
        if odd:
            xs_ = xs_[::-1]; xp_ = xp_[:, ::-1]; sd = sd[::-1]; sr = sr[::-1]
        m["xs"] = f(xs_)
        m["xp"] = f(xp_).reshape(2 * TP, D)
        m["cond"] = f(np.stack([c[s], c_ctx]))
        m["sd0"] = f(sd).reshape(2, 512, 128)
        m["sr0"] = f(sr).reshape(2, 256, 128)
        in_maps.append(m)
    if "nc" not in _NC_CACHE:
        _NC_CACHE["nc"] = build_program()
    res = run_bass_kernel_spmd(_NC_CACHE["nc"], in_maps, core_ids=list(range(8)))
    y_prompt = np.zeros((16, TP, D), np.float32)
    y_sample = np.zeros((4, TS, D), np.float32)
    new_sd = np.zeros((16, 1, 2, 4, 128, 128), np.float32)
    new_sr = np.zeros((16, 1, 2, 4, 64, 128), np.float32)
    for core in range(8):
        s, odd = core // 2, core % 2
        r = res.results[core]
        ys_ = np.asarray(r["ys"], dtype=np.float32)
        yp_ = np.asarray(r["yp"], dtype=np.float32).reshape(2, TP, D)
        sd_ = np.asarray(r["nsd"], dtype=np.float32).reshape(2, 2, 4, 128, 128)
        sr_ = np.asarray(r["nsr"], dtype=np.float32).reshape(2, 2, 4, 64, 128)
        if odd:
            y_sample[s, OWN:] = ys_[::-1]
            y_prompt[2 * core:2 * core + 2] = yp_[:, ::-1]
            new_sd[2 * core:2 * core + 2, 0] = sd_[:, ::-1]
            new_sr[2 * core:2 * core + 2, 0] = sr_[:, ::-1]
        else:
            y_sample[s, :OWN] = ys_
            y_prompt[2 * core:2 * core + 2] = yp_
            new_sd[2 * core:2 * core + 2, 0] = sd_
            new_sr[2 * core:2 * core + 2, 0] = sr_
    return (y_prompt, y_sample, new_sd, new_sr)
```
